# Optimizing a Trainium2 kernel written in Bass

```python
import math
import jax
import jax.numpy as jnp
from jax import lax
import numpy as np

D_MODEL = 1024
BATCH = 8
SEQ = 2048
DEPTH = 2

GRID_W = 64
CTX_LEN = 256
ROPE_THETA = 10000.0
NORM_EPS = 1e-6
LB_FLOOR = 1e-30
Q_BLOCK = 128
CHUNK = 64

DA_HEADS = 4
DA_QK_DIM = 64
DA_V_DIM = 2 * DA_QK_DIM
HG_HEADS = 4
HG_K_DIM = 64
HG_V_DIM = 64
GD_HEADS = 4
GD_K_DIM = 64
GD_V_DIM = 64
GD_CONV = 3
FFN_CONV = 3
D_FF = 2816

DA_QK_W = DA_HEADS * 2 * DA_QK_DIM
DA_V_W = DA_HEADS * DA_V_DIM
HG_K_W = HG_HEADS * HG_K_DIM
HG_V_W = HG_HEADS * HG_V_DIM
GD_K_W = GD_HEADS * GD_K_DIM
GD_V_W = GD_HEADS * GD_V_DIM
MIX_W = DA_V_W + HG_V_W + GD_V_W
IN_SPLIT = (DA_QK_W, DA_QK_W, DA_V_W, HG_K_W, HG_V_W, 2 * HG_K_W, HG_V_W,
            2 * GD_K_W + GD_V_W, 2 * GD_HEADS, 2 * GD_HEADS, GD_V_W)
IN_COLS = sum(IN_SPLIT)
F32 = jnp.float32

kernel_name = "hybrid_diffattn_hgrn2_gdn_convffn_dit"


def rms_norm(x, g):
    xf = x.astype(F32)
    y = xf * lax.rsqrt(jnp.mean(xf * xf, axis=-1, keepdims=True) + NORM_EPS)
    return (y * g.astype(F32)).astype(x.dtype)


def l2_normalize(x):
    return x * lax.rsqrt(jnp.sum(x * x, axis=-1, keepdims=True) + NORM_EPS)


def modulate(h, shift, scale):
    return h * (1.0 + scale) + shift


def split_cols(p):
    return jnp.split(p, np.cumsum(IN_SPLIT)[:-1].tolist(), axis=-1)


def to_heads(x, n_heads):
    b, l, _ = x.shape
    return x.reshape(b, l, n_heads, -1).transpose(0, 2, 1, 3)


def from_heads(x):
    b, h, l, d = x.shape
    return x.transpose(0, 2, 1, 3).reshape(b, l, h * d)


def depthwise_conv(x, w):
    k, l = w.shape[0], x.shape[1]
    xp = jnp.pad(x, ((0, 0), (k // 2, k // 2), (0, 0)))
    return sum(xp[:, j:j + l] * w[j] for j in range(k))


def masked_decay(diff, mask):
    return jnp.where(mask, jnp.exp(jnp.minimum(diff, 0.0)), 0.0)


def axial_rope_tables(n_rows):
    n_freq = DA_QK_DIM // 4
    inv = ROPE_THETA ** (-jnp.arange(n_freq, dtype=F32) / n_freq)
    rows = jnp.repeat(jnp.arange(n_rows, dtype=F32), GRID_W)
    cols = jnp.tile(jnp.arange(GRID_W, dtype=F32), n_rows)
    ang = jnp.concatenate([rows[:, None] * inv, cols[:, None] * inv], axis=-1)
    return jnp.cos(ang), jnp.sin(ang)


def apply_rope(x, cos, sin):
    xa, xb = jnp.split(x, 2, axis=-1)
    return jnp.concatenate([xa * cos - xb * sin, xb * cos + xa * sin], axis=-1).astype(x.dtype)


def diff_softmax_attend(q, k, v, lam):
    s = jnp.einsum('bhmqd,bhmkd->bhmqk', q, k).astype(F32) * DA_QK_DIM ** -0.5
    p = jax.nn.softmax(s, axis=-1)
    w = p[:, :, 0] - lam * p[:, :, 1]
    return jnp.einsum('bhqk,bhkd->bhqd', w.astype(v.dtype), v)


def blocked_diff_attend(q, k, v, lam):
    b, h, m, l, d = q.shape
    nb = l // Q_BLOCK
    qb = jnp.moveaxis(q.reshape(b, h, m, nb, Q_BLOCK, d), 3, 0)
    o = lax.map(lambda qi: diff_softmax_attend(qi, k, v, lam), qb)
    return jnp.moveaxis(o, 0, 2).reshape(b, h, l, -1)


def diff_attention_mixer(q_l, k_l, v_l, q_c, k_c, v_c, lam_p, subln_g, lam_init, cos, sin, need_ctx):
    def qk_heads(p):
        b, l, _ = p.shape
        return p.reshape(b, l, DA_HEADS, 2, DA_QK_DIM).transpose(0, 2, 3, 1, 4)

    ql = apply_rope(qk_heads(q_l), cos, sin)
    kl = apply_rope(qk_heads(k_l), cos, sin)
    kc = qk_heads(k_c)
    vl, vc = to_heads(v_l, DA_HEADS), to_heads(v_c, DA_HEADS)
    lp = lam_p.astype(F32)
    lam = jnp.exp(jnp.sum(lp[0] * lp[1])) - jnp.exp(jnp.sum(lp[2] * lp[3])) + lam_init

    def finish(o):
        return from_heads(rms_norm(o, subln_g) * (1.0 - lam_init))

    k_all = jnp.concatenate([kc, kl], axis=3)
    v_all = jnp.concatenate([vc, vl], axis=2)
    o_lat = finish(blocked_diff_attend(ql, k_all, v_all, lam))
    o_ctx = finish(diff_softmax_attend(qk_heads(q_c), kc, vc, lam)) if need_ctx else None
    return o_lat, o_ctx


def chunk_scan(step, s0, seqs):
    l = seqs[0].shape[2]
    n = l // CHUNK

    def to_chunks(a):
        return jnp.moveaxis(a.reshape(a.shape[:2] + (n, CHUNK) + a.shape[3:]), 2, 0)

    s_final, o = lax.scan(step, s0, tuple(to_chunks(a) for a in seqs))
    o = jnp.moveaxis(o, 0, 2)
    return o.reshape(o.shape[:2] + (l,) + o.shape[4:]), s_final


def flip_time(seqs):
    return tuple(jnp.flip(a, axis=2) for a in seqs)


def bidirectional_scan(step, s0, ctx_fwd, lat_fwd, ctx_bwd, lat_bwd):
    o_cf, s_cf = chunk_scan(step, s0, ctx_fwd)
    o_lf, _ = chunk_scan(step, s_cf, lat_fwd)
    o_cb, s_cb = chunk_scan(step, s0, flip_time(ctx_bwd))
    o_lb, _ = chunk_scan(step, s_cb, flip_time(lat_bwd))
    return o_lf + jnp.flip(o_lb, axis=2), o_cf + jnp.flip(o_cb, axis=2)


def hgrn2_chunk_step(state, inp):
    q, k, v, log_f = inp
    c = q.shape[2]
    lower = jnp.tril(jnp.ones((c, c), dtype=bool))[:, :, None]
    b = jnp.cumsum(log_f, axis=2)
    decay = masked_decay(b[:, :, :, None, :] - b[:, :, None, :, :], lower)
    scores = jnp.einsum('bhtd,bhsd,bhtsd->bhts', q, k, decay)
    out = jnp.einsum('bhtd,bhde->bhte', q * jnp.exp(b), state) + jnp.einsum('bhts,bhse->bhte', scores, v)
    b_last = b[:, :, -1:, :]
    state = jnp.exp(b_last[:, :, 0, :])[..., None] * state + jnp.einsum('bhsd,bhse->bhde', k * jnp.exp(b_last - b), v)
    return state, out


def hgrn2_mixer(q_l, i_l, f_l, g_l, q_c, i_c, f_c, g_c, lb, norm_g, need_ctx):
    lbh = lb.astype(F32).reshape(1, 2, HG_HEADS, 1, HG_K_DIM)
    log_lb = jnp.log(jnp.maximum(lbh, LB_FLOOR))
    log_1m_lb = jnp.log1p(-lbh)

    def prep(q, i, f):
        qh = to_heads(jax.nn.silu(q), HG_HEADS).astype(F32)
        vh = to_heads(i, HG_HEADS).astype(F32)
        z = to_heads(f, 2 * HG_HEADS).astype(F32).reshape(f.shape[0], 2, HG_HEADS, f.shape[1], HG_K_DIM)
        k = (1.0 - lbh) * jax.nn.sigmoid(-z)
        log_f = jnp.logaddexp(log_lb, log_1m_lb + jax.nn.log_sigmoid(z))
        return tuple((qh, k[:, d], vh, log_f[:, d]) for d in range(2))

    fwd_c, bwd_c = prep(q_c, i_c, f_c)
    fwd_l, bwd_l = prep(q_l, i_l, f_l)
    s0 = jnp.zeros((q_l.shape[0], HG_HEADS, HG_K_DIM, HG_V_DIM), F32)
    o_l, o_c = bidirectional_scan(hgrn2_chunk_step, s0, fwd_c, fwd_l, bwd_c, bwd_l)

    def finish(o, g):
        return from_heads(rms_norm(o, norm_g)).astype(g.dtype) * jax.nn.silu(g)

    return finish(o_l, g_l), (finish(o_c, g_c) if need_ctx else None)


def gdn_chunk_step(state, inp):
    q, k, v, log_alpha, beta = inp
    c = q.shape[2]
    lower = jnp.tril(jnp.ones((c, c), dtype=bool))
    strict = jnp.tril(jnp.ones((c, c), dtype=bool), -1)
    gc = jnp.cumsum(log_alpha, axis=-1)
    decay = masked_decay(gc[..., :, None] - gc[..., None, :], lower)
    kk = jnp.einsum('bhtd,bhsd->bhts', k, k)
    tri = jnp.eye(c, dtype=q.dtype) + jnp.where(strict, beta[..., :, None] * kk * decay, 0.0)
    rhs = jnp.concatenate([v * beta[..., None], k * (beta * jnp.exp(gc))[..., None]], axis=-1)
    sol = lax.linalg.triangular_solve(tri, rhs, left_side=True, lower=True, unit_diagonal=True)
    dv = v.shape[-1]
    u, w = sol[..., :dv], sol[..., dv:]
    v_new = u - jnp.einsum('bhtd,bhde->bhte', w, state)
    scores = jnp.einsum('bhtd,bhsd->bhts', q, k) * decay
    out = jnp.einsum('bhtd,bhde->bhte', q * jnp.exp(gc)[..., None], state) + jnp.einsum('bhts,bhse->bhte', scores, v_new)
    g_last = gc[..., -1:]
    state = jnp.exp(g_last)[..., None] * state + jnp.einsum('bhsd,bhse->bhde', k * jnp.exp(g_last - gc)[..., None], v_new)
    return state, out


def gated_deltanet_mixer(qkv_l, a_l, b_l, g_l, qkv_c, a_c, b_c, g_c, conv_w, a_log, dt_bias, norm_g, need_ctx):
    def prep(qkv, a_in, b_in):
        qkv = jax.nn.silu(depthwise_conv(qkv, conv_w))
        q, k, v = jnp.split(qkv, [GD_K_W, 2 * GD_K_W], axis=-1)
        qh = l2_normalize(to_heads(q, GD_HEADS).astype(F32)) * GD_K_DIM ** -0.5
        kh = l2_normalize(to_heads(k, GD_HEADS).astype(F32))
        vh = to_heads(v, GD_HEADS).astype(F32)
        bsz, l = a_in.shape[0], a_in.shape[1]
        a_t = jnp.swapaxes(a_in.astype(F32), 1, 2).reshape(bsz, 2, GD_HEADS, l)
        beta = jax.nn.sigmoid(jnp.swapaxes(b_in.astype(F32), 1, 2).reshape(bsz, 2, GD_HEADS, l))
        log_alpha = -jnp.exp(a_log.astype(F32))[None, :, :, None] * jax.nn.softplus(a_t + dt_bias.astype(F32)[None, :, :, None])
        return tuple((qh, kh, vh, log_alpha[:, d], beta[:, d]) for d in range(2))

    fwd_c, bwd_c = prep(qkv_c, a_c, b_c)
    fwd_l, bwd_l = prep(qkv_l, a_l, b_l)
    s0 = jnp.zeros((qkv_l.shape[0], GD_HEADS, GD_K_DIM, GD_V_DIM), F32)
    o_l, o_c = bidirectional_scan(gdn_chunk_step, s0, fwd_c, fwd_l, bwd_c, bwd_l)

    def finish(o, g):
        return from_heads(rms_norm(o, norm_g)).astype(g.dtype) * jax.nn.silu(g)

    return finish(o_l, g_l), (finish(o_c, g_c) if need_ctx else None)


def conv_ffn(h, w_up, conv_w, conv_b, w_down):
    u = depthwise_conv(h @ w_up, conv_w) + conv_b
    gate, val = jnp.split(u, 2, axis=-1)
    return (jax.nn.silu(gate) * val) @ w_down


def setup_inputs(seed: int = 0) -> dict:
    key = jax.random.key(seed)
    ks = jax.random.split(key, 24)

    def nrm(k, shape, s):
        return jax.random.normal(k, shape, F32) * s

    dt = jnp.exp(jax.random.uniform(ks[15], (DEPTH, 2, GD_HEADS), F32, math.log(1e-3), math.log(1e-1)))
    return {
        'x': nrm(ks[0], (BATCH, SEQ, D_MODEL), 1.0),
        'c': nrm(ks[1], (BATCH, D_MODEL), 1.0),
        'ctx': nrm(ks[2], (BATCH, CTX_LEN, D_MODEL), 1.0),
        'c_ctx': nrm(ks[3], (D_MODEL,), 1.0),
        'ada_w': nrm(ks[4], (DEPTH, D_MODEL, 6 * D_MODEL), 0.5 * D_MODEL ** -0.5),
        'ada_b': nrm(ks[5], (DEPTH, 6 * D_MODEL), 0.01),
        'norm_g': 1.0 + nrm(ks[6], (DEPTH, 4, D_MODEL), 0.02),
        'w_in': nrm(ks[7], (DEPTH, D_MODEL, IN_COLS), D_MODEL ** -0.5),
        'w_out': nrm(ks[8], (DEPTH, MIX_W, D_MODEL), MIX_W ** -0.5),
        'da_lambda': nrm(ks[9], (DEPTH, 4, DA_QK_DIM), 0.1),
        'da_subln_g': 1.0 + nrm(ks[10], (DEPTH, DA_V_DIM), 0.02),
        'hg_lb_logits': nrm(ks[11], (DEPTH, 2, HG_K_W), 0.1),
        'hg_norm_g': 1.0 + nrm(ks[12], (DEPTH, HG_V_DIM), 0.02),
        'gd_conv_w': nrm(ks[13], (DEPTH, GD_CONV, 2 * GD_K_W + GD_V_W), GD_CONV ** -0.5),
        'gd_a_log': jnp.log(jax.random.uniform(ks[14], (DEPTH, 2, GD_HEADS), F32, 1.0, 16.0)),
        'gd_dt_bias': dt + jnp.log(-jnp.expm1(-dt)),
        'gd_norm_g': 1.0 + nrm(ks[16], (DEPTH, GD_V_DIM), 0.02),
        'ffn_w_up': nrm(ks[17], (DEPTH, D_MODEL, 2 * D_FF), D_MODEL ** -0.5),
        'ffn_conv_w': nrm(ks[18], (DEPTH, FFN_CONV, 2 * D_FF), FFN_CONV ** -0.5),
        'ffn_conv_b': nrm(ks[19], (DEPTH, 2 * D_FF), 0.01),
        'ffn_w_down': nrm(ks[20], (DEPTH, D_FF, D_MODEL), D_FF ** -0.5),
    }


def reference(x, c, ctx, c_ctx, ada_w, ada_b, norm_g, w_in, w_out, da_lambda, da_subln_g,
              hg_lb_logits, hg_norm_g, gd_conv_w, gd_a_log, gd_dt_bias, gd_norm_g,
              ffn_w_up, ffn_conv_w, ffn_conv_b, ffn_w_down):
    n_rows = x.shape[1] // GRID_W
    cos, sin = axial_rope_tables(n_rows)
    lb_w = jax.nn.softmax(hg_lb_logits.astype(F32), axis=0)
    lb_all = jnp.cumsum(lb_w, axis=0) - lb_w[0]
    cond_lat = jax.nn.silu(c)[:, None, :]
    cond_ctx = jax.nn.silu(c_ctx)
    h_ctx = ctx
    for layer in range(DEPTH):
        need_ctx = layer < DEPTH - 1
        lam_init = 0.8 - 0.6 * math.exp(-0.3 * layer)
        mod_l = jnp.split(cond_lat @ ada_w[layer] + ada_b[layer], 6, axis=-1)
        mod_c = jnp.split(cond_ctx @ ada_w[layer] + ada_b[layer], 6, axis=-1)

        p_l = split_cols(modulate(rms_norm(x, norm_g[layer, 0]), mod_l[0], mod_l[1]) @ w_in[layer])
        p_c = split_cols(modulate(rms_norm(h_ctx, norm_g[layer, 0]), mod_c[0], mod_c[1]) @ w_in[layer])
        oa_l, oa_c = diff_attention_mixer(p_l[0], p_l[1], p_l[2], p_c[0], p_c[1], p_c[2],
                                          da_lambda[layer], da_subln_g[layer], lam_init, cos, sin, need_ctx)
        ob_l, ob_c = hgrn2_mixer(p_l[3], p_l[4], p_l[5], p_l[6], p_c[3], p_c[4], p_c[5], p_c[6],
                                 lb_all[layer], hg_norm_g[layer], need_ctx)
        oc_l, oc_c = gated_deltanet_mixer(p_l[7], p_l[8], p_l[9], p_l[10], p_c[7], p_c[8], p_c[9], p_c[10],
                                          gd_conv_w[layer], gd_a_log[layer], gd_dt_bias[layer], gd_norm_g[layer], need_ctx)
        mix_l = jnp.concatenate([oa_l, ob_l, oc_l], axis=-1) @ w_out[layer]
        x = x + mod_l[2] * rms_norm(mix_l, norm_g[layer, 1])

        ff_l = conv_ffn(modulate(rms_norm(x, norm_g[layer, 2]), mod_l[3], mod_l[4]),
                        ffn_w_up[layer], ffn_conv_w[layer], ffn_conv_b[layer], ffn_w_down[layer])
        x = x + mod_l[5] * rms_norm(ff_l, norm_g[layer, 3])

        if need_ctx:
            mix_c = jnp.concatenate([oa_c, ob_c, oc_c], axis=-1) @ w_out[layer]
            h_ctx = h_ctx + mod_c[2] * rms_norm(mix_c, norm_g[layer, 1])
            ff_c = conv_ffn(modulate(rms_norm(h_ctx, norm_g[layer, 2]), mod_c[3], mod_c[4]),
                            ffn_w_up[layer], ffn_conv_w[layer], ffn_conv_b[layer], ffn_w_down[layer])
            h_ctx = h_ctx + mod_c[5] * rms_norm(ff_c, norm_g[layer, 3])
    return x
```

```python
import math
from contextlib import ExitStack

import numpy as np
import concourse.bass as bass
import concourse.mybir as mybir
from concourse.bass_utils import run_bass_kernel_spmd

F32 = mybir.dt.float32
BF16 = mybir.dt.bfloat16
AF = mybir.ActivationFunctionType
ALU = mybir.AluOpType
AX = mybir.AxisListType

D = 1024
SEQ = 2048
CTX = 256
NT = SEQ + CTX
DEPTH = 2
DFF = 2816
INC = 3856
EPS = 1e-6
KC = 8
O_DAQ, O_DAK, O_DAV = 0, 512, 1024
O_HGQ, O_HGI, O_HGF, O_HGG = 1536, 1792, 2048, 2560
O_GDQKV, O_GDA, O_GDB, O_GDG = 2816, 3584, 3592, 3600


class T:
    __slots__ = ("ap", "wr", "rd", "sem", "cnt", "name", "psum")

    def __init__(self, ap, name=""):
        self.ap = ap
        self.wr = None
        self.rd = {}
        self.sem = None
        self.cnt = 0
        self.name = name
        self.psum = False

    def __getitem__(self, k):
        return self.ap[k]


class KB:
    def __init__(self):
        self.nc = bass.Bass("TRN2", target_bir_lowering=False)
        nc = self.nc
        self.es = ExitStack()
        self.eng = dict(pe=nc.tensor, act=nc.scalar, dve=nc.vector, pool=nc.gpsimd, sp=nc.sync)
        self.esem = {}
        self.ecnt = {}
        self.waited = {}
        for e in self.eng:
            self.esem[e] = self.newsem("e_" + e)
            self.ecnt[e] = 0
            self.waited[e] = {}
        self.nsem = 0
        self.dma_ts = []
        self.sem_pool = []
        self.psum_banks = []
        self.psum_i = 0
        self.uid = 0

    def newsem(self, name):
        return self.es.enter_context(self.nc.semaphore(name))

    def sb(self, name, shape, dtype=F32):
        t = self.es.enter_context(self.nc.sbuf_tensor("s_" + name, list(shape), dtype))
        return T(t, name)

    def dram(self, name, shape, dtype=F32, kind="Internal"):
        return T(self.nc.dram_tensor(name, list(shape), dtype, kind=kind).ap(), name)

    def init_psum(self):
        for i in range(8):
            t = self.es.enter_context(self.nc.psum_tensor("psb%d" % i, [128, 512], F32))
            self.psum_banks.append(T(t, "psb%d" % i))
            self.psum_banks[-1].psum = True

    def ps(self):
        t = self.psum_banks[self.psum_i % 8]
        self.psum_i += 1
        return t

    def _deps(self, e, reads, writes, pe_self=False):
        need = {}

        def add(ev):
            if ev is None:
                return
            s, v = ev
            if e == "pe" and s is self.esem["pe"] and not pe_self:
                return
            k = id(s)
            if k not in need or need[k][1] < v:
                need[k] = (s, v)

        for t in reads:
            add(t.wr)
            if t.psum:
                for ev in t.rd.values():
                    if ev[0] is not self.esem[e]:
                        add(ev)
        for t in writes:
            add(t.wr)
            for ev in t.rd.values():
                add(ev)
        w = self.waited[e]
        for k, (s, v) in need.items():
            if w.get(k, 0) >= v:
                continue
            self.eng[e].wait_ge(s, v)
            w[k] = v

    def _record(self, ev, reads, writes):
        k = id(ev[0])
        for t in reads:
            if k not in t.rd or t.rd[k][1] < ev[1]:
                t.rd[k] = ev
        for t in writes:
            t.wr = ev
            t.rd = {}

    def op(self, e, fn, reads=(), writes=(), pe_self=False):
        self._deps(e, reads, writes, pe_self)
        ins = fn(self.eng[e])
        self.ecnt[e] += 1
        ins.then_inc(self.esem[e], 1)
        ev = (self.esem[e], self.ecnt[e])
        self._record(ev, reads, writes)
        return ev

    def dma(self, q, out_t, out_ap, in_t, in_ap, semt=None, **kw):
        self._deps(q, [in_t], [out_t])
        st = semt if semt is not None else out_t
        if st.sem is None:
            if self.sem_pool:
                st.sem, st.cnt = self.sem_pool.pop()
            else:
                st.sem = self.newsem("d%d" % self.nsem)
                self.nsem += 1
            self.dma_ts.append(st)
        ins = self.eng[q].dma_start(out=out_ap, in_=in_ap, **kw)
        ins.then_inc(st.sem, 16)
        st.cnt += 16
        ev = (st.sem, st.cnt)
        self._record(ev, [in_t], [out_t])
        return ev

    def barrier(self):
        evs = {}
        for e in self.eng:
            if self.ecnt[e] > 0:
                evs[id(self.esem[e])] = (self.esem[e], self.ecnt[e])
        for t in self.dma_ts:
            evs[id(t.sem)] = (t.sem, t.cnt)
        for e in self.eng:
            for k, ev in evs.items():
                if e == "pe" and ev[0] is self.esem["pe"]:
                    continue
                self.wait_ev(e, ev)

    def phase(self):
        return Phase(self)

    def wait_ev(self, e, ev):
        s, v = ev
        k = id(s)
        if self.waited[e].get(k, 0) >= v:
            return
        self.eng[e].wait_ge(s, v)
        self.waited[e][k] = v


class Phase:
    def __init__(self, kb):
        self.kb = kb
        self.es = ExitStack()
        self.ts = []

    def sb(self, name, shape, dtype=F32):
        self.kb.uid += 1
        t = self.es.enter_context(self.kb.nc.sbuf_tensor("p%d_%s" % (self.kb.uid, name), list(shape), dtype))
        tt = T(t, name)
        self.ts.append(tt)
        return tt

    def __enter__(self):
        return self

    def __exit__(self, *a):
        self.kb.barrier()
        for t in self.ts:
            if t.sem is not None and t in self.kb.dma_ts:
                self.kb.dma_ts.remove(t)
                self.kb.sem_pool.append((t.sem, t.cnt))
                t.sem = None
        self.es.close()
        return False


def build(dbg=None, nlayers=DEPTH, skip=()):
    kb = KB()
    nc = kb.nc
    dbg = dbg or {}
    x_d = T(nc.dram_tensor("x", [SEQ, D], F32, kind="ExternalInput").ap())
    ctx_d = T(nc.dram_tensor("ctx", [CTX, D], F32, kind="ExternalInput").ap())
    cvT_d = T(nc.dram_tensor("cvT", [128, KC, 2], F32, kind="ExternalInput").ap())
    ada_w_d = T(nc.dram_tensor("ada_w", [DEPTH, D, 6 * D], F32, kind="ExternalInput").ap())
    ada_bc_d = T(nc.dram_tensor("ada_bc", [DEPTH, 128, 48], F32, kind="ExternalInput").ap())
    ada_b_d = T(nc.dram_tensor("ada_b", [DEPTH, 6 * D], F32, kind="ExternalInput").ap())
    ng_d = T(nc.dram_tensor("norm_g", [DEPTH, 4, D], F32, kind="ExternalInput").ap())
    ngc_d = T(nc.dram_tensor("norm_gc", [DEPTH, 4, 128, KC], F32, kind="ExternalInput").ap())
    w_in_d = T(nc.dram_tensor("w_in", [DEPTH, D, INC], F32, kind="ExternalInput").ap())
    w_out_d = T(nc.dram_tensor("w_out", [DEPTH, D, D], F32, kind="ExternalInput").ap())
    ident_d = T(nc.dram_tensor("ident", [128, 128], F32, kind="ExternalInput").ap())
    out_d = T(nc.dram_tensor("out", [SEQ, D], F32, kind="ExternalOutput").ap())
    gates_d = kb.dram("gates_scr", [DEPTH, 2, 2, D])
    dbg_d = {}
    for name, (shp, dt_) in dbg.items():
        if name == "mix_in":
            continue
        dbg_d[name] = T(nc.dram_tensor("dbg_" + name, list(shp), BF16 if dt_ == "bf16" else F32,
                                       kind="ExternalOutput").ap())

    kb.init_psum()
    xs = [kb.sb("x%d" % i, [128, D]) for i in range(18)]
    hT = kb.sb("hT", [128, KC, NT], BF16)
    ident = kb.sb("ident", [128, 128])
    identb = kb.sb("identb", [128, 128], BF16)
    cvT = kb.sb("cvT", [128, KC, 2])
    scT = kb.sb("scT", [128, KC, 2], BF16)
    modc = kb.sb("modc", [128, DEPTH, 4, KC, 2])
    gs_c = kb.sb("gs_c", [128, DEPTH, 2, KC, 2])
    sh_c = kb.sb("sh_c", [128, DEPTH, 2, KC, 2])
    ngc = kb.sb("ngc", [128, DEPTH, 4, KC])
    adab_c = kb.sb("adab_c", [128, DEPTH, 48])
    wada = [kb.sb("wada%d" % i, [128, KC, 512], BF16) for i in range(2)]
    stat = kb.sb("stat", [128, 64])
    junk = kb.sb("junk", [128, D])
    xn = kb.sb("xn", [128, D], BF16)
    ph0 = kb.phase()
    grow = ph0.sb("grow", [2, 2, D])
    brow = ph0.sb("brow", [2, 2, D])
    grow_g = ph0.sb("grow_g", [2, 2, D])

    finals = []
    kb.dma("sp", ident, ident[:], ident_d, ident_d[:])
    kb.dma("sp", cvT, cvT[:], cvT_d, cvT_d[:])
    kb.dma("sp", ngc, ngc[:], ngc_d, ngc_d.ap.rearrange("l j p c -> p l j c"))
    kb.dma("sp", adab_c, adab_c[:], ada_bc_d, ada_bc_d.ap.rearrange("l p c -> p l c"))
    for i in range(2):
        kb.dma("sp", xs[i], xs[i][:], ctx_d, ctx_d[i * 128:(i + 1) * 128, :])
    for i in range(16):
        kb.dma("sp", xs[2 + i], xs[2 + i][:], x_d, x_d[i * 128:(i + 1) * 128, :])
    kb.op("dve", lambda e: e.tensor_copy(out=identb[:], in_=ident[:]), [ident], [identb])
    kb.op("act", lambda e: e.activation(out=scT[:], in_=cvT[:], func=AF.Silu), [cvT], [scT])

    for l in range(nlayers):
        for seg_i, j in enumerate((0, 1, 3, 4)):
            for half in range(2):
                wt = wada[(seg_i * 2 + half) % 2]
                c0 = j * D + half * 512
                kb.dma("pool", wt, wt[:], ada_w_d,
                       ada_w_d.ap[l, :, c0:c0 + 512].rearrange("(c p) n -> p c n", p=128))
                pst = kb.ps()
                for fc in range(4):
                    for kc in range(KC):
                        kb.op("pe", lambda e, fc=fc, kc=kc: e.matmul(
                            pst[:, fc * 2:fc * 2 + 2], lhsT=wt[:, kc, fc * 128:(fc + 1) * 128],
                            rhs=scT[:, kc, :], start=(kc == 0), stop=(kc == KC - 1)),
                            [wt, scT], [pst])
                for fc in range(4):
                    cc = half * 4 + fc
                    kb.op("dve", lambda e, fc=fc, cc=cc: e.tensor_scalar(
                        out=modc[:, l, seg_i, cc, :], in0=pst[:, fc * 2:fc * 2 + 2],
                        scalar1=adab_c[:, l, j * 8 + cc:j * 8 + cc + 1], scalar2=None, op0=ALU.add),
                        [pst, adab_c], [modc])
        for m in range(2):
            for r in range(2):
                kb.op("dve", lambda e, m=m, r=r: e.scalar_tensor_tensor(
                    out=gs_c[:, l, m, :, r], in0=modc[:, l, 2 * m + 1, :, r], scalar=1.0,
                    in1=ngc[:, l, 2 * m, :], op0=ALU.add, op1=ALU.mult), [modc, ngc], [gs_c])
                kb.op("dve", lambda e, m=m, r=r: e.tensor_copy(
                    out=sh_c[:, l, m, :, r], in_=modc[:, l, 2 * m, :, r]), [modc], [sh_c])
        for m, j in enumerate((2, 5)):
            kb.dma("sp", brow, brow[0:1, m, :], ada_b_d, ada_b_d.ap[l:l + 1, j * D:(j + 1) * D])
            kb.dma("sp", brow, brow[1:2, m, :], ada_b_d, ada_b_d.ap[l:l + 1, j * D:(j + 1) * D])
            kb.dma("sp", grow_g, grow_g[0:1, m, :], ng_d, ng_d.ap[l, 2 * m + 1:2 * m + 2, :])
            kb.dma("sp", grow_g, grow_g[1:2, m, :], ng_d, ng_d.ap[l, 2 * m + 1:2 * m + 2, :])
            for half in range(2):
                wt = wada[half]
                c0 = j * D + half * 512
                kb.dma("pool", wt, wt[:], ada_w_d,
                       ada_w_d.ap[l, :, c0:c0 + 512].rearrange("(c p) n -> p c n", p=128))
                pst = kb.ps()
                for kc in range(KC):
                    kb.op("pe", lambda e, kc=kc: e.matmul(
                        pst[0:2, :], lhsT=scT[:, kc, :], rhs=wt[:, kc, :],
                        start=(kc == 0), stop=(kc == KC - 1)), [wt, scT], [pst])
                kb.op("dve", lambda e, half=half, m=m: e.tensor_tensor(
                    out=grow[0:2, m, half * 512:(half + 1) * 512], in0=pst[0:2, :],
                    in1=brow[0:2, m, half * 512:(half + 1) * 512], op=ALU.add), [pst, brow], [grow])
            kb.op("dve", lambda e, m=m: e.tensor_tensor(
                out=grow[0:2, m, :], in0=grow[0:2, m, :], in1=grow_g[0:2, m, :], op=ALU.mult),
                [grow, grow_g], [grow])
            kb.dma("sp", gates_d, gates_d.ap[l, m, :, :], grow, grow[0:2, m, :], semt=grow)

    ph0.__exit__()
    st = dict(kb=kb, nc=nc, l=None, xs=xs, hT=hT, ident=ident, identb=identb, gs_c=gs_c, sh_c=sh_c,
              stat=stat, junk=junk, xn=xn, wbuf=wada, w_in_d=w_in_d, w_out_d=w_out_d, gates_d=gates_d,
              dbg_d=dbg_d, finals=finals, x_d=x_d, out_d=out_d, ng_d=ng_d)
    extra_inputs(st)
    if "mix_in" in dbg:
        st["mix_in"] = T(nc.dram_tensor("mix_in", [NT, D], BF16, kind="ExternalInput").ap())
    for l in range(nlayers):
        st["l"] = l
        norm_to_hT(st, l, 0)
        if "hT%d" % l in dbg_d:
            dump_hT(st, "hT%d" % l)
        if "stop_norm" in dbg:
            break
        if "da" not in skip:
            da_mixer(st, l)
        if "hg" not in skip:
            hg_mixer(st, l)
        if "gd" not in skip:
            (gd_mixer if "gd1" in skip else gd_mixer2)(st, l)
        if "mix_in" in dbg:
            pass
        if "mix%d" % l in dbg_d:
            d_ = dbg_d["mix%d" % l]
            for i_ in range(18):
                finals.append(kb.dma("sp", d_, d_.ap[i_ * 128:(i_ + 1) * 128, :], st["mix_d"],
                                     st["mix_d"].ap[i_ * 128:(i_ + 1) * 128, :], semt=d_))
        if "op" not in skip:
            out_proj(st, l)
        if "x1_%d" % l in dbg_d:
            d_ = dbg_d["x1_%d" % l]
            for i_ in range(18):
                finals.append(kb.dma("sp", d_, d_.ap[i_ * 128:(i_ + 1) * 128, :], xs[i_], xs[i_][:], semt=xs[i_]))
        ftiles = range(18) if l < DEPTH - 1 else range(2, 18)
        if "ffn" not in skip:
            norm_to_hT(st, l, 1, ftiles)
            ffn(st, l)
        if "x2_%d" % l in dbg_d:
            d_ = dbg_d["x2_%d" % l]
            for i_ in range(18):
                finals.append(kb.dma("sp", d_, d_.ap[i_ * 128:(i_ + 1) * 128, :], xs[i_], xs[i_][:], semt=xs[i_]))
    for i_ in range(16):
        finals.append(kb.dma("sp", out_d, out_d.ap[i_ * 128:(i_ + 1) * 128, :], xs[2 + i_], xs[2 + i_][:],
                             semt=xs[2 + i_]))
    for ev in finals:
        kb.wait_ev("sp", ev)
    kb.es.close()
    return nc


def extra_inputs(st):
    kb, nc = st["kb"], st["nc"]
    st["ropeC_d"] = T(nc.dram_tensor("ropeC", [128, SEQ], F32, kind="ExternalInput").ap())
    st["ropeS_d"] = T(nc.dram_tensor("ropeS", [128, SEQ], F32, kind="ExternalInput").ap())
    st["rotm_d"] = T(nc.dram_tensor("rotm", [128, 128], F32, kind="ExternalInput").ap())
    st["dalam_d"] = T(nc.dram_tensor("da_lambda", [DEPTH, 256], F32, kind="ExternalInput").ap())
    st["subln_d"] = T(nc.dram_tensor("da_subln_g", [DEPTH, 128], F32, kind="ExternalInput").ap())
    st["mix_d"] = kb.dram("mix_scr", [NT, D], BF16)
    st["wup_d"] = T(nc.dram_tensor("ffn_w_up", [DEPTH, D, 2 * DFF], F32, kind="ExternalInput").ap())
    st["wdn_d"] = T(nc.dram_tensor("ffn_w_down", [DEPTH, DFF, D], F32, kind="ExternalInput").ap())
    st["fcw_d"] = T(nc.dram_tensor("ffn_cw", [DEPTH, 128, 44, 3], F32, kind="ExternalInput").ap())
    st["fcb_d"] = T(nc.dram_tensor("ffn_cb", [DEPTH, 128, 44], F32, kind="ExternalInput").ap())
    st["hglb_d"] = T(nc.dram_tensor("hg_lb_c", [128, DEPTH, 4], F32, kind="ExternalInput").ap())
    st["hgng_d"] = T(nc.dram_tensor("hg_ng_c", [128, DEPTH], F32, kind="ExternalInput").ap())
    for nm, shp in (("cmask", [128, 512]), ("bdm", [128, 128]), ("hgmask", [32, 2, 64])):
        d_ = T(nc.dram_tensor("c_" + nm, shp, F32, kind="ExternalInput").ap())
        t_ = kb.sb(nm, shp)
        kb.dma("sp", t_, t_[:], d_, d_[:])
        st[nm] = t_
    st["ogd_d"] = kb.dram("ogd_scr", [64, 4, NT])
    st["gqkv_d"] = kb.dram("gqkv_scr", [12, 64, NT], BF16)
    st["gdcw128_d"] = T(nc.dram_tensor("gd_cw128", [128, DEPTH, 6, 3], F32, kind="ExternalInput").ap())
    st["gdcw_d"] = T(nc.dram_tensor("gd_cw_c", [64, DEPTH, 12, 3], F32, kind="ExternalInput").ap())
    st["gdng_d"] = T(nc.dram_tensor("gd_ng_c", [64, DEPTH], F32, kind="ExternalInput").ap())
    st["gddt_d"] = T(nc.dram_tensor("gd_dtb", [DEPTH, 8], F32, kind="ExternalInput").ap())
    st["gdal_d"] = T(nc.dram_tensor("gd_alog", [DEPTH, 8], F32, kind="ExternalInput").ap())
    for nm in ("M1c", "M2c", "M3c", "Ic"):
        st["gc_" + nm] = T(nc.dram_tensor("c_g" + nm, [64, 512], F32, kind="ExternalInput").ap())
    for nm in ("gm_L", "gm_U", "gm_LI", "gm_UI", "gm_ones"):
        d_ = T(nc.dram_tensor("c_" + nm, [64, 64], F32, kind="ExternalInput").ap())
        t_ = kb.sb(nm, [64, 64])
        kb.dma("sp", t_, t_[:], d_, d_[:])
        st[nm] = t_
    bdb = kb.sb("bdb", [128, 128], BF16)
    kb.op("dve", lambda e: e.tensor_copy(out=bdb[:], in_=st["bdm"][:]), [st["bdm"]], [bdb])
    st["bdb"] = bdb
    rotf = kb.sb("rotf", [128, 128])
    rotb = kb.sb("rotb", [128, 128], BF16)
    kb.dma("sp", rotf, rotf[:], st["rotm_d"], st["rotm_d"][:])
    kb.op("dve", lambda e: e.tensor_copy(out=rotb[:], in_=rotf[:]), [rotf], [rotb])
    st["rotb"] = rotb
    epst = kb.sb("epst", [128, 1])
    kb.op("dve", lambda e: e.memset(epst[:], EPS), [], [epst])
    st["epst"] = epst


def load_w(st, wt, w_d, l, col0, ncols, dcol0=0, q="pool"):
    kb = st["kb"]
    kb.dma(q, wt, wt[:, :, dcol0:dcol0 + ncols], w_d,
           w_d.ap[l, :, col0:col0 + ncols].rearrange("(c p) n -> p c n", p=128))


def rstd_cols(st, src_ap, dst_ap, n, tmp_ap, rd, wr):
    kb = st["kb"]
    epst = st["epst"]
    np_ = src_ap.shape[0]
    kb.op("act", lambda e: e.activation(out=tmp_ap, in_=src_ap, func=AF.Ln, scale=1.0 / n, bias=epst[0:np_, :]),
          rd + [epst], wr)
    kb.op("act", lambda e: e.activation(out=dst_ap, in_=tmp_ap, func=AF.Exp, scale=-0.5), wr, wr)


def norm_to_hT(st, l, m, tiles=range(18)):
    kb = st["kb"]
    xs, hT, stat, junk, xn, identb = st["xs"], st["hT"], st["stat"], st["junk"], st["xn"], st["identb"]
    gs_c, sh_c = st["gs_c"], st["sh_c"]
    for i in tiles:
        kb.op("dve", lambda e, i=i: e.tensor_tensor(out=junk[:], in0=xs[i][:], in1=xs[i][:], op=ALU.mult),
              [xs[i]], [junk])
        kb.op("dve", lambda e, i=i: e.reduce_sum(out=stat[:, i:i + 1], in_=junk[:], axis=AX.X), [junk], [stat])
    rstd_cols(st, stat[:, 0:18], stat[:, 36:54], D, stat[:, 18:36], [stat], [stat])
    for i in tiles:
        r = 1 if i < 2 else 0
        kb.op("dve", lambda e, i=i: e.tensor_scalar(out=xn[:], in0=xs[i][:], scalar1=stat[:, 36 + i:37 + i],
                                                    scalar2=None, op0=ALU.mult), [xs[i], stat], [xn])
        pst = kb.ps()
        pb = pst.ap.bitcast(BF16)
        for kc in range(KC):
            kb.op("pe", lambda e, kc=kc: e.transpose(out=pb[:, kc * 128:(kc + 1) * 128],
                                                      in_=xn[:, kc * 128:(kc + 1) * 128], identity=identb[:]),
                  [xn, identb], [pst])
        for kc in range(KC):
            kb.op("act", lambda e, kc=kc, i=i, r=r: e.activation(
                out=hT[:, kc, i * 128:(i + 1) * 128], in_=pb[:, kc * 128:(kc + 1) * 128], func=AF.Identity,
                scale=gs_c[:, l, m, kc, r:r + 1], bias=sh_c[:, l, m, kc, r:r + 1]), [pst, gs_c, sh_c], [hT])


def dump_hT(st, name):
    kb = st["kb"]
    d = st["dbg_d"][name]
    for kc in range(KC):
        st["finals"].append(kb.dma("sp", d, d.ap[:, kc, :], st["hT"], st["hT"][:, kc, :], semt=st["hT"]))


def dump(st, name, t, ap):
    kb = st["kb"]
    d = st["dbg_d"][name]
    st["finals"].append(kb.dma("sp", d, d[:], t, ap, semt=t))


def da_mixer(st, l):
    kb = st["kb"]
    hT, wbuf, w_in_d, stat, rotb, dbg_d = st["hT"], st["wbuf"], st["w_in_d"], st["stat"], st["rotb"], st["dbg_d"]
    PB = kb.psum_banks
    lam_init = 0.8 - 0.6 * math.exp(-0.3 * l)
    need_ctx = l < DEPTH - 1
    with kb.phase() as ph:
        kT = ph.sb("kT", [128, 4, NT], BF16)
        vaug = ph.sb("vaug", [128, 18, 4, 132], BF16)
        qTs = [ph.sb("qT%d" % i, [128, 512], BF16) for i in range(2)]
        rawb = ph.sb("rawb", [128, 512], BF16)
        rc = ph.sb("rc", [128, 512])
        rs = ph.sb("rs", [128, 512])
        t1 = ph.sb("t1", [128, 512])
        t2 = ph.sb("t2", [128, 512])
        pts = [ph.sb("pt%d" % i, [128, 512], BF16) for i in range(2)]
        o0 = ph.sb("o0", [128, 4, 128])
        o1s = [ph.sb("o1_%d" % i, [128, 128]) for i in range(4)]
        ostage = ph.sb("ostage", [128, 4, 512], BF16)
        lamt = ph.sb("lamt", [128, 256])
        lamj = ph.sb("lamj", [128, 128])
        lams = ph.sb("lams", [128, 8])
        sg = ph.sb("sg", [128, 128])
        dst = ph.sb("dast", [128, 16])
        dst2 = ph.sb("dast2", [128, 16])
        kb.dma("sp", lamt, lamt[:], st["dalam_d"], st["dalam_d"].ap[l:l + 1, :].partition_broadcast(128))
        kb.dma("sp", sg, sg[:], st["subln_d"], st["subln_d"].ap[l:l + 1, :].partition_broadcast(128))
        for j in range(2):
            kb.op("dve", lambda e, j=j: e.tensor_tensor(out=lamj[:, 0:64], in0=lamt[:, 128 * j:128 * j + 64],
                                                        in1=lamt[:, 128 * j + 64:128 * j + 128], op=ALU.mult),
                  [lamt], [lamj])
            kb.op("dve", lambda e, j=j: e.reduce_sum(out=lams[:, j:j + 1], in_=lamj[:, 0:64], axis=AX.X),
                  [lamj], [lams])
        kb.op("act", lambda e: e.activation(out=lams[:, 2:4], in_=lams[:, 0:2], func=AF.Exp), [lams], [lams])
        kb.op("dve", lambda e: e.scalar_tensor_tensor(out=lams[:, 4:5], in0=lams[:, 3:4], scalar=-lam_init,
                                                      in1=lams[:, 2:3], op0=ALU.add, op1=ALU.subtract),
              [lams], [lams])
        kb.op("dve", lambda e: e.tensor_scalar(out=sg[:], in0=sg[:], scalar1=1.0 - lam_init, scalar2=None,
                                               op0=ALU.mult), [sg], [sg])
        for i_ in range(18):
            kb.op("pool", lambda e, i_=i_: e.memset(vaug[:, i_, :, :], 1.0), [], [vaug])

        def rope_block(ps_t, n, tok0, dst_t, dst_ap, lat0):
            kb.dma("sp", rc, rc[:, 0:n], st["ropeC_d"], st["ropeC_d"][:, lat0:lat0 + n])
            kb.dma("sp", rs, rs[:, 0:n], st["ropeS_d"], st["ropeS_d"][:, lat0:lat0 + n])
            kb.op("act", lambda e: e.activation(out=rawb[:, 0:n], in_=ps_t[:, 0:n], func=AF.Identity), [ps_t], [rawb])
            ps2 = PB[6]
            kb.op("pe", lambda e: e.matmul(ps2[:, 0:n], lhsT=rotb[:], rhs=rawb[:, 0:n], start=True, stop=True),
                  [rotb, rawb], [ps2])
            kb.op("dve", lambda e: e.tensor_tensor(out=t1[:, 0:n], in0=ps_t[:, 0:n], in1=rc[:, 0:n], op=ALU.mult),
                  [ps_t, rc], [t1])
            kb.op("dve", lambda e: e.tensor_tensor(out=t2[:, 0:n], in0=ps2[:, 0:n], in1=rs[:, 0:n], op=ALU.mult),
                  [ps2, rs], [t2])
            kb.op("dve", lambda e: e.tensor_tensor(out=dst_ap, in0=t1[:, 0:n], in1=t2[:, 0:n], op=ALU.add),
                  [t1, t2], [dst_t])

        def proj_fm(wt, wc0, tok0, n, ps_t):
            for kc in range(KC):
                kb.op("pe", lambda e, kc=kc: e.matmul(ps_t[:, 0:n], lhsT=wt[:, kc, wc0:wc0 + 128],
                                                      rhs=hT[:, kc, tok0:tok0 + n], start=(kc == 0),
                                                      stop=(kc == KC - 1)), [wt, hT], [ps_t])

        def rope_ops(ps_t, n, dst_t, dst_ap, lat0):
            ps2 = PB[6]
            return [
                lambda: kb.dma("sp", rc, rc[:, 0:n], st["ropeC_d"], st["ropeC_d"][:, lat0:lat0 + n]),
                lambda: kb.dma("sp", rs, rs[:, 0:n], st["ropeS_d"], st["ropeS_d"][:, lat0:lat0 + n]),
                lambda: kb.op("act", lambda e: e.activation(out=rawb[:, 0:n], in_=ps_t[:, 0:n], func=AF.Identity),
                              [ps_t], [rawb]),
                lambda: kb.op("pe", lambda e: e.matmul(ps2[:, 0:n], lhsT=rotb[:], rhs=rawb[:, 0:n], start=True, stop=True),
                              [rotb, rawb], [ps2]),
                lambda: kb.op("dve", lambda e: e.tensor_tensor(out=t1[:, 0:n], in0=ps_t[:, 0:n], in1=rc[:, 0:n], op=ALU.mult),
                              [ps_t, rc], [t1]),
                lambda: kb.op("dve", lambda e: e.tensor_tensor(out=t2[:, 0:n], in0=ps2[:, 0:n], in1=rs[:, 0:n], op=ALU.mult),
                              [ps2, rs], [t2]),
                lambda: kb.op("dve", lambda e: e.tensor_tensor(out=dst_ap, in0=t1[:, 0:n], in1=t2[:, 0:n], op=ALU.add),
                              [t1, t2], [dst_t]),
            ]

        def q_ops(tok0, nq, is_lat, h, qdst):
            pq = PB[7]
            ops = []
            for kc in range(KC):
                ops.append(lambda kc=kc: kb.op("pe", lambda e: e.matmul(
                    pq[:, 0:nq], lhsT=wq[:, kc, h * 128:(h + 1) * 128], rhs=hT[:, kc, tok0:tok0 + nq],
                    start=(kc == 0), stop=(kc == KC - 1)), [wq, hT], [pq]))
            if is_lat:
                ops += rope_ops(pq, nq, qdst, qdst[:, 0:nq], tok0 - 256)
            else:
                ops.append(lambda: kb.op("act", lambda e: e.activation(out=qdst[:, 0:nq], in_=pq[:, 0:nq], func=AF.Identity),
                                         [pq], [qdst]))
            return ops

        wv = wbuf[0]
        load_w(st, wv, w_in_d, l, O_DAV, 512)
        for i in range(18):
            pv = PB[4 + (i % 2)]
            for kc in range(KC):
                kb.op("pe", lambda e, kc=kc, i=i: e.matmul(pv[:, :], lhsT=hT[:, kc, i * 128:(i + 1) * 128],
                                                           rhs=wv[:, kc, :], start=(kc == 0), stop=(kc == KC - 1)),
                      [wv, hT], [pv])
            for h_ in range(4):
                kb.op("act", lambda e, i=i, h_=h_: e.activation(out=vaug[:, i, h_, 0:128],
                                                         in_=pv[:, h_ * 128:(h_ + 1) * 128], func=AF.Identity), [pv], [vaug])
        wk = wbuf[1]
        load_w(st, wk, w_in_d, l, O_DAK, 512)
        for h in range(4):
            pk = PB[4 + (h % 2)]
            proj_fm(wk, h * 128, 0, 256, pk)
            kb.op("act", lambda e, h=h: e.activation(out=kT[:, h, 0:256], in_=pk[:, 0:256], func=AF.Identity), [pk], [kT])
            for j in range(4):
                pk = PB[4 + (j % 2)]
                proj_fm(wk, h * 128, 256 + 512 * j, 512, pk)
                rope_block(pk, 512, 256 + 512 * j, kT, kT[:, h, 256 + 512 * j:256 + 512 * (j + 1)], 512 * j)
        if "kT" in dbg_d and l == 0:
            for h_ in range(4):
                d_ = st["dbg_d"]["kT"]
                st["finals"].append(kb.dma("sp", d_, d_.ap[:, h_, :], kT, kT[:, h_, :], semt=kT))
        wq = wbuf[0]
        load_w(st, wq, w_in_d, l, O_DAQ, 512)
        qblocks = [(256 + 512 * j, 512, True, list(range(18))) for j in range(4)]
        if need_ctx:
            qblocks.append((0, 256, False, [0, 1]))
        pairs = [(tok0, nq, is_lat, ktiles, h) for (tok0, nq, is_lat, ktiles) in qblocks for h in range(4)]
        for op_ in q_ops(pairs[0][0], pairs[0][1], pairs[0][2], pairs[0][4], qTs[0]):
            op_()
        for pi_, (tok0, nq, is_lat, ktiles, h) in enumerate(pairs):
            nqt = nq // 128
            qT = qTs[pi_ % 2]
            if pi_ + 1 < len(pairs):
                nx = pairs[pi_ + 1]
                pending = q_ops(nx[0], nx[1], nx[2], nx[4], qTs[(pi_ + 1) % 2])
            else:
                pending = []
            if True:
                for m in range(2):
                    def score(ki, m=m, h=h):
                        kt = ktiles[ki]
                        sps = PB[4 + (ki % 2)]
                        kb.op("pe", lambda e: e.matmul(
                            sps[:, 0:nq], lhsT=kT[m * 64:(m + 1) * 64, h, kt * 128:(kt + 1) * 128],
                            rhs=qT[m * 64:(m + 1) * 64, 0:nq], start=True, stop=True), [kT, qT], [sps])
                        pt = pts[ki % 2]
                        kb.op("act", lambda e: e.activation(out=pt[:, 0:nq], in_=sps[:, 0:nq], func=AF.Exp,
                                                            scale=0.125), [sps], [pt])

                    def pv(ki, h=h):
                        kt = ktiles[ki]
                        pt = pts[ki % 2]
                        for qi in range(nqt):
                            kb.op("pe", lambda e, qi=qi: e.matmul(
                                PB[qi][:, 0:129], lhsT=pt[:, qi * 128:(qi + 1) * 128], rhs=vaug[:, kt, h, 0:129],
                                start=(ki == 0), stop=(ki == len(ktiles) - 1)), [pt, vaug], [PB[qi]])

                    score(0)
                    for ki in range(len(ktiles)):
                        if ki + 1 < len(ktiles):
                            score(ki + 1)
                        pv(ki)
                        if pending:
                            pending.pop(0)()
                    for qi in range(nqt):
                        acc = PB[qi]
                        c = m * 4 + qi
                        kb.op("dve", lambda e, acc=acc, c=c: e.reciprocal(out=dst[:, c:c + 1], in_=acc[:, 128:129]),
                              [acc], [dst])
                        if m == 0:
                            kb.op("dve", lambda e, acc=acc, c=c, qi=qi: e.tensor_scalar(
                                out=o0[:, qi, :], in0=acc[:, 0:128], scalar1=dst[:, c:c + 1], scalar2=None,
                                op0=ALU.mult), [acc, dst], [o0])
                        else:
                            kb.op("dve", lambda e, c=c: e.tensor_tensor(out=dst[:, c:c + 1], in0=dst[:, c:c + 1],
                                                                        in1=lams[:, 4:5], op=ALU.mult),
                                  [dst, lams], [dst])
                            kb.op("dve", lambda e, acc=acc, c=c, qi=qi: e.scalar_tensor_tensor(
                                out=o1s[qi][:], in0=acc[:, 0:128], scalar=dst[:, c:c + 1], in1=o0[:, qi, :],
                                op0=ALU.mult, op1=ALU.add), [acc, dst, o0], [o1s[qi]])
                    if m == 1:
                        for qi in range(nqt):
                            kb.op("dve", lambda e, qi=qi: e.tensor_tensor(out=lamj[:], in0=o1s[qi][:], in1=o1s[qi][:],
                                                                          op=ALU.mult), [o1s[qi]], [lamj])
                            kb.op("dve", lambda e, qi=qi: e.reduce_sum(out=dst2[:, qi:qi + 1], in_=lamj[:], axis=AX.X),
                                  [lamj], [dst2])
                        rstd_cols(st, dst2[:, 0:nqt], dst2[:, 8:8 + nqt], 128, dst2[:, 4:4 + nqt], [dst2], [dst2])
                        for qi in range(nqt):
                            kb.op("dve", lambda e, qi=qi, h=h: e.scalar_tensor_tensor(
                                out=ostage[:, qi, h * 128:(h + 1) * 128], in0=o1s[qi][:], scalar=dst2[:, 8 + qi:9 + qi],
                                in1=sg[:], op0=ALU.mult, op1=ALU.mult), [o1s[qi], dst2, sg], [ostage])
            while pending:
                pending.pop(0)()
            if h == 3:
                for qi in range(nqt):
                    kb.dma("sp", st["mix_d"], st["mix_d"].ap[tok0 + qi * 128:tok0 + (qi + 1) * 128, 0:512],
                           ostage, ostage[:, qi, :], semt=ostage)


def out_proj(st, l):
    kb = st["kb"]
    xs, stat, junk, identb, wbuf = st["xs"], st["stat"], st["junk"], st["identb"], st["wbuf"]
    PB = kb.psum_banks
    mix_src = st.get("mix_in", st["mix_d"])
    tiles = list(range(18)) if l < DEPTH - 1 else list(range(2, 18))
    with kb.phase() as ph:
        gts = [ph.sb("gt%d" % r, [128, D]) for r in range(2)]
        mt = ph.sb("mt", [128, D], BF16)
        mT = ph.sb("mT", [128, KC, 128], BF16)
        sq = ph.sb("sq", [128, D])
        for r in range(2):
            kb.dma("sp", gts[r], gts[r][:], st["gates_d"], st["gates_d"].ap[l, 0, r:r + 1, :].partition_broadcast(128))
        for half in range(2):
            load_w(st, wbuf[half], st["w_out_d"], l, half * 512, 512)
        for i in tiles:
            r = 1 if i < 2 else 0
            kb.dma("sp", mt, mt[:], mix_src, mix_src.ap[i * 128:(i + 1) * 128, :])
            pst = PB[6 + (i % 2)]
            pb = pst.ap.bitcast(BF16)
            for kc in range(KC):
                kb.op("pe", lambda e, kc=kc: e.transpose(out=pb[:, kc * 128:(kc + 1) * 128],
                                                          in_=mt[:, kc * 128:(kc + 1) * 128], identity=identb[:]),
                      [mt, identb], [pst])
            kb.op("act", lambda e: e.activation(out=mT[:].rearrange("p c t -> p (c t)"), in_=pb[:, :],
                                                func=AF.Identity), [pst], [mT])
            pss = [PB[2 * (i % 2)], PB[2 * (i % 2) + 1]]
            for half in range(2):
                for kc in range(KC):
                    kb.op("pe", lambda e, kc=kc, half=half: e.matmul(
                        pss[half][:, :], lhsT=mT[:, kc, :], rhs=wbuf[half][:, kc, :], start=(kc == 0),
                        stop=(kc == KC - 1)), [mT, wbuf[half]], [pss[half]])
                kb.op("act", lambda e, half=half: e.activation(out=sq[:, half * 512:(half + 1) * 512],
                                                               in_=pss[half][:, :], func=AF.Square),
                      [pss[half]], [sq])
            kb.op("dve", lambda e: e.reduce_sum(out=stat[:, 56:57], in_=sq[:], axis=AX.X), [sq], [stat])
            rstd_cols(st, stat[:, 56:57], stat[:, 58:59], D, stat[:, 57:58], [stat], [stat])
            for half in range(2):
                kb.op("dve", lambda e, half=half, r=r: e.scalar_tensor_tensor(
                    out=sq[:, half * 512:(half + 1) * 512], in0=pss[half][:, :], scalar=stat[:, 58:59],
                    in1=gts[r][:, half * 512:(half + 1) * 512], op0=ALU.mult, op1=ALU.mult),
                    [pss[half], stat, gts[r]], [sq])
            kb.op("dve", lambda e, i=i: e.tensor_tensor(out=xs[i][:], in0=xs[i][:], in1=sq[:], op=ALU.add),
                  [xs[i], sq], [xs[i]])


def ffn(st, l):
    kb, nc = st["kb"], st["nc"]
    xs, hT, stat = st["xs"], st["hT"], st["stat"]
    PB = kb.psum_banks
    sbs = [(256 + 512 * j, 512, 1 if j > 0 else 0, 1 if j < 3 else 0) for j in range(4)]
    if l < DEPTH - 1:
        sbs.append((0, 256, 0, 0))
    NJ = DFF // 128
    with kb.phase() as ph:
        wdb = [ph.sb("wd%d" % i, [128, D], BF16) for i in range(3)]
        aT = ph.sb("aT", [128, NJ, 512], BF16)
        ur_ = [[ph.sb("ur%d_%d" % (g, p_), [128, 516]) for g in range(2)] for p_ in range(2)]
        cg_ = [[ph.sb("cg%d_%d" % (g, p_), [128, 512]) for g in range(2)] for p_ in range(2)]
        sgt_ = [ph.sb("sgt%d" % p_, [128, 512]) for p_ in range(2)]
        wu = [[ph.sb("wu%d_%d" % (i, g_), [128, KC, 128], BF16) for g_ in range(2)] for i in range(3)]
        cw = ph.sb("cw", [128, 2 * NJ, 3])
        cb = ph.sb("cb", [128, 2 * NJ])
        gts = [ph.sb("fgt%d" % r, [128, D]) for r in range(2)]
        sq = st["junk"]
        kb.dma("sp", cw, cw[:], st["fcw_d"], st["fcw_d"].ap[l])
        kb.dma("sp", cb, cb[:], st["fcb_d"], st["fcb_d"].ap[l])
        for r in range(2):
            kb.dma("sp", gts[r], gts[r][:], st["gates_d"], st["gates_d"].ap[l, 1, r:r + 1, :].partition_broadcast(128))
        wi = 0
        wdi = 0
        for (tok0, n, hl, hr) in sbs:
            w = n + hl + hr
            c0 = tok0 - hl
            blocks = [(0, w // 2), (w // 2, w - w // 2)] if w > 512 else [(0, w)]
            for p_ in range(2):
                for g in range(2):
                    if not hl:
                        kb.op("dve", lambda e, g=g, p_=p_: e.memset(ur_[p_][g][:, 0:1], 0.0), [], [ur_[p_][g]])
                    if not hr:
                        kb.op("dve", lambda e, g=g, n=n, p_=p_: e.memset(ur_[p_][g][:, n + 1:n + 2], 0.0), [], [ur_[p_][g]])
            for j in range(NJ):
                ur, cg, sgt = ur_[j % 2], cg_[j % 2], sgt_[j % 2]
                wt = wu[wi % 3]
                wi += 1
                load_w(st, wt[0], st["wup_d"], l, j * 128, 128, 0)
                load_w(st, wt[1], st["wup_d"], l, DFF + j * 128, 128, 0)
                for g in range(2):
                    for bi, (b0, bw) in enumerate(blocks):
                        pu = PB[4 + ((2 * g + bi) % 4)]
                        for kc in range(KC):
                            kb.op("pe", lambda e, kc=kc, g=g, b0=b0, bw=bw, pu=pu, wt=wt: e.matmul(
                                pu[:, 0:bw], lhsT=wt[g][:, kc, :],
                                rhs=hT[:, kc, c0 + b0:c0 + b0 + bw], start=(kc == 0), stop=(kc == KC - 1)),
                                [wt[g], hT], [pu])
                        o0_ = 1 - hl + b0
                        kb.op("act", lambda e, g=g, pu=pu, bw=bw, o0_=o0_: e.activation(
                            out=ur[g][:, o0_:o0_ + bw], in_=pu[:, 0:bw], func=AF.Identity), [pu], [ur[g]])
                    ch = g * NJ + j
                    kb.op("dve", lambda e, g=g, ch=ch, n=n: e.tensor_scalar(
                        out=cg[g][:, 0:n], in0=ur[g][:, 0:n], scalar1=cw[:, ch, 0:1], scalar2=cb[:, ch:ch + 1],
                        op0=ALU.mult, op1=ALU.add), [ur[g], cw, cb], [cg[g]])
                    for k_ in (1, 2):
                        kb.op("dve", lambda e, g=g, ch=ch, n=n, k_=k_: e.scalar_tensor_tensor(
                            out=cg[g][:, 0:n], in0=ur[g][:, k_:k_ + n], scalar=cw[:, ch, k_:k_ + 1],
                            in1=cg[g][:, 0:n], op0=ALU.mult, op1=ALU.add), [ur[g], cw, cg[g]], [cg[g]])
                kb.op("act", lambda e, n=n: e.activation(out=sgt[:, 0:n], in_=cg[0][:, 0:n], func=AF.Silu),
                      [cg[0]], [sgt])
                kb.op("dve", lambda e, n=n, j=j: e.tensor_tensor(out=aT[:, j, 0:n], in0=sgt[:, 0:n],
                                                                 in1=cg[1][:, 0:n], op=ALU.mult),
                      [sgt, cg[1]], [aT])
            nt = n // 128
            for p0 in range(0, nt, 4):
                ntp = min(4, nt - p0)
                for j in range(NJ):
                    wd = wdb[wdi % 3]
                    wdi += 1
                    kb.dma("pool", wd, wd[:], st["wdn_d"], st["wdn_d"].ap[l, j * 128:(j + 1) * 128, :])
                    for t in range(ntp):
                        for half in range(2):
                            acc = PB[t * 2 + half]
                            kb.op("pe", lambda e, j=j, t=t, half=half, acc=acc, p0=p0, wd=wd: e.matmul(
                                acc[:, :], lhsT=aT[:, j, (p0 + t) * 128:(p0 + t + 1) * 128],
                                rhs=wd[:, half * 512:(half + 1) * 512], start=(j == 0), stop=(j == NJ - 1)),
                                [aT, wd], [acc])
                for t in range(ntp):
                    i = tok0 // 128 + p0 + t
                    r = 1 if i < 2 else 0
                    pss = [PB[t * 2], PB[t * 2 + 1]]
                    for half in range(2):
                        kb.op("act", lambda e, half=half, pss=pss: e.activation(
                            out=sq[:, half * 512:(half + 1) * 512], in_=pss[half][:, :], func=AF.Square),
                            [pss[half]], [sq])
                    kb.op("dve", lambda e: e.reduce_sum(out=stat[:, 56:57], in_=sq[:], axis=AX.X), [sq], [stat])
                    rstd_cols(st, stat[:, 56:57], stat[:, 58:59], D, stat[:, 57:58], [stat], [stat])
                    for half in range(2):
                        kb.op("dve", lambda e, half=half, r=r, pss=pss: e.scalar_tensor_tensor(
                            out=sq[:, half * 512:(half + 1) * 512], in0=pss[half][:, :], scalar=stat[:, 58:59],
                            in1=gts[r][:, half * 512:(half + 1) * 512], op0=ALU.mult, op1=ALU.mult),
                            [pss[half], stat, gts[r]], [sq])
                    kb.op("dve", lambda e, i=i: e.tensor_tensor(out=xs[i][:], in0=xs[i][:], in1=sq[:], op=ALU.add),
                          [xs[i], sq], [xs[i]])


def proj_fm_g(st, wt, wc0, tok0, n, ps_t, ncols=128):
    kb, hT = st["kb"], st["hT"]
    for kc in range(KC):
        kb.op("pe", lambda e, kc=kc: e.matmul(ps_t[0:ncols, 0:n], lhsT=wt[:, kc, wc0:wc0 + ncols],
                                              rhs=hT[:, kc, tok0:tok0 + n], start=(kc == 0),
                                              stop=(kc == KC - 1)), [wt, hT], [ps_t])


def store_fm_to_mix(st, ph_tiles, res, tok0, n, col0):
    kb = st["kb"]
    PB = kb.psum_banks
    stage = ph_tiles["stage"]
    identb = st["identb"]
    for ti in range(n // 128):
        pst = PB[2 + (ti % 2)]
        pb = pst.ap.bitcast(BF16)
        kb.op("pe", lambda e, ti=ti: e.transpose(out=pb[:, 0:128], in_=res[:, ti * 128:(ti + 1) * 128],
                                                  identity=identb[:]), [res, identb], [pst])
        kb.op("act", lambda e, ti=ti: e.activation(out=stage[:, ti, :], in_=pb[:, 0:128], func=AF.Identity),
              [pst], [stage])
    for ti in range(n // 128):
        kb.dma("sp", st["mix_d"], st["mix_d"].ap[tok0 + ti * 128:tok0 + (ti + 1) * 128, col0:col0 + 128],
               stage, stage[:, ti, :], semt=stage)


def hg_mixer(st, l):
    kb, nc = st["kb"], st["nc"]
    hT, wbuf, w_in_d = st["hT"], st["wbuf"], st["w_in_d"]
    PB = kb.psum_banks
    C = 32
    blocks = [(0, 256)] + [(256 + 512 * j, 512) for j in range(4)]
    with kb.phase() as ph:
        lbl = ph.sb("lbl", [128, DEPTH, 4])
        lbs = ph.sb("lbs", [128, 8, 4])
        ngc_h = ph.sb("ngc_h", [128, DEPTH])
        OT = ph.sb("OT", [128, 2, NT])
        wqg = ph.sb("wqg", [128, KC, 512], BF16)
        wv = wbuf[0]
        wf = wbuf[1]
        E = ph.sb("hE", [128, 512])
        SG = ph.sb("hSG", [128, 512])
        LF = ph.sb("hLF", [128, 512])
        KT = ph.sb("hKT", [128, 512])
        Bb = ph.sb("hB", [128, 512])
        Bd = ph.sb("hBd", [128, 512])
        EX = ph.sb("hEX", [128, 512])
        QS = ph.sb("hQS", [128, 512])
        eBt = [ph.sb("heBt%d" % i, [128, 16]) for i in range(4)]
        qt = [ph.sb("hqt%d" % i, [128, 512], BF16) for i in range(4)]
        kt = [ph.sb("hkt%d" % i, [128, 512], BF16) for i in range(4)]
        kh = [ph.sb("hkh%d" % i, [128, 512], BF16) for i in range(4)]
        vT = [ph.sb("hvT%d" % i, [128, 512], BF16) for i in range(4)]
        S = [ph.sb("hS%d" % i, [128, 128]) for i in range(4)]
        Sb = [ph.sb("hSb%d" % i, [128, 128], BF16) for i in range(4)]
        tmpS = [ph.sb("htS%d" % i, [128, 128]) for i in range(4)]
        sc = [ph.sb("hsc%d" % i, [32, 64], BF16) for i in range(4)]
        khat = [ph.sb("hkhat%d" % i, [32, 128], BF16) for i in range(4)]
        vch = [ph.sb("hvch%d" % i, [32, 128], BF16) for i in range(4)]
        vm0 = [ph.sb("hvm0%d" % i, [32, 128], BF16) for i in range(4)]
        vm1 = [ph.sb("hvm1%d" % i, [32, 128], BF16) for i in range(4)]
        RES = ph.sb("hRES", [128, 512], BF16)
        stage = T(st["xn"].ap[:, 0:512].rearrange("p (a b) -> p a b", a=4), "hstage")
        cm = st["cmask"]
        bdm, bdb = st["bdm"], st["bdb"]
        mk = st["hgmask"]
        kb.dma("sp", lbl, lbl[:], st["hglb_d"], st["hglb_d"][:])
        kb.dma("sp", ngc_h, ngc_h[:], st["hgng_d"], st["hgng_d"][:])
        kb.op("act", lambda e: e.activation(out=lbl[:], in_=lbl[:], func=AF.Exp), [lbl], [lbl])
        kb.op("dve", lambda e: e.tensor_copy(out=lbs[:, 0, :], in_=lbl[:, 0, :]), [lbl], [lbs])
        for j in range(1, DEPTH):
            kb.op("dve", lambda e, j=j: e.tensor_tensor(out=lbs[:, 0, :], in0=lbs[:, 0, :], in1=lbl[:, j, :],
                                                        op=ALU.add), [lbs, lbl], [lbs])
        kb.op("dve", lambda e: e.reciprocal(out=lbs[:, 1, :], in_=lbs[:, 0, :]), [lbs], [lbs])
        kb.op("dve", lambda e: e.memset(lbs[:, 2, :], 0.0), [], [lbs])
        for j in range(1, l + 1):
            kb.op("dve", lambda e, j=j: e.tensor_tensor(out=lbs[:, 2, :], in0=lbs[:, 2, :], in1=lbl[:, j, :],
                                                        op=ALU.add), [lbs, lbl], [lbs])
        kb.op("dve", lambda e: e.tensor_tensor(out=lbs[:, 3, :], in0=lbs[:, 2, :], in1=lbs[:, 1, :], op=ALU.mult),
              [lbs], [lbs])
        kb.op("dve", lambda e: e.tensor_scalar(out=lbs[:, 4, :], in0=lbs[:, 3, :], scalar1=-1.0, scalar2=1.0,
                                               op0=ALU.mult, op1=ALU.add), [lbs], [lbs])
        kb.op("dve", lambda e: e.tensor_scalar(out=lbs[:, 5, :], in0=lbs[:, 3, :], scalar1=-1.0, scalar2=None,
                                               op0=ALU.add), [lbs], [lbs])
        kb.op("dve", lambda e: e.tensor_scalar(out=lbs[:, 6, :], in0=lbs[:, 3, :], scalar1=1e-30, scalar2=None,
                                               op0=ALU.max), [lbs], [lbs])
        load_w(st, wqg, w_in_d, l, O_HGQ, 256, 0)
        load_w(st, wqg, w_in_d, l, O_HGG, 256, 256)
        load_w(st, wv, w_in_d, l, O_HGI, 256, 0)
        load_w(st, wf, w_in_d, l, O_HGF, 512, 0)
        rot = [0]

        def psr():
            t = PB[4 + (rot[0] % 4)]
            rot[0] += 1
            return t

        for q in range(4):
            kb.op("dve", lambda e, q=q: e.memset(S[q][:], 0.0), [], [S[q]])
            kb.op("dve", lambda e, q=q: e.memset(Sb[q][:], 0.0), [], [Sb[q]])
            kb.op("dve", lambda e, q=q: e.memset(vm0[q][:], 0.0), [], [vm0[q]])
            kb.op("dve", lambda e, q=q: e.memset(vm1[q][:], 0.0), [], [vm1[q]])
        borders = [blocks, [blocks[0]] + blocks[:0:-1]]
        written = set()
        for bi in range(len(blocks)):
            for dr in range(2):
                tok0, n = borders[dr][bi]
                nch = n // C
                for hp in range(2):
                    q = dr * 2 + hp
                    c = dr * 2 + hp
                    pz = psr()
                    proj_fm_g(st, wf, c * 128, tok0, n, pz)
                    kb.op("act", lambda e, pz=pz: e.activation(out=E[:, 0:n], in_=pz[:, 0:n], func=AF.Exp, scale=-1.0),
                          [pz], [E])
                    kb.op("dve", lambda e: e.tensor_scalar(out=SG[:, 0:n], in0=E[:, 0:n], scalar1=1.0, scalar2=None,
                                                           op0=ALU.add), [E], [SG])
                    kb.op("dve", lambda e: e.reciprocal(out=SG[:, 0:n], in_=SG[:, 0:n]), [SG], [SG])
                    kb.op("dve", lambda e, c=c: e.tensor_scalar(out=E[:, 0:n], in0=SG[:, 0:n], scalar1=lbs[:, 4, c:c + 1],
                                                                scalar2=lbs[:, 6, c:c + 1], op0=ALU.mult, op1=ALU.add),
                          [SG, lbs], [E])
                    kb.op("act", lambda e: e.activation(out=LF[:, 0:n], in_=E[:, 0:n], func=AF.Ln), [E], [LF])
                    kb.op("dve", lambda e, c=c: e.tensor_scalar(out=KT[:, 0:n], in0=SG[:, 0:n], scalar1=-1.0,
                                                                scalar2=lbs[:, 5, c:c + 1], op0=ALU.add, op1=ALU.mult),
                          [SG, lbs], [KT])
                    kb.op("dve", lambda e: e.tensor_tensor_scan(out=Bb[:, 0:n], data0=cm[:, 0:n], data1=LF[:, 0:n],
                                                                initial=0.0, op0=ALU.mult, op1=ALU.add),
                          [cm, LF], [Bb])
                    B3 = Bb[:, 0:n].rearrange("p (c k) -> p c k", k=C)
                    Bd3 = Bd[:, 0:n].rearrange("p (c k) -> p c k", k=C)
                    LF3 = LF[:, 0:n].rearrange("p (c k) -> p c k", k=C)
                    btb = B3[:, :, C - 1:C].to_broadcast([128, nch, C])
                    if dr == 1:
                        kb.op("dve", lambda e, btb=btb, B3=B3, Bd3=Bd3: e.tensor_tensor(out=Bd3, in0=btb, in1=B3,
                                                                                      op=ALU.subtract), [Bb], [Bd])
                        kb.op("dve", lambda e, Bd3=Bd3, LF3=LF3: e.tensor_tensor(out=Bd3, in0=Bd3, in1=LF3, op=ALU.add),
                              [Bd, LF], [Bd])
                    else:
                        kb.op("dve", lambda e: e.tensor_copy(out=Bd[:, 0:n], in_=Bb[:, 0:n]), [Bb], [Bd])
                    kb.op("act", lambda e, hp=hp, q=q, B3=B3: e.activation(
                        out=eBt[q][:, 0:nch], in_=B3[:, :, C - 1], func=AF.Exp), [Bb], [eBt[q]])
                    pvv = psr()
                    proj_fm_g(st, wv, hp * 128, tok0, n, pvv)
                    kb.op("act", lambda e, pvv=pvv, hp=hp, q=q: e.activation(out=vT[q][:, 0:n], in_=pvv[:, 0:n],
                                                                        func=AF.Identity), [pvv], [vT[q]])
                    pq = psr()
                    proj_fm_g(st, wqg, hp * 128, tok0, n, pq)
                    kb.op("act", lambda e, pq=pq: e.activation(out=QS[:, 0:n], in_=pq[:, 0:n], func=AF.Silu), [pq], [QS])
                    kb.op("act", lambda e: e.activation(out=EX[:, 0:n], in_=Bd[:, 0:n], func=AF.Exp), [Bd], [EX])
                    kb.op("dve", lambda e, hp=hp, q=q: e.tensor_tensor(out=qt[q][:, 0:n], in0=QS[:, 0:n], in1=EX[:, 0:n],
                                                                  op=ALU.mult), [QS, EX], [qt[q]])
                    kb.op("act", lambda e: e.activation(out=EX[:, 0:n], in_=Bd[:, 0:n], func=AF.Exp, scale=-1.0),
                          [Bd], [EX])
                    kb.op("dve", lambda e, hp=hp, q=q: e.tensor_tensor(out=kt[q][:, 0:n], in0=KT[:, 0:n], in1=EX[:, 0:n],
                                                                  op=ALU.mult), [KT, EX], [kt[q]])
                    kb.op("dve", lambda e, btb=btb, Bd3=Bd3: e.tensor_tensor(out=Bd3, in0=btb, in1=Bd3, op=ALU.subtract),
                          [Bb, Bd], [Bd])
                    kb.op("act", lambda e: e.activation(out=EX[:, 0:n], in_=Bd[:, 0:n], func=AF.Exp), [Bd], [EX])
                    kb.op("dve", lambda e, hp=hp, q=q: e.tensor_tensor(out=kh[q][:, 0:n], in0=KT[:, 0:n], in1=EX[:, 0:n],
                                                                  op=ALU.mult), [KT, EX], [kh[q]])
            nch = borders[0][bi][1] // C
            for ci in range(nch):
                for dr in range(2):
                    tok0, n = borders[dr][bi]
                    ck = ci if dr == 0 else nch - 1 - ci
                    c0 = ck * C
                    for hp in range(2):
                        q = dr * 2 + hp
                        pss_ = psr()
                        for hh in range(2):
                            kb.op("pe", lambda e, hh=hh, hp=hp, q=q, pss_=pss_: e.matmul(
                                pss_[0:32, hh * 32:(hh + 1) * 32], lhsT=kt[q][hh * 64:(hh + 1) * 64, c0:c0 + C],
                                rhs=qt[q][hh * 64:(hh + 1) * 64, c0:c0 + C], start=True, stop=True),
                                [kt[q], qt[q]], [pss_], pe_self=(hh == 1))
                        kb.op("dve", lambda e, hp=hp, q=q, pss_=pss_: e.tensor_tensor(
                            out=sc[q][:, :], in0=pss_[0:32, 0:64], in1=mk[:, dr, :], op=ALU.mult), [pss_, mk], [sc[q]])
                        psk = psr()
                        pkb = psk.ap.bitcast(BF16)
                        kb.op("pe", lambda e, hp=hp, q=q, pkb=pkb: e.transpose(out=pkb[0:32, 0:128], in_=kh[q][:, c0:c0 + C],
                                                                          identity=st["identb"][:]),
                              [kh[q], st["identb"]], [psk])
                        kb.op("act", lambda e, hp=hp, q=q, pkb=pkb: e.activation(out=khat[q][:, :], in_=pkb[0:32, 0:128],
                                                                            func=AF.Identity), [psk], [khat[q]])
                        psv = psr()
                        pvb = psv.ap.bitcast(BF16)
                        kb.op("pe", lambda e, hp=hp, q=q, pvb=pvb: e.transpose(out=pvb[0:32, 0:128], in_=vT[q][:, c0:c0 + C],
                                                                          identity=st["identb"][:]),
                              [vT[q], st["identb"]], [psv])
                        kb.op("act", lambda e, hp=hp, q=q, pvb=pvb: e.activation(out=vch[q][:, :], in_=pvb[0:32, 0:128],
                                                                            func=AF.Identity), [psv], [vch[q]])
                        kb.op("dve", lambda e, hp=hp, q=q: e.tensor_copy(out=vm0[q][:, 0:64], in_=vch[q][:, 0:64]),
                              [vch[q]], [vm0[q]])
                        kb.op("dve", lambda e, hp=hp, q=q: e.tensor_copy(out=vm1[q][:, 64:128], in_=vch[q][:, 64:128]),
                              [vch[q]], [vm1[q]])
                        po = PB[q]
                        kb.op("pe", lambda e, hp=hp, q=q, po=po: e.matmul(po[:, c0:c0 + C], lhsT=vm0[q][:, :],
                                                                     rhs=sc[q][:, 0:32], start=True, stop=False),
                              [vm0[q], sc[q]], [po])
                        kb.op("pe", lambda e, hp=hp, q=q, po=po: e.matmul(po[:, c0:c0 + C], lhsT=vm1[q][:, :],
                                                                     rhs=sc[q][:, 32:64], start=False, stop=False),
                              [vm1[q], sc[q]], [po])
                        kb.op("pe", lambda e, hp=hp, q=q, po=po: e.matmul(po[:, c0:c0 + C], lhsT=Sb[q][:, :],
                                                                     rhs=qt[q][:, c0:c0 + C], start=False, stop=True),
                              [Sb[q], qt[q]], [po])
                        pst_ = psr()
                        kb.op("pe", lambda e, hp=hp, q=q, pst_=pst_: e.matmul(pst_[:, 0:128], lhsT=khat[q][:, :],
                                                                         rhs=vch[q][:, :], start=True, stop=True),
                              [khat[q], vch[q]], [pst_])
                        kb.op("dve", lambda e, hp=hp, q=q, pst_=pst_: e.tensor_tensor(
                            out=tmpS[q][:, :], in0=pst_[:, 0:128], in1=bdm[:, :], op=ALU.mult), [pst_, bdm], [tmpS[q]])
                        kb.op("dve", lambda e, hp=hp, q=q, ck=ck: e.scalar_tensor_tensor(
                            out=S[q][:, :], in0=S[q][:, :], scalar=eBt[q][:, ck:ck + 1], in1=tmpS[q][:, :],
                            op0=ALU.mult, op1=ALU.add), [S[q], eBt[q], tmpS[q]], [S[q]])
                        kb.op("act", lambda e, hp=hp, q=q: e.activation(out=Sb[q][:, :], in_=S[q][:, :], func=AF.Identity),
                              [S[q]], [Sb[q]])

            for dr in range(2):
                tok0, n = borders[dr][bi]
                for hp in range(2):
                    po = PB[dr * 2 + hp]
                    if (hp, tok0) not in written:
                        written.add((hp, tok0))
                        kb.op("act", lambda e, hp=hp, po=po, tok0=tok0, n=n: e.activation(
                            out=OT[:, hp, tok0:tok0 + n], in_=po[:, 0:n], func=AF.Identity), [po], [OT])
                    else:
                        kb.op("dve", lambda e, hp=hp, po=po, tok0=tok0, n=n: e.tensor_tensor(
                            out=OT[:, hp, tok0:tok0 + n], in0=po[:, 0:n], in1=OT[:, hp, tok0:tok0 + n], op=ALU.add),
                            [po, OT], [OT])
        for (tok0, n) in blocks:
            for hp in range(2):
                kb.op("act", lambda e, hp=hp: e.activation(out=RES[:, 0:n], in_=OT[:, hp, tok0:tok0 + n],
                                                           func=AF.Square), [OT], [RES])
                pn = psr()
                kb.op("pe", lambda e, pn=pn: e.matmul(pn[:, 0:n], lhsT=bdb[:, :], rhs=RES[:, 0:n], start=True, stop=True),
                      [bdb, RES], [pn])
                kb.op("act", lambda e, pn=pn: e.activation(out=E[:, 0:n], in_=pn[:, 0:n], func=AF.Ln, scale=1.0 / 64,
                                                           bias=st["epst"][:, :]), [pn, st["epst"]], [E])
                kb.op("act", lambda e: e.activation(out=E[:, 0:n], in_=E[:, 0:n], func=AF.Exp, scale=-0.5), [E], [E])
                pg = psr()
                proj_fm_g(st, wqg, 256 + hp * 128, tok0, n, pg)
                kb.op("act", lambda e, pg=pg: e.activation(out=QS[:, 0:n], in_=pg[:, 0:n], func=AF.Silu), [pg], [QS])
                kb.op("dve", lambda e, hp=hp: e.tensor_tensor(out=E[:, 0:n], in0=E[:, 0:n], in1=OT[:, hp, tok0:tok0 + n],
                                                              op=ALU.mult), [E, OT], [E])
                kb.op("dve", lambda e: e.scalar_tensor_tensor(out=RES[:, 0:n], in0=E[:, 0:n], scalar=ngc_h[:, l:l + 1],
                                                              in1=QS[:, 0:n], op0=ALU.mult, op1=ALU.mult),
                      [E, ngc_h, QS], [RES])
                store_fm_to_mix(st, dict(stage=stage), RES, tok0, n, 512 + hp * 128)


def gd_mixer(st, l):
    kb, nc = st["kb"], st["nc"]
    hT, wbuf, w_in_d, ident = st["hT"], st["wbuf"], st["w_in_d"], st["ident"]
    PB = kb.psum_banks
    C = 64
    NB = 256
    blocks = [(0, 256, 0, 0)] + [(256 + NB * j, NB, 1 if j > 0 else 0, 1 if j < 7 else 0) for j in range(8)]
    mL, mU, mLI, mUI, ones64 = st["gm_L"], st["gm_U"], st["gm_LI"], st["gm_UI"], st["gm_ones"]
    I64 = ident[0:64, 0:64]
    with kb.phase() as ph:
        wqk = wbuf[0]
        wvab = wbuf[1]
        wg = ph.sb("gwg", [128, KC, 256], BF16)
        obuf = ph.sb("gobuf", [64, 4, NB])
        obuf2 = ph.sb("gobuf2", [64, 4, NB])
        ogd_d = st["ogd_d"]
        cwc = ph.sb("gcw", [64, DEPTH, 12, 3])
        ngc_g = ph.sb("gng", [64, DEPTH])
        dtb = ph.sb("gdtb", [64, 8])
        nexpA = ph.sb("gnexpA", [64, 8])
        ones1 = ph.sb("gones1", [64, 1])
        ur = ph.sb("gur", [64, NB + 4])
        cg = ph.sb("gcg", [64, NB])
        sqb = ph.sb("gsq", [64, NB])
        rsd = ph.sb("grsd", [64, NB])
        QT = [ph.sb("gQT%d" % h, [64, NB]) for h in range(4)]
        KT = [ph.sb("gKT%d" % h, [64, NB]) for h in range(4)]
        VT = [ph.sb("gVT%d" % h, [64, NB]) for h in range(4)]
        RES = [ph.sb("gRES%d" % h, [64, NB], BF16) for h in range(4)]
        stage = ph.sb("gstage", [128, 256], BF16)
        gab = ph.sb("gab", [64, 16])
        gx = ph.sb("ggx", [64, 8])
        la_all = ph.sb("gla", [64, 8])
        be_all = ph.sb("gbe", [64, 8])
        names = ["Lm", "LM2", "DT", "DTs", "N", "NT", "P0", "P1", "PT0", "PT1", "R0", "R1", "Z", "vn", "SC", "QgT",
                 "Khat", "Ktok", "Vtok", "EG"]
        tl = [{nm: ph.sb("g%s%d" % (nm, h), [64, 64]) for nm in names} for h in range(4)]
        gsb = [ph.sb("ggsb%d" % h, [64, 8]) for h in range(4)]
        S = [ph.sb("gS%d" % h, [64, 64]) for h in range(4)]
        kb.dma("sp", cwc, cwc[:], st["gdcw_d"], st["gdcw_d"][:])
        kb.dma("sp", ngc_g, ngc_g[:], st["gdng_d"], st["gdng_d"][:])
        kb.dma("sp", dtb, dtb[:], st["gddt_d"], st["gddt_d"].ap[l:l + 1, :].partition_broadcast(64))
        kb.dma("sp", nexpA, nexpA[:], st["gdal_d"], st["gdal_d"].ap[l:l + 1, :].partition_broadcast(64))
        kb.op("act", lambda e: e.activation(out=nexpA[:], in_=nexpA[:], func=AF.Exp), [nexpA], [nexpA])
        kb.op("dve", lambda e: e.tensor_scalar(out=nexpA[:], in0=nexpA[:], scalar1=-1.0, scalar2=None, op0=ALU.mult),
              [nexpA], [nexpA])
        kb.op("dve", lambda e: e.memset(ones1[:], 1.0), [], [ones1])
        load_w(st, wqk, w_in_d, l, O_GDQKV, 512, 0)
        load_w(st, wvab, w_in_d, l, O_GDQKV + 512, 256, 0)
        load_w(st, wvab, w_in_d, l, O_GDA, 16, 256)
        load_w(st, wg, w_in_d, l, O_GDG, 256, 0)
        rot = [0]

        def psr():
            t = PB[2 + (rot[0] % 6)]
            rot[0] += 1
            return t

        def precompute_block(tok0, n, hl, hr):
            w = n + hl + hr
            c0 = tok0 - hl
            if not hl:
                kb.op("dve", lambda e: e.memset(ur[:, 0:1], 0.0), [], [ur])
            if not hr:
                kb.op("dve", lambda e: e.memset(ur[:, n + 1:n + 2], 0.0), [], [ur])
            for ty in range(3):
                for h in range(4):
                    wt = wqk if ty < 2 else wvab
                    wc0 = ty * 256 + h * 64 if ty < 2 else h * 64
                    pu = psr()
                    proj_fm_g(st, wt, wc0, c0, w, pu, ncols=64)
                    o0_ = 1 - hl
                    kb.op("act", lambda e, pu=pu, o0_=o0_: e.activation(out=ur[:, o0_:o0_ + w], in_=pu[0:64, 0:w],
                                                                        func=AF.Identity), [pu], [ur])
                    ch = ty * 4 + h
                    kb.op("dve", lambda e, ch=ch: e.tensor_scalar(out=cg[:, 0:n], in0=ur[:, 0:n],
                                                                  scalar1=cwc[:, l, ch, 0:1], scalar2=None, op0=ALU.mult),
                          [ur, cwc], [cg])
                    for k_ in (1, 2):
                        kb.op("dve", lambda e, ch=ch, k_=k_: e.scalar_tensor_tensor(
                            out=cg[:, 0:n], in0=ur[:, k_:k_ + n], scalar=cwc[:, l, ch, k_:k_ + 1], in1=cg[:, 0:n],
                            op0=ALU.mult, op1=ALU.add), [ur, cwc, cg], [cg])
                    dstt = (QT, KT, VT)[ty][h]
                    if ty == 2:
                        kb.op("act", lambda e, dstt=dstt: e.activation(out=dstt[:, 0:n], in_=cg[:, 0:n], func=AF.Silu),
                              [cg], [dstt])
                        continue
                    kb.op("act", lambda e: e.activation(out=cg[:, 0:n], in_=cg[:, 0:n], func=AF.Silu), [cg], [cg])
                    kb.op("act", lambda e: e.activation(out=sqb[:, 0:n], in_=cg[:, 0:n], func=AF.Square), [cg], [sqb])
                    pn = psr()
                    kb.op("pe", lambda e, pn=pn: e.matmul(pn[0:64, 0:n], lhsT=ones64[:, :], rhs=sqb[:, 0:n],
                                                          start=True, stop=True), [ones64, sqb], [pn])
                    kb.op("act", lambda e, pn=pn: e.activation(out=rsd[:, 0:n], in_=pn[0:64, 0:n], func=AF.Ln,
                                                               bias=st["epst"][0:64, :]), [pn, st["epst"]], [rsd])
                    kb.op("act", lambda e: e.activation(out=rsd[:, 0:n], in_=rsd[:, 0:n], func=AF.Exp, scale=-0.5),
                          [rsd], [rsd])
                    sc_ = 0.125 if ty == 0 else 1.0
                    kb.op("dve", lambda e, dstt=dstt, sc_=sc_: e.scalar_tensor_tensor(
                        out=dstt[:, 0:n], in0=cg[:, 0:n], scalar=sc_, in1=rsd[:, 0:n], op0=ALU.mult, op1=ALU.mult),
                        [cg, rsd], [dstt])

        def finish_block(tok0, n):
            for h in range(4):
                kb.op("act", lambda e, h=h: e.activation(out=sqb[:, 0:n], in_=obuf2[:, h, 0:n], func=AF.Square),
                      [obuf2], [sqb])
                pn = psr()
                kb.op("pe", lambda e, pn=pn: e.matmul(pn[0:64, 0:n], lhsT=ones64[:, :], rhs=sqb[:, 0:n], start=True,
                                                      stop=True), [ones64, sqb], [pn])
                kb.op("act", lambda e, pn=pn: e.activation(out=rsd[:, 0:n], in_=pn[0:64, 0:n], func=AF.Ln, scale=1.0 / 64,
                                                           bias=st["epst"][0:64, :]), [pn, st["epst"]], [rsd])
                kb.op("act", lambda e: e.activation(out=rsd[:, 0:n], in_=rsd[:, 0:n], func=AF.Exp, scale=-0.5),
                      [rsd], [rsd])
                pg = psr()
                proj_fm_g(st, wg, h * 64, tok0, n, pg, ncols=64)
                kb.op("act", lambda e, pg=pg: e.activation(out=cg[:, 0:n], in_=pg[0:64, 0:n], func=AF.Silu), [pg], [cg])
                kb.op("dve", lambda e, h=h: e.tensor_tensor(out=rsd[:, 0:n], in0=rsd[:, 0:n], in1=obuf2[:, h, 0:n],
                                                            op=ALU.mult), [rsd, obuf2], [rsd])
                kb.op("dve", lambda e, h=h: e.scalar_tensor_tensor(out=RES[h][:, 0:n], in0=rsd[:, 0:n],
                                                                   scalar=ngc_g[:, l:l + 1], in1=cg[:, 0:n],
                                                                   op0=ALU.mult, op1=ALU.mult), [rsd, ngc_g, cg], [RES[h]])
            for ti in range(n // 128):
                pst = psr()
                pb = pst.ap.bitcast(BF16)
                for h in range(4):
                    kb.op("pe", lambda e, h=h, ti=ti, pb=pb: e.transpose(
                        out=pb[:, h * 64:(h + 1) * 64], in_=RES[h][:, ti * 128:(ti + 1) * 128],
                        identity=st["identb"][0:64, 0:64]), [RES[h], st["identb"]], [pst])
                kb.op("act", lambda e, pb=pb: e.activation(out=stage[:, :], in_=pb[:, 0:256], func=AF.Identity),
                      [pst], [stage])
                kb.dma("sp", st["mix_d"], st["mix_d"].ap[tok0 + ti * 128:tok0 + (ti + 1) * 128, 768:1024],
                       stage, stage[:, :], semt=stage)

        for dr in range(2):
            M1 = mL if dr == 0 else mU
            M2 = mUI if dr == 0 else mLI
            M3 = mU if dr == 0 else mL
            for h in range(4):
                kb.op("dve", lambda e, h=h: e.memset(S[h][:], 0.0), [], [S[h]])
            border = blocks if dr == 0 else [blocks[0]] + blocks[:0:-1]
            for (tok0, n, hl, hr) in border:
                precompute_block(tok0, n, hl, hr)
                nch = n // C
                corder = range(nch) if dr == 0 else range(nch - 1, -1, -1)
                for ck in corder:
                    c0 = ck * C
                    pab = psr()
                    for kc in range(KC):
                        kb.op("pe", lambda e, kc=kc, pab=pab: e.matmul(
                            pab[0:64, 0:16], lhsT=hT[:, kc, tok0 + c0:tok0 + c0 + C], rhs=wvab[:, kc, 256:272],
                            start=(kc == 0), stop=(kc == KC - 1)), [hT, wvab], [pab])
                    kb.op("act", lambda e, pab=pab: e.activation(out=gab[:, :], in_=pab[0:64, 0:16], func=AF.Identity),
                          [pab], [gab])
                    kb.op("dve", lambda e: e.tensor_tensor(out=gx[:, :], in0=gab[:, 0:8], in1=dtb[:, :], op=ALU.add),
                          [gab, dtb], [gx])
                    kb.op("act", lambda e: e.activation(out=gx[:, :], in_=gx[:, :], func=AF.Exp), [gx], [gx])
                    kb.op("act", lambda e: e.activation(out=gx[:, :], in_=gx[:, :], func=AF.Ln, bias=ones1[:, :]),
                          [gx, ones1], [gx])
                    kb.op("dve", lambda e: e.tensor_tensor(out=la_all[:, :], in0=gx[:, :], in1=nexpA[:, :], op=ALU.mult),
                          [gx, nexpA], [la_all])
                    kb.op("act", lambda e: e.activation(out=be_all[:, :], in_=gab[:, 8:16], func=AF.Exp, scale=-1.0),
                          [gab], [be_all])
                    kb.op("dve", lambda e: e.tensor_scalar(out=be_all[:, :], in0=be_all[:, :], scalar1=1.0, scalar2=None,
                                                           op0=ALU.add), [be_all], [be_all])
                    kb.op("dve", lambda e: e.reciprocal(out=be_all[:, :], in_=be_all[:, :]), [be_all], [be_all])
                    for h in range(4):
                        t = tl[h]
                        g = gsb[h]
                        cidx = dr * 4 + h
                        la = la_all[:, cidx:cidx + 1]
                        be = be_all[:, cidx:cidx + 1]
                        KTc = KT[h][:, c0:c0 + C]
                        QTc = QT[h][:, c0:c0 + C]
                        VTc = VT[h][:, c0:c0 + C]
                        for src, dn in ((KTc, "Ktok"), (VTc, "Vtok")):
                            pt_ = psr()
                            kb.op("pe", lambda e, pt_=pt_, src=src: e.transpose(out=pt_[0:64, 0:64], in_=src, identity=I64),
                                  [KT[h], VT[h], ident], [pt_])
                            kb.op("act", lambda e, pt_=pt_, dn=dn, t=t: e.activation(out=t[dn][:, :], in_=pt_[0:64, 0:64],
                                                                                 func=AF.Identity), [pt_], [t[dn]])
                        kb.op("dve", lambda e, t=t, la=la: e.tensor_scalar(out=t["Lm"][:, :], in0=M1[:, :], scalar1=la,
                                                                           scalar2=None, op0=ALU.mult),
                              [M1, la_all], [t["Lm"]])
                        kb.op("dve", lambda e, t=t, la=la: e.tensor_scalar(out=t["LM2"][:, :], in0=M2[:, :], scalar1=la,
                                                                           scalar2=None, op0=ALU.mult),
                              [M2, la_all], [t["LM2"]])
                        pd = psr()
                        kb.op("pe", lambda e, pd=pd, t=t: e.matmul(pd[0:64, 0:64], lhsT=t["Lm"][:, :], rhs=M2[:, :],
                                                                   start=True, stop=True), [t["Lm"], M2], [pd])
                        kb.op("pe", lambda e, pd=pd, t=t: e.matmul(pd[0:64, 64:128], lhsT=ones64[:, :], rhs=t["LM2"][:, :],
                                                                   start=True, stop=True), [t["LM2"], ones64], [pd])
                        kb.op("pe", lambda e, pd=pd, la=la: e.matmul(pd[0:64, 128:129], lhsT=M2[:, :], rhs=la,
                                                                     start=True, stop=True), [M2, la_all], [pd])
                        kb.op("pe", lambda e, pd=pd, la=la: e.matmul(pd[0:64, 129:130], lhsT=ones64[:, :], rhs=la,
                                                                     start=True, stop=True), [ones64, la_all], [pd])
                        kb.op("act", lambda e, pd=pd, t=t: e.activation(out=t["DT"][:, :], in_=pd[0:64, 0:64], func=AF.Exp),
                              [pd], [t["DT"]])
                        kb.op("act", lambda e, pd=pd, t=t: e.activation(out=t["EG"][:, :], in_=pd[0:64, 64:128],
                                                                        func=AF.Exp), [pd], [t["EG"]])
                        kb.op("act", lambda e, pd=pd, g=g: e.activation(out=g[:, 0:2], in_=pd[0:64, 128:130],
                                                                        func=AF.Identity), [pd], [g])
                        kb.op("dve", lambda e, t=t: e.tensor_tensor(out=t["DTs"][:, :], in0=t["DT"][:, :], in1=M3[:, :],
                                                                    op=ALU.mult), [t["DT"], M3], [t["DTs"]])
                        kb.op("dve", lambda e, t=t: e.tensor_tensor(out=t["DT"][:, :], in0=t["DT"][:, :], in1=M2[:, :],
                                                                    op=ALU.mult), [t["DT"], M2], [t["DT"]])
                        kb.op("act", lambda e, g=g: e.activation(out=g[:, 2:3], in_=g[:, 0:1], func=AF.Exp), [g], [g])
                        kb.op("dve", lambda e, g=g: e.tensor_scalar(out=g[:, 3:4], in0=g[:, 2:3], scalar1=-1.0, scalar2=None,
                                                                    op0=ALU.mult), [g], [g])
                        kb.op("dve", lambda e, g=g: e.tensor_tensor(out=g[:, 6:7], in0=g[:, 1:2], in1=g[:, 0:1],
                                                                    op=ALU.subtract), [g], [g])
                        kb.op("act", lambda e, g=g: e.activation(out=g[:, 4:5], in_=g[:, 6:7], func=AF.Exp), [g], [g])
                        kb.op("act", lambda e, g=g: e.activation(out=g[:, 5:6], in_=g[:, 1:2], func=AF.Exp), [g], [g])
                        pkk = psr()
                        kb.op("pe", lambda e, pkk=pkk, KTc=KTc: e.matmul(pkk[0:64, 0:64], lhsT=KTc, rhs=KTc, start=True,
                                                                         stop=True), [KT[h]], [pkk])
                        kb.op("pe", lambda e, pkk=pkk, KTc=KTc, QTc=QTc: e.matmul(pkk[0:64, 64:128], lhsT=KTc, rhs=QTc,
                                                                                   start=True, stop=True),
                              [KT[h], QT[h]], [pkk])
                        kb.op("dve", lambda e, pkk=pkk, t=t, be=be: e.scalar_tensor_tensor(
                            out=t["N"][:, :], in0=pkk[0:64, 0:64], scalar=be, in1=t["DTs"][:, :], op0=ALU.mult,
                            op1=ALU.mult), [pkk, be_all, t["DTs"]], [t["N"]])
                        kb.op("dve", lambda e, pkk=pkk, t=t: e.tensor_tensor(out=t["SC"][:, :], in0=pkk[0:64, 64:128],
                                                                            in1=t["DT"][:, :], op=ALU.mult),
                              [pkk, t["DT"]], [t["SC"]])
                        kb.op("dve", lambda e, t=t, QTc=QTc: e.tensor_tensor(out=t["QgT"][:, :], in0=QTc, in1=t["EG"][:, :],
                                                                            op=ALU.mult), [QT[h], t["EG"]], [t["QgT"]])
                        kb.op("dve", lambda e, t=t, g=g: e.tensor_scalar(out=t["Khat"][:, :], in0=t["Ktok"][:, :],
                                                                         scalar1=g[:, 4:5], scalar2=None, op0=ALU.mult),
                              [t["Ktok"], g], [t["Khat"]])
                        pt_ = psr()
                        kb.op("pe", lambda e, pt_=pt_, t=t: e.transpose(out=pt_[0:64, 0:64], in_=t["N"][:, :], identity=I64),
                              [t["N"], ident], [pt_])
                        kb.op("act", lambda e, pt_=pt_, t=t: e.activation(out=t["NT"][:, :], in_=pt_[0:64, 0:64],
                                                                          func=AF.Identity), [pt_], [t["NT"]])
                        kb.op("dve", lambda e, t=t: e.tensor_tensor(out=t["R0"][:, :], in0=I64, in1=t["N"][:, :],
                                                                    op=ALU.subtract), [ident, t["N"]], [t["R0"]])
                    for lev in range(5):
                        for h in range(4):
                            t = tl[h]
                            Pc = t["N"] if lev == 0 else t["P%d" % (lev % 2)]
                            PTc = t["NT"] if lev == 0 else t["PT%d" % (lev % 2)]
                            Pn = t["P%d" % ((lev + 1) % 2)]
                            PTn = t["PT%d" % ((lev + 1) % 2)]
                            Rc = t["R%d" % (lev % 2)]
                            Rn = t["R%d" % ((lev + 1) % 2)]
                            pp = psr()
                            kb.op("pe", lambda e, pp=pp, Pc=Pc, PTc=PTc: e.matmul(pp[0:64, 0:64], lhsT=Pc[:, :], rhs=PTc[:, :],
                                                                                  start=True, stop=True), [Pc, PTc], [pp])
                            if lev < 4:
                                kb.op("pe", lambda e, pp=pp, Pc=Pc, PTc=PTc: e.matmul(pp[0:64, 64:128], lhsT=PTc[:, :],
                                                                                      rhs=Pc[:, :], start=True, stop=True),
                                      [Pc, PTc], [pp])
                            kb.op("act", lambda e, pp=pp, PTn=PTn: e.activation(out=PTn[:, :], in_=pp[0:64, 0:64],
                                                                                func=AF.Identity), [pp], [PTn])
                            if lev < 4:
                                kb.op("act", lambda e, pp=pp, Pn=Pn: e.activation(out=Pn[:, :], in_=pp[0:64, 64:128],
                                                                                  func=AF.Identity), [pp], [Pn])
                            pr = psr()
                            kb.op("pe", lambda e, pr=pr, PTn=PTn, Rc=Rc: e.matmul(pr[0:64, 0:64], lhsT=PTn[:, :], rhs=Rc[:, :],
                                                                                  start=True, stop=True), [PTn, Rc], [pr])
                            kb.op("dve", lambda e, pr=pr, Rc=Rc, Rn=Rn: e.tensor_tensor(out=Rn[:, :], in0=pr[0:64, 0:64],
                                                                                        in1=Rc[:, :], op=ALU.add),
                                  [pr, Rc], [Rn])
                    for h in range(4):
                        t = tl[h]
                        pks = psr()
                        kb.op("pe", lambda e, pks=pks, h=h: e.matmul(pks[0:64, 0:64], lhsT=KT[h][:, c0:c0 + C], rhs=S[h][:, :],
                                                                     start=True, stop=True), [KT[h], S[h]], [pks])
                        tl[h]["_pks"] = pks
                    for h in range(4):
                        t, g = tl[h], gsb[h]
                        pks = t["_pks"]
                        kb.op("dve", lambda e, pks=pks, t=t, g=g: e.scalar_tensor_tensor(
                            out=t["Z"][:, :], in0=pks[0:64, 0:64], scalar=g[:, 3:4], in1=t["Vtok"][:, :], op0=ALU.mult,
                            op1=ALU.add), [pks, g, t["Vtok"]], [t["Z"]])
                    for h in range(4):
                        t = tl[h]
                        pvn = psr()
                        kb.op("pe", lambda e, pvn=pvn, t=t: e.matmul(pvn[0:64, 0:64], lhsT=t["R1"][:, :], rhs=t["Z"][:, :],
                                                                     start=True, stop=True), [t["R1"], t["Z"]], [pvn])
                        t["_pvn"] = pvn
                    for h in range(4):
                        t = tl[h]
                        pvn = t["_pvn"]
                        cidx = dr * 4 + h
                        kb.op("act", lambda e, pvn=pvn, t=t, cidx=cidx: e.activation(
                            out=t["vn"][:, :], in_=pvn[0:64, 0:64], func=AF.Identity, scale=be_all[:, cidx:cidx + 1]),
                            [pvn, be_all], [t["vn"]])
                    for h in range(4):
                        t, g = tl[h], gsb[h]
                        po = PB[h // 2]
                        oc = (h % 2) * 256 + c0
                        kb.op("pe", lambda e, po=po, t=t, oc=oc: e.matmul(po[0:64, oc:oc + C], lhsT=t["vn"][:, :],
                                                                          rhs=t["SC"][:, :], start=True, stop=False),
                              [t["vn"], t["SC"]], [po])
                        kb.op("pe", lambda e, po=po, t=t, oc=oc, h=h: e.matmul(po[0:64, oc:oc + C], lhsT=S[h][:, :],
                                                                               rhs=t["QgT"][:, :], start=False, stop=True),
                              [S[h], t["QgT"]], [po])
                        psn = psr()
                        kb.op("pe", lambda e, psn=psn, t=t: e.matmul(psn[0:64, 0:64], lhsT=t["Khat"][:, :], rhs=t["vn"][:, :],
                                                                     start=True, stop=True), [t["Khat"], t["vn"]], [psn])
                        kb.op("dve", lambda e, psn=psn, h=h, g=g: e.scalar_tensor_tensor(
                            out=S[h][:, :], in0=S[h][:, :], scalar=g[:, 5:6], in1=psn[0:64, 0:64], op0=ALU.mult,
                            op1=ALU.add), [S[h], g, psn], [S[h]])
                if dr == 0:
                    for h in range(4):
                        po = PB[h // 2]
                        oc = (h % 2) * 256
                        kb.op("act", lambda e, po=po, oc=oc, h=h: e.activation(
                            out=obuf[:, h, 0:n], in_=po[0:64, oc:oc + n], func=AF.Identity), [po], [obuf])
                    for h in range(4):
                        kb.dma("sp", ogd_d, ogd_d.ap[:, h, tok0:tok0 + n], obuf, obuf[:, h, 0:n], semt=obuf)
                else:
                    for h in range(4):
                        kb.dma("sp", obuf2, obuf2[:, h, 0:n], ogd_d, ogd_d.ap[:, h, tok0:tok0 + n])
                    for h in range(4):
                        po = PB[h // 2]
                        oc = (h % 2) * 256
                        kb.op("dve", lambda e, po=po, oc=oc, h=h: e.tensor_tensor(
                            out=obuf2[:, h, 0:n], in0=po[0:64, oc:oc + n], in1=obuf2[:, h, 0:n], op=ALU.add),
                            [po, obuf2], [obuf2])
                    finish_block(tok0, n)


def gd_precompute_all(st, l):
    kb = st["kb"]
    hT, wbuf, w_in_d = st["hT"], st["wbuf"], st["w_in_d"]
    PB = kb.psum_banks
    gq_d = st["gqkv_d"]
    bdb = st["bdb"]
    blocks = [(0, 256, 0, 0)] + [(256 + 512 * j, 512, 1 if j > 0 else 0, 1 if j < 3 else 0) for j in range(4)]
    with kb.phase() as ph:
        wqk = wbuf[0]
        wv = wbuf[1]
        cw = ph.sb("pcw", [128, DEPTH, 6, 3])
        ur = [ph.sb("pur%d" % i, [128, 516]) for i in range(2)]
        cg = [ph.sb("pcg%d" % i, [128, 512]) for i in range(2)]
        ee = [ph.sb("pee%d" % i, [128, 512]) for i in range(2)]
        sq = [ph.sb("psq%d" % i, [128, 512], BF16) for i in range(2)]
        rs = [ph.sb("prs%d" % i, [128, 512]) for i in range(2)]
        ot = [ph.sb("pot%d" % i, [128, 512], BF16) for i in range(2)]
        kb.dma("sp", cw, cw[:], st["gdcw128_d"], st["gdcw128_d"][:])
        load_w(st, wqk, w_in_d, l, O_GDQKV, 512, 0)
        load_w(st, wv, w_in_d, l, O_GDQKV + 512, 256, 0)
        it = 0
        for (tok0, n, hl, hr) in blocks:
            w = n + hl + hr
            c0 = tok0 - hl
            mblocks = [(0, w // 2), (w // 2, w - w // 2)] if w > 512 else [(0, w)]
            for p_ in range(2):
                if not hl:
                    kb.op("dve", lambda e, p_=p_: e.memset(ur[p_][:, 0:1], 0.0), [], [ur[p_]])
                if not hr:
                    kb.op("dve", lambda e, p_=p_: e.memset(ur[p_][:, n + 1:n + 2], 0.0), [], [ur[p_]])
            for cc in range(6):
                ty, hp = cc // 2, cc % 2
                p_ = it % 2
                it += 1
                u_, c_, e_, s_, r_, o_ = ur[p_], cg[p_], ee[p_], sq[p_], rs[p_], ot[p_]
                wt = wqk if ty < 2 else wv
                wc0 = ty * 256 + hp * 128 if ty < 2 else hp * 128
                for bi, (b0, bw) in enumerate(mblocks):
                    pu = PB[(2 * it + bi) % 8]
                    for kc in range(KC):
                        kb.op("pe", lambda e, kc=kc, pu=pu, b0=b0, bw=bw: e.matmul(
                            pu[:, 0:bw], lhsT=wt[:, kc, wc0:wc0 + 128], rhs=hT[:, kc, c0 + b0:c0 + b0 + bw],
                            start=(kc == 0), stop=(kc == KC - 1)), [wt, hT], [pu])
                    o0_ = 1 - hl + b0
                    kb.op("act", lambda e, pu=pu, bw=bw, o0_=o0_: e.activation(out=u_[:, o0_:o0_ + bw], in_=pu[:, 0:bw],
                                                                              func=AF.Identity), [pu], [u_])
                kb.op("dve", lambda e: e.tensor_scalar(out=c_[:, 0:n], in0=u_[:, 0:n], scalar1=cw[:, l, cc, 0:1],
                                                       scalar2=None, op0=ALU.mult), [u_, cw], [c_])
                for k_ in (1, 2):
                    kb.op("dve", lambda e, k_=k_: e.scalar_tensor_tensor(
                        out=c_[:, 0:n], in0=u_[:, k_:k_ + n], scalar=cw[:, l, cc, k_:k_ + 1], in1=c_[:, 0:n],
                        op0=ALU.mult, op1=ALU.add), [u_, cw, c_], [c_])
                kb.op("act", lambda e: e.activation(out=e_[:, 0:n], in_=c_[:, 0:n], func=AF.Exp, scale=-1.0), [c_], [e_])
                kb.op("dve", lambda e: e.tensor_scalar(out=e_[:, 0:n], in0=e_[:, 0:n], scalar1=1.0, scalar2=None,
                                                       op0=ALU.add), [e_], [e_])
                kb.op("dve", lambda e: e.reciprocal(out=e_[:, 0:n], in_=e_[:, 0:n]), [e_], [e_])
                if ty == 2:
                    kb.op("dve", lambda e: e.tensor_tensor(out=o_[:, 0:n], in0=c_[:, 0:n], in1=e_[:, 0:n], op=ALU.mult),
                          [c_, e_], [o_])
                else:
                    kb.op("dve", lambda e: e.tensor_tensor(out=c_[:, 0:n], in0=c_[:, 0:n], in1=e_[:, 0:n], op=ALU.mult),
                          [c_, e_], [c_])
                    kb.op("act", lambda e: e.activation(out=s_[:, 0:n], in_=c_[:, 0:n], func=AF.Square), [c_], [s_])
                    pn = PB[(2 * it + 5) % 8]
                    kb.op("pe", lambda e, pn=pn: e.matmul(pn[:, 0:n], lhsT=bdb[:, :], rhs=s_[:, 0:n], start=True, stop=True),
                          [bdb, s_], [pn])
                    kb.op("act", lambda e, pn=pn: e.activation(out=r_[:, 0:n], in_=pn[:, 0:n], func=AF.Ln,
                                                               bias=st["epst"][:, :]), [pn, st["epst"]], [r_])
                    kb.op("act", lambda e: e.activation(out=r_[:, 0:n], in_=r_[:, 0:n], func=AF.Exp, scale=-0.5), [r_], [r_])
                    sc_ = 0.125 if ty == 0 else 1.0
                    kb.op("dve", lambda e, sc_=sc_: e.scalar_tensor_tensor(
                        out=o_[:, 0:n], in0=c_[:, 0:n], scalar=sc_, in1=r_[:, 0:n], op0=ALU.mult, op1=ALU.mult),
                        [c_, r_], [o_])
                i0 = ty * 4 + 2 * hp
                kb.dma("sp", gq_d, gq_d.ap[i0:i0 + 2].rearrange("t d n -> (t d) n")[:, tok0:tok0 + n], o_, o_[:, 0:n],
                       semt=o_)


def gd_mixer2(st, l):
    kb, nc = st["kb"], st["nc"]
    hT, wbuf, w_in_d, identb = st["hT"], st["wbuf"], st["w_in_d"], st["identb"]
    PB = kb.psum_banks
    C = 64
    NB = 256
    blocks = [(0, 256, 0, 0)] + [(256 + NB * j, NB, 1 if j > 0 else 0, 1 if j < 7 else 0) for j in range(8)]
    order = [list(range(9)), [0] + list(range(8, 0, -1))]
    ones64 = st["gm_ones"]
    M2f = [st["gm_UI"], st["gm_LI"]]
    Ib64 = identb[0:64, 0:64]
    I64f = st["ident"][0:64, 0:64]

    def v3(t_):
        return t_[:, :].rearrange("p (c k) -> p c k", k=64)

    def bc8(ap2):
        return ap2.rearrange("p (c o) -> p c o", o=1).to_broadcast([64, 8, 64])

    gd_precompute_all(st, l)
    with kb.phase() as ph:
        wvab = wbuf[1]
        wg = ph.sb("gwg", [128, KC, 256], BF16)
        cwc = ph.sb("gcw", [64, DEPTH, 12, 3])
        ngc_g = ph.sb("gng", [64, DEPTH])
        dtb = ph.sb("gdtb", [64, 8])
        nexpA = ph.sb("gnexpA", [64, 8])
        ones1 = ph.sb("gones1", [64, 1])
        ob = [ph.sb("gob%d" % d_, [64, 4, NB]) for d_ in range(2)]
        obuf2 = T(st["junk"].ap[0:64, :].rearrange("p (h n) -> p h n", h=4), "obuf2")
        obuf2_owner = st["junk"]
        ur = ph.sb("gur", [64, NB + 4])
        cg = ph.sb("gcg", [64, NB])
        sqb = ph.sb("gsq", [64, NB])
        rsd = ph.sb("grsd", [64, NB])
        QT = [[ph.sb("gQT%d%d" % (d_, h), [64, NB], BF16) for h in range(4)] for d_ in range(2)]
        KT = [[ph.sb("gKT%d%d" % (d_, h), [64, NB], BF16) for h in range(4)] for d_ in range(2)]
        VT = [[ph.sb("gVT%d%d" % (d_, h), [64, NB], BF16) for h in range(4)] for d_ in range(2)]
        RES = [ph.sb("gRES%d" % h, [64, NB], BF16) for h in range(4)]
        stage = ph.sb("gstage", [128, 256], BF16)
        mk = {nm: ph.sb("g" + nm, [64, 512], BF16) for nm in ("M1c", "M2c", "M3c", "Ic")}
        f32n = ["Lm", "DT", "EG", "S", "tmpA"]
        f32rn = ["R0", "R1", "N", "NT", "P0", "P1", "PT0", "PT1"]
        b16n = ["Rb", "Z", "vn", "SC", "Khat", "Ktok", "Vtok", "Sb"]
        t = {nm: ph.sb("g2" + nm, [64, 512]) for nm in f32n}
        t.update({nm: ph.sb("g2" + nm, [64, 512], mybir.dt.float32r) for nm in f32rn})
        t.update({nm: ph.sb("g2" + nm, [64, 512], BF16) for nm in b16n})
        t["LM2"] = t["tmpA"]
        aB = [ph.sb("gaB%d" % d_, [64, 4, 4]) for d_ in range(2)]
        bB = [ph.sb("gbB%d" % d_, [64, 4, 4]) for d_ in range(2)]
        laB = [ph.sb("glaB%d" % d_, [64, 4, 4]) for d_ in range(2)]
        beB = [ph.sb("gbeB%d" % d_, [64, 4, 4]) for d_ in range(2)]
        la_all = ph.sb("gla", [64, 8])
        be_all = ph.sb("gbe", [64, 8])
        g = ph.sb("ggsb", [64, 48])
        for nm in ("M1c", "M2c", "M3c", "Ic"):
            kb.dma("pool", mk[nm], mk[nm][:], st["gc_" + nm], st["gc_" + nm][:])
        kb.dma("sp", cwc, cwc[:], st["gdcw_d"], st["gdcw_d"][:])
        kb.dma("sp", ngc_g, ngc_g[:], st["gdng_d"], st["gdng_d"][:])
        kb.dma("sp", dtb, dtb[:], st["gddt_d"], st["gddt_d"].ap[l:l + 1, :].partition_broadcast(64))
        kb.dma("sp", nexpA, nexpA[:], st["gdal_d"], st["gdal_d"].ap[l:l + 1, :].partition_broadcast(64))
        kb.op("act", lambda e: e.activation(out=nexpA[:], in_=nexpA[:], func=AF.Exp), [nexpA], [nexpA])
        kb.op("dve", lambda e: e.tensor_scalar(out=nexpA[:], in0=nexpA[:], scalar1=-1.0, scalar2=None, op0=ALU.mult),
              [nexpA], [nexpA])
        kb.op("dve", lambda e: e.memset(ones1[:], 1.0), [], [ones1])
        kb.op("dve", lambda e: e.memset(t["S"][:], 0.0), [], [t["S"]])
        kb.op("dve", lambda e: e.memset(t["Sb"][:], 0.0), [], [t["Sb"]])
        load_w(st, wvab, w_in_d, l, O_GDA, 16, 256)
        load_w(st, wg, w_in_d, l, O_GDG, 256, 0)
        rot = [0]

        def psr():
            t_ = PB[rot[0] % 8]
            rot[0] += 1
            return t_

        def precompute_block(dr, tok0, n, hl, hr):
            nchb = n // C
            for ck in range(nchb):
                pab = psr()
                for kc in range(KC):
                    kb.op("pe", lambda e, kc=kc, pab=pab, ck=ck: e.matmul(
                        pab[0:64, 0:16], lhsT=hT[:, kc, tok0 + ck * C:tok0 + (ck + 1) * C], rhs=wvab[:, kc, 256:272],
                        start=(kc == 0), stop=(kc == KC - 1)), [hT, wvab], [pab])
                kb.op("act", lambda e, pab=pab, ck=ck: e.activation(out=aB[dr][:, ck, :], in_=pab[0:64, dr * 4:dr * 4 + 4],
                                                                    func=AF.Identity), [pab], [aB[dr]])
                kb.op("act", lambda e, pab=pab, ck=ck: e.activation(out=bB[dr][:, ck, :], in_=pab[0:64, 8 + dr * 4:12 + dr * 4],
                                                                    func=AF.Identity), [pab], [bB[dr]])
                kb.op("dve", lambda e, ck=ck: e.tensor_tensor(out=aB[dr][:, ck, :], in0=aB[dr][:, ck, :],
                                                              in1=dtb[:, dr * 4:dr * 4 + 4], op=ALU.add), [aB[dr], dtb], [aB[dr]])
            kb.op("act", lambda e: e.activation(out=aB[dr][:, :, :], in_=aB[dr][:, :, :], func=AF.Exp), [aB[dr]], [aB[dr]])
            kb.op("act", lambda e: e.activation(out=aB[dr][:, :, :], in_=aB[dr][:, :, :], func=AF.Ln, bias=ones1[:, :]),
                  [aB[dr], ones1], [aB[dr]])
            for ck in range(nchb):
                kb.op("dve", lambda e, ck=ck: e.tensor_tensor(out=laB[dr][:, ck, :], in0=aB[dr][:, ck, :],
                                                              in1=nexpA[:, dr * 4:dr * 4 + 4], op=ALU.mult),
                      [aB[dr], nexpA], [laB[dr]])
            kb.op("act", lambda e: e.activation(out=beB[dr][:, :, :], in_=bB[dr][:, :, :], func=AF.Exp, scale=-1.0),
                  [bB[dr]], [beB[dr]])
            kb.op("dve", lambda e: e.tensor_scalar(out=beB[dr][:, :, :], in0=beB[dr][:, :, :], scalar1=1.0, scalar2=None,
                                                   op0=ALU.add), [beB[dr]], [beB[dr]])
            kb.op("dve", lambda e: e.reciprocal(out=beB[dr][:, :, :], in_=beB[dr][:, :, :]), [beB[dr]], [beB[dr]])
            gq_d = st["gqkv_d"]
            for ty in range(3):
                for h in range(4):
                    dstt = (QT, KT, VT)[ty][dr][h]
                    kb.dma("sp", dstt, dstt[:, 0:n], gq_d, gq_d.ap[ty * 4 + h, :, tok0:tok0 + n])

        def finish_block(tok0, n):
            for h in range(4):
                kb.op("act", lambda e, h=h: e.activation(out=sqb[:, 0:n], in_=obuf2[:, h, 0:n], func=AF.Square),
                      [st["junk"]], [sqb])
                pn = psr()
                kb.op("pe", lambda e, pn=pn: e.matmul(pn[0:64, 0:n], lhsT=ones64[:, :], rhs=sqb[:, 0:n], start=True,
                                                      stop=True), [ones64, sqb], [pn])
                kb.op("act", lambda e, pn=pn: e.activation(out=rsd[:, 0:n], in_=pn[0:64, 0:n], func=AF.Ln, scale=1.0 / 64,
                                                           bias=st["epst"][0:64, :]), [pn, st["epst"]], [rsd])
                kb.op("act", lambda e: e.activation(out=rsd[:, 0:n], in_=rsd[:, 0:n], func=AF.Exp, scale=-0.5),
                      [rsd], [rsd])
                pg = psr()
                proj_fm_g(st, wg, h * 64, tok0, n, pg, ncols=64)
                kb.op("act", lambda e, pg=pg: e.activation(out=cg[:, 0:n], in_=pg[0:64, 0:n], func=AF.Silu), [pg], [cg])
                kb.op("dve", lambda e, h=h: e.tensor_tensor(out=rsd[:, 0:n], in0=rsd[:, 0:n], in1=obuf2[:, h, 0:n],
                                                            op=ALU.mult), [rsd, st["junk"]], [rsd])
                kb.op("dve", lambda e, h=h: e.scalar_tensor_tensor(out=RES[h][:, 0:n], in0=rsd[:, 0:n],
                                                                   scalar=ngc_g[:, l:l + 1], in1=cg[:, 0:n],
                                                                   op0=ALU.mult, op1=ALU.mult), [rsd, ngc_g, cg], [RES[h]])
            for ti in range(n // 128):
                pst = psr()
                pb = pst.ap.bitcast(BF16)
                for h in range(4):
                    kb.op("pe", lambda e, h=h, ti=ti, pb=pb: e.transpose(
                        out=pb[:, h * 64:(h + 1) * 64], in_=RES[h][:, ti * 128:(ti + 1) * 128],
                        identity=Ib64), [RES[h], identb], [pst])
                kb.op("act", lambda e, pb=pb: e.activation(out=stage[:, :], in_=pb[:, 0:256], func=AF.Identity),
                      [pst], [stage])
                kb.dma("sp", st["mix_d"], st["mix_d"].ap[tok0 + ti * 128:tok0 + (ti + 1) * 128, 768:1024],
                       stage, stage[:, :], semt=stage)

        ogd_d = st["ogd_d"]
        stored = set()
        nsteps = 36
        for step in range(nsteps):
            bi, cj = step // 4, step % 4
            cur = []
            for dr in range(2):
                b = order[dr][bi]
                tok0, n, hl, hr = blocks[b]
                if cj == 0:
                    precompute_block(dr, tok0, n, hl, hr)
                ck = cj if dr == 0 else 3 - cj
                cur.append((b, tok0, ck * C))
            for dr in range(2):
                ckd = cur[dr][2] // C
                kb.op("dve", lambda e, dr=dr, ckd=ckd: e.tensor_copy(out=la_all[:, dr * 4:dr * 4 + 4], in_=laB[dr][:, ckd, :]),
                      [laB[dr]], [la_all])
                kb.op("dve", lambda e, dr=dr, ckd=ckd: e.tensor_copy(out=be_all[:, dr * 4:dr * 4 + 4], in_=beB[dr][:, ckd, :]),
                      [beB[dr]], [be_all])

            def opnd(tiles, c):
                dr, h = c // 4, c % 4
                return tiles[dr][h][:, cur[dr][2]:cur[dr][2] + C], tiles[dr][h]

            for src, dn in ((KT, "Ktok"), (VT, "Vtok")):
                pt_ = psr()
                ptb = pt_.ap.bitcast(BF16)
                for c in range(8):
                    ap_, tt_ = opnd(src, c)
                    kb.op("pe", lambda e, ptb=ptb, ap_=ap_, c=c: e.transpose(out=ptb[0:64, c * 64:(c + 1) * 64], in_=ap_,
                                                                             identity=Ib64), [tt_, identb], [pt_])
                kb.op("act", lambda e, ptb=ptb, dn=dn: e.activation(out=t[dn][:, :], in_=ptb[0:64, 0:512],
                                                                    func=AF.Identity), [pt_], [t[dn]])
            kb.op("dve", lambda e: e.tensor_tensor(out=v3(t["Lm"]), in0=v3(mk["M1c"]), in1=bc8(la_all[:, 0:8]), op=ALU.mult),
                  [mk["M1c"], la_all], [t["Lm"]])
            kb.op("dve", lambda e: e.tensor_tensor(out=v3(t["LM2"]), in0=v3(mk["M2c"]), in1=bc8(la_all[:, 0:8]), op=ALU.mult),
                  [mk["M2c"], la_all], [t["LM2"]])
            pd1 = psr()
            for c in range(8):
                kb.op("pe", lambda e, c=c, pd1=pd1: e.matmul(pd1[0:64, c * 64:(c + 1) * 64], lhsT=t["Lm"][:, c * 64:(c + 1) * 64],
                                                             rhs=M2f[c // 4][:, :], start=True, stop=True),
                      [t["Lm"], M2f[c // 4]], [pd1])
            pd2 = psr()
            kb.op("pe", lambda e, pd2=pd2: e.matmul(pd2[0:64, 0:512], lhsT=ones64[:, :], rhs=t["LM2"][:, :], start=True,
                                                    stop=True), [ones64, t["LM2"]], [pd2])
            pd3 = psr()
            for dr in range(2):
                kb.op("pe", lambda e, dr=dr, pd3=pd3: e.matmul(pd3[0:64, dr * 4:dr * 4 + 4], lhsT=M2f[dr][:, :],
                                                               rhs=la_all[:, dr * 4:dr * 4 + 4], start=True, stop=True),
                      [M2f[dr], la_all], [pd3])
            kb.op("pe", lambda e, pd3=pd3: e.matmul(pd3[0:64, 8:16], lhsT=ones64[:, :], rhs=la_all[:, 0:8], start=True,
                                                    stop=True), [ones64, la_all], [pd3])
            kb.op("act", lambda e, pd1=pd1: e.activation(out=t["DT"][:, :], in_=pd1[0:64, 0:512], func=AF.Exp), [pd1], [t["DT"]])
            kb.op("act", lambda e, pd2=pd2: e.activation(out=t["EG"][:, :], in_=pd2[0:64, 0:512], func=AF.Exp), [pd2], [t["EG"]])
            kb.op("act", lambda e, pd3=pd3: e.activation(out=g[:, 0:16], in_=pd3[0:64, 0:16], func=AF.Identity), [pd3], [g])
            kb.op("dve", lambda e: e.tensor_tensor(out=t["DT"][:, :], in0=t["DT"][:, :], in1=mk["M2c"][:, :], op=ALU.mult),
                  [t["DT"], mk["M2c"]], [t["DT"]])
            kb.op("act", lambda e: e.activation(out=g[:, 16:24], in_=g[:, 0:8], func=AF.Exp), [g], [g])
            kb.op("dve", lambda e: e.tensor_scalar(out=g[:, 16:24], in0=g[:, 16:24], scalar1=-1.0, scalar2=None,
                                                   op0=ALU.mult), [g], [g])
            kb.op("dve", lambda e: e.tensor_tensor(out=g[:, 40:48], in0=g[:, 8:16], in1=g[:, 0:8], op=ALU.subtract),
                  [g], [g])
            kb.op("act", lambda e: e.activation(out=g[:, 24:32], in_=g[:, 40:48], func=AF.Exp), [g], [g])
            kb.op("act", lambda e: e.activation(out=g[:, 32:40], in_=g[:, 8:16], func=AF.Exp), [g], [g])
            pkk = psr()
            psc = psr()
            for c in range(8):
                kap, ktt = opnd(KT, c)
                qap, qtt = opnd(QT, c)
                kb.op("pe", lambda e, c=c, kap=kap, pkk=pkk: e.matmul(pkk[0:64, c * 64:(c + 1) * 64], lhsT=kap, rhs=kap,
                                                                      start=True, stop=True), [ktt], [pkk])
                kb.op("pe", lambda e, c=c, kap=kap, qap=qap, psc=psc: e.matmul(psc[0:64, c * 64:(c + 1) * 64], lhsT=kap,
                                                                               rhs=qap, start=True, stop=True),
                      [ktt, qtt], [psc])
            kb.op("dve", lambda e, pkk=pkk: e.tensor_tensor(out=t["tmpA"][:, :], in0=pkk[0:64, 0:512], in1=t["DT"][:, :],
                                                            op=ALU.mult), [pkk, t["DT"]], [t["tmpA"]])
            kb.op("dve", lambda e: e.tensor_tensor(out=t["tmpA"][:, :], in0=t["tmpA"][:, :], in1=mk["M3c"][:, :], op=ALU.mult),
                  [t["tmpA"], mk["M3c"]], [t["tmpA"]])
            kb.op("dve", lambda e: e.tensor_tensor(out=v3(t["N"]), in0=v3(t["tmpA"]), in1=bc8(be_all[:, 0:8]), op=ALU.mult),
                  [t["tmpA"], be_all], [t["N"]])
            kb.op("dve", lambda e, psc=psc: e.tensor_tensor(out=t["SC"][:, :], in0=psc[0:64, 0:512], in1=t["DT"][:, :],
                                                            op=ALU.mult), [psc, t["DT"]], [t["SC"]])
            kb.op("dve", lambda e: e.tensor_tensor(out=v3(t["Khat"]), in0=v3(t["Ktok"]), in1=bc8(g[:, 24:32]), op=ALU.mult),
                  [t["Ktok"], g], [t["Khat"]])
            pnt = psr()
            for c in range(8):
                kb.op("pe", lambda e, c=c, pnt=pnt: e.transpose(out=pnt[0:64, c * 64:(c + 1) * 64],
                                                                in_=t["N"][:, c * 64:(c + 1) * 64].bitcast(F32), identity=I64f),
                      [t["N"], st["ident"]], [pnt])
            kb.op("act", lambda e, pnt=pnt: e.activation(out=t["NT"][:, :], in_=pnt[0:64, 0:512], func=AF.Identity),
                  [pnt], [t["NT"]])
            kb.op("dve", lambda e: e.tensor_tensor(out=t["R0"][:, :], in0=mk["Ic"][:, :], in1=t["N"][:, :], op=ALU.subtract),
                  [mk["Ic"], t["N"]], [t["R0"]])
            for lev in range(5):
                Pc = t["N"] if lev == 0 else t["P%d" % (lev % 2)]
                PTc = t["NT"] if lev == 0 else t["PT%d" % (lev % 2)]
                Pn = t["P%d" % ((lev + 1) % 2)]
                PTn = t["PT%d" % ((lev + 1) % 2)]
                Rc = t["R%d" % (lev % 2)]
                Rn = t["R%d" % ((lev + 1) % 2)]
                pp1 = psr()
                for c in range(8):
                    sl = slice(c * 64, (c + 1) * 64)
                    kb.op("pe", lambda e, sl=sl, pp1=pp1: e.matmul(pp1[0:64, sl], lhsT=Pc[:, sl], rhs=PTc[:, sl], start=True,
                                                                   stop=True), [Pc, PTc], [pp1])
                kb.op("act", lambda e, pp1=pp1: e.activation(out=PTn[:, :], in_=pp1[0:64, 0:512], func=AF.Identity),
                      [pp1], [PTn])
                if lev < 4:
                    pp2 = psr()
                    for c in range(8):
                        sl = slice(c * 64, (c + 1) * 64)
                        kb.op("pe", lambda e, sl=sl, pp2=pp2: e.matmul(pp2[0:64, sl], lhsT=PTc[:, sl], rhs=Pc[:, sl],
                                                                       start=True, stop=True), [Pc, PTc], [pp2])
                    kb.op("dve", lambda e, pp2=pp2: e.tensor_copy(out=Pn[:, :], in_=pp2[0:64, 0:512]), [pp2], [Pn])
                pr = psr()
                for c in range(8):
                    sl = slice(c * 64, (c + 1) * 64)
                    kb.op("pe", lambda e, sl=sl, pr=pr: e.matmul(pr[0:64, sl], lhsT=PTn[:, sl], rhs=Rc[:, sl], start=True,
                                                                 stop=True), [PTn, Rc], [pr])
                kb.op("dve", lambda e, pr=pr: e.tensor_tensor(out=Rn[:, :], in0=pr[0:64, 0:512], in1=Rc[:, :], op=ALU.add),
                      [pr, Rc], [Rn])
                if lev == 4:
                    kb.op("act", lambda e: e.activation(out=t["Rb"][:, :], in_=Rn[:, :], func=AF.Identity), [Rn], [t["Rb"]])
            pks = psr()
            for c in range(8):
                sl = slice(c * 64, (c + 1) * 64)
                kap, ktt = opnd(KT, c)
                kb.op("pe", lambda e, sl=sl, kap=kap, pks=pks: e.matmul(pks[0:64, sl], lhsT=kap, rhs=t["Sb"][:, sl], start=True,
                                                                        stop=True), [ktt, t["Sb"]], [pks])
            kb.op("dve", lambda e, pks=pks: e.tensor_tensor(out=v3(t["tmpA"]), in0=pks[0:64, 0:512].rearrange("p (c k) -> p c k", k=64),
                                                            in1=bc8(g[:, 16:24]), op=ALU.mult), [pks, g], [t["tmpA"]])
            kb.op("dve", lambda e: e.tensor_tensor(out=t["Z"][:, :], in0=t["tmpA"][:, :], in1=t["Vtok"][:, :], op=ALU.add),
                  [t["tmpA"], t["Vtok"]], [t["Z"]])
            pvn = psr()
            for c in range(8):
                sl = slice(c * 64, (c + 1) * 64)
                kb.op("pe", lambda e, sl=sl, pvn=pvn: e.matmul(pvn[0:64, sl], lhsT=t["Rb"][:, sl], rhs=t["Z"][:, sl], start=True,
                                                               stop=True), [t["Rb"], t["Z"]], [pvn])
            kb.op("dve", lambda e, pvn=pvn: e.tensor_tensor(out=v3(t["vn"]), in0=pvn[0:64, 0:512].rearrange("p (c k) -> p c k", k=64),
                                                            in1=bc8(be_all[:, 0:8]), op=ALU.mult), [pvn, be_all], [t["vn"]])
            poi = psr()
            pos_ = psr()
            psn = psr()
            for c in range(8):
                sl = slice(c * 64, (c + 1) * 64)
                qap, qtt = opnd(QT, c)
                kb.op("pe", lambda e, sl=sl, poi=poi: e.matmul(poi[0:64, sl], lhsT=t["vn"][:, sl], rhs=t["SC"][:, sl], start=True,
                                                               stop=True), [t["vn"], t["SC"]], [poi])
                kb.op("pe", lambda e, sl=sl, qap=qap, pos_=pos_: e.matmul(pos_[0:64, sl], lhsT=t["Sb"][:, sl], rhs=qap, start=True,
                                                                          stop=True), [t["Sb"], qtt], [pos_])
                kb.op("pe", lambda e, sl=sl, psn=psn: e.matmul(psn[0:64, sl], lhsT=t["Khat"][:, sl], rhs=t["vn"][:, sl], start=True,
                                                               stop=True), [t["Khat"], t["vn"]], [psn])
            kb.op("dve", lambda e: e.tensor_tensor(out=v3(t["S"]), in0=v3(t["S"]), in1=bc8(g[:, 32:40]), op=ALU.mult),
                  [t["S"], g], [t["S"]])
            kb.op("dve", lambda e, psn=psn: e.tensor_tensor(out=t["S"][:, :], in0=psn[0:64, 0:512], in1=t["S"][:, :], op=ALU.add),
                  [psn, t["S"]], [t["S"]])
            kb.op("act", lambda e: e.activation(out=t["Sb"][:, :], in_=t["S"][:, :], func=AF.Identity), [t["S"]], [t["Sb"]])
            kb.op("dve", lambda e, pos_=pos_: e.tensor_tensor(out=t["tmpA"][:, :], in0=pos_[0:64, 0:512], in1=t["EG"][:, :],
                                                              op=ALU.mult), [pos_, t["EG"]], [t["tmpA"]])
            for dr in range(2):
                c0 = cur[dr][2]
                kb.op("dve", lambda e, dr=dr, c0=c0, poi=poi: e.tensor_tensor(
                    out=ob[dr][:, :, c0:c0 + C], in0=poi[0:64, dr * 256:(dr + 1) * 256].rearrange("p (c k) -> p c k", k=64),
                    in1=t["tmpA"][:, dr * 256:(dr + 1) * 256].rearrange("p (c k) -> p c k", k=64), op=ALU.add),
                    [poi, t["tmpA"]], [ob[dr]])
            if cj == 3:
                for dr in range(2):
                    b, tok0, _ = cur[dr]
                    n = blocks[b][1]
                    if b not in stored:
                        for h in range(4):
                            kb.dma("sp", ogd_d, ogd_d.ap[:, h, tok0:tok0 + n], ob[dr], ob[dr][:, h, 0:n], semt=ob[dr])
                        stored.add(b)
                    else:
                        for h in range(4):
                            kb.dma("sp", st["junk"], obuf2[:, h, 0:n], ogd_d, ogd_d.ap[:, h, tok0:tok0 + n])
                        kb.op("dve", lambda e, dr=dr, n=n: e.tensor_tensor(out=obuf2[:, :, 0:n], in0=obuf2[:, :, 0:n],
                                                                           in1=ob[dr][:, :, 0:n], op=ALU.add),
                              [st["junk"], ob[dr]], [st["junk"]])
                        finish_block(tok0, n)


_CONST = {}


def _consts():
    if _CONST:
        return _CONST
    _CONST["ident"] = np.eye(128, dtype=np.float32)
    nf = 16
    inv = (10000.0 ** (-np.arange(nf, dtype=np.float32) / nf)).astype(np.float32)
    t = np.arange(SEQ)
    rows = (t // 64).astype(np.float32)
    cols = (t % 64).astype(np.float32)
    ang = np.concatenate([rows[:, None] * inv, cols[:, None] * inv], axis=-1).astype(np.float32)
    cosT = np.cos(ang).T.astype(np.float32)
    sinT = np.sin(ang).T.astype(np.float32)
    _CONST["ropeC"] = np.ascontiguousarray(np.tile(cosT, (4, 1)))
    _CONST["ropeS"] = np.ascontiguousarray(np.tile(sinT, (4, 1)))
    rot = np.zeros((128, 128), np.float32)
    for m in range(2):
        for d in range(64):
            i = m * 64 + d
            if d < 32:
                rot[m * 64 + d + 32, i] = -1.0
            else:
                rot[m * 64 + d - 32, i] = 1.0
    _CONST["rotm"] = rot
    cm = np.ones((128, 512), np.float32)
    cm[:, ::32] = 0.0
    _CONST["c_cmask"] = cm
    bd = np.zeros((128, 128), np.float32)
    bd[:64, :64] = 1.0
    bd[64:, 64:] = 1.0
    _CONST["c_bdm"] = bd
    ii = np.arange(32)
    up = (ii[:, None] <= ii[None, :]).astype(np.float32)
    lo = (ii[:, None] >= ii[None, :]).astype(np.float32)
    i6 = np.arange(64)
    _CONST["c_gm_L"] = (i6[:, None] > i6[None, :]).astype(np.float32)
    _CONST["c_gm_U"] = (i6[:, None] < i6[None, :]).astype(np.float32)
    _CONST["c_gm_LI"] = (i6[:, None] >= i6[None, :]).astype(np.float32)
    _CONST["c_gm_UI"] = (i6[:, None] <= i6[None, :]).astype(np.float32)
    _CONST["c_gm_ones"] = np.ones((64, 64), np.float32)
    L_, U_, LI_, UI_ = _CONST["c_gm_L"], _CONST["c_gm_U"], _CONST["c_gm_LI"], _CONST["c_gm_UI"]
    _CONST["c_gM1c"] = np.ascontiguousarray(np.concatenate([L_] * 4 + [U_] * 4, axis=1))
    _CONST["c_gM2c"] = np.ascontiguousarray(np.concatenate([UI_] * 4 + [LI_] * 4, axis=1))
    _CONST["c_gM3c"] = np.ascontiguousarray(np.concatenate([U_] * 4 + [L_] * 4, axis=1))
    _CONST["c_gIc"] = np.ascontiguousarray(np.concatenate([np.eye(64, dtype=np.float32)] * 8, axis=1))
    _CONST["c_hgmask"] = np.ascontiguousarray(np.stack([np.tile(up, (1, 2)), np.tile(lo, (1, 2))], axis=1))
    return _CONST


def make_in_maps(inputs):
    cst = _consts()
    f = lambda a: np.ascontiguousarray(np.asarray(a, dtype=np.float32))
    x, c, ctx, c_ctx = f(inputs["x"]), f(inputs["c"]), f(inputs["ctx"]), f(inputs["c_ctx"])
    ada_w, ada_b = f(inputs["ada_w"]), f(inputs["ada_b"])
    norm_g = f(inputs["norm_g"])
    shared = {
        "ada_w": ada_w, "ada_b": ada_b,
        "ada_bc": np.ascontiguousarray(ada_b.reshape(DEPTH, 48, 128).transpose(0, 2, 1)),
        "norm_g": norm_g,
        "norm_gc": np.ascontiguousarray(norm_g.reshape(DEPTH, 4, KC, 128).transpose(0, 1, 3, 2)),
        "w_in": f(inputs["w_in"]), "w_out": f(inputs["w_out"]),
        "ident": cst["ident"], "ropeC": cst["ropeC"], "ropeS": cst["ropeS"], "rotm": cst["rotm"],
        "da_lambda": f(inputs["da_lambda"]).reshape(DEPTH, 256),
        "da_subln_g": f(inputs["da_subln_g"]),
        "c_cmask": cst["c_cmask"], "c_bdm": cst["c_bdm"], "c_hgmask": cst["c_hgmask"],
        "hg_lb_c": np.ascontiguousarray(f(inputs["hg_lb_logits"]).reshape(DEPTH, 4, 128).transpose(2, 0, 1)),
        "hg_ng_c": np.ascontiguousarray(np.tile(f(inputs["hg_norm_g"]), (1, 2)).T),
        "c_gm_L": cst["c_gm_L"], "c_gm_U": cst["c_gm_U"], "c_gm_LI": cst["c_gm_LI"], "c_gm_UI": cst["c_gm_UI"],
        "c_gm_ones": cst["c_gm_ones"], "c_gM1c": cst["c_gM1c"], "c_gM2c": cst["c_gM2c"], "c_gM3c": cst["c_gM3c"],
        "c_gIc": cst["c_gIc"],
        "gd_cw_c": np.ascontiguousarray(f(inputs["gd_conv_w"]).reshape(DEPTH, 3, 12, 64).transpose(3, 0, 2, 1)),
        "gd_ng_c": np.ascontiguousarray(f(inputs["gd_norm_g"]).T),
        "gd_cw128": np.ascontiguousarray(f(inputs["gd_conv_w"]).reshape(DEPTH, 3, 6, 128).transpose(3, 0, 2, 1)),
        "gd_dtb": f(inputs["gd_dt_bias"]).reshape(DEPTH, 8), "gd_alog": f(inputs["gd_a_log"]).reshape(DEPTH, 8),
        "ffn_w_up": f(inputs["ffn_w_up"]), "ffn_w_down": f(inputs["ffn_w_down"]),
        "ffn_cw": np.ascontiguousarray(f(inputs["ffn_conv_w"]).reshape(DEPTH, 3, 44, 128).transpose(0, 3, 2, 1)),
        "ffn_cb": np.ascontiguousarray(f(inputs["ffn_conv_b"]).reshape(DEPTH, 44, 128).transpose(0, 2, 1)),
    }
    maps = []
    for b in range(8):
        cv = np.stack([c[b], c_ctx], axis=1)
        m = dict(shared)
        m["x"] = x[b]
        m["ctx"] = ctx[b]
        m["cvT"] = np.ascontiguousarray(cv.reshape(KC, 128, 2).transpose(1, 0, 2))
        maps.append(m)
    return maps


def kernel(**inputs):
    nc = build()
    maps = make_in_maps(inputs)
    res = run_bass_kernel_spmd(nc, maps, core_ids=list(range(8)))
    return np.stack([np.asarray(r["out"]) for r in res.results], axis=0).astype(np.float32)
```

```python
import math
from contextlib import ExitStack

import numpy as np
import concourse.bass as bass
import concourse.mybir as mybir
from concourse.bass_utils import run_bass_kernel_spmd

F32 = mybir.dt.float32
BF16 = mybir.dt.bfloat16
AF = mybir.ActivationFunctionType
ALU = mybir.AluOpType
AX = mybir.AxisListType

D = 1024
SEQ = 2048
CTX = 256
NT = SEQ + CTX
DEPTH = 2
DFF = 2816
INC = 3856
EPS = 1e-6
KC = 8
O_DAQ, O_DAK, O_DAV = 0, 512, 1024
O_HGQ, O_HGI, O_HGF, O_HGG = 1536, 1792, 2048, 2560
O_GDQKV, O_GDA, O_GDB, O_GDG = 2816, 3584, 3592, 3600


class T:
    __slots__ = ("ap", "wr", "rd", "sem", "cnt", "name", "psum")

    def __init__(self, ap, name=""):
        self.ap = ap
        self.wr = None
        self.rd = {}
        self.sem = None
        self.cnt = 0
        self.name = name
        self.psum = False

    def __getitem__(self, k):
        return self.ap[k]


class KB:
    def __init__(self):
        self.nc = bass.Bass("TRN2", target_bir_lowering=False)
        nc = self.nc
        self.es = ExitStack()
        self.eng = dict(pe=nc.tensor, act=nc.scalar, dve=nc.vector, pool=nc.gpsimd, sp=nc.sync)
        self.esem = {}
        self.ecnt = {}
        self.waited = {}
        for e in self.eng:
            self.esem[e] = self.newsem("e_" + e)
            self.ecnt[e] = 0
            self.waited[e] = {}
        self.nsem = 0
        self.dma_ts = []
        self.sem_pool = []
        self.psum_banks = []
        self.psum_i = 0
        self.uid = 0

    def newsem(self, name):
        return self.es.enter_context(self.nc.semaphore(name))

    def sb(self, name, shape, dtype=F32):
        t = self.es.enter_context(self.nc.sbuf_tensor("s_" + name, list(shape), dtype))
        return T(t, name)

    def dram(self, name, shape, dtype=F32, kind="Internal"):
        return T(self.nc.dram_tensor(name, list(shape), dtype, kind=kind).ap(), name)

    def init_psum(self):
        for i in range(8):
            t = self.es.enter_context(self.nc.psum_tensor("psb%d" % i, [128, 512], F32))
            self.psum_banks.append(T(t, "psb%d" % i))
            self.psum_banks[-1].psum = True

    def ps(self):
        t = self.psum_banks[self.psum_i % 8]
        self.psum_i += 1
        return t

    def _deps(self, e, reads, writes, pe_self=False):
        need = {}

        def add(ev):
            if ev is None:
                return
            s, v = ev
            if e == "pe" and s is self.esem["pe"] and not pe_self:
                return
            k = id(s)
            if k not in need or need[k][1] < v:
                need[k] = (s, v)

        for t in reads:
            add(t.wr)
            if t.psum:
                for ev in t.rd.values():
                    if ev[0] is not self.esem[e]:
                        add(ev)
        for t in writes:
            add(t.wr)
            for ev in t.rd.values():
                add(ev)
        w = self.waited[e]
        for k, (s, v) in need.items():
            if w.get(k, 0) >= v:
                continue
            self.eng[e].wait_ge(s, v)
            w[k] = v

    def _record(self, ev, reads, writes):
        k = id(ev[0])
        for t in reads:
            if k not in t.rd or t.rd[k][1] < ev[1]:
                t.rd[k] = ev
        for t in writes:
            t.wr = ev
            t.rd = {}

    def op(self, e, fn, reads=(), writes=(), pe_self=False):
        self._deps(e, reads, writes, pe_self)
        ins = fn(self.eng[e])
        self.ecnt[e] += 1
        ins.then_inc(self.esem[e], 1)
        ev = (self.esem[e], self.ecnt[e])
        self._record(ev, reads, writes)
        return ev

    def dma(self, q, out_t, out_ap, in_t, in_ap, semt=None, **kw):
        self._deps(q, [in_t], [out_t])
        st = semt if semt is not None else out_t
        if st.sem is None:
            if self.sem_pool:
                st.sem, st.cnt = self.sem_pool.pop()
            else:
                st.sem = self.newsem("d%d" % self.nsem)
                self.nsem += 1
            self.dma_ts.append(st)
        ins = self.eng[q].dma_start(out=out_ap, in_=in_ap, **kw)
        ins.then_inc(st.sem, 16)
        st.cnt += 16
        ev = (st.sem, st.cnt)
        self._record(ev, [in_t], [out_t])
        return ev

    def barrier(self):
        evs = {}
        for e in self.eng:
            if self.ecnt[e] > 0:
                evs[id(self.esem[e])] = (self.esem[e], self.ecnt[e])
        for t in self.dma_ts:
            evs[id(t.sem)] = (t.sem, t.cnt)
        for e in self.eng:
            for k, ev in evs.items():
                if e == "pe" and ev[0] is self.esem["pe"]:
                    continue
                self.wait_ev(e, ev)

    def phase(self):
        return Phase(self)

    def wait_ev(self, e, ev):
        s, v = ev
        k = id(s)
        if self.waited[e].get(k, 0) >= v:
            return
        self.eng[e].wait_ge(s, v)
        self.waited[e][k] = v


class Phase:
    def __init__(self, kb):
        self.kb = kb
        self.es = ExitStack()
        self.ts = []

    def sb(self, name, shape, dtype=F32):
        self.kb.uid += 1
        t = self.es.enter_context(self.kb.nc.sbuf_tensor("p%d_%s" % (self.kb.uid, name), list(shape), dtype))
        tt = T(t, name)
        self.ts.append(tt)
        return tt

    def __enter__(self):
        return self

    def __exit__(self, *a):
        self.kb.barrier()
        for t in self.ts:
            if t.sem is not None and t in self.kb.dma_ts:
                self.kb.dma_ts.remove(t)
                self.kb.sem_pool.append((t.sem, t.cnt))
                t.sem = None
        self.es.close()
        return False


def build(dbg=None, nlayers=DEPTH, skip=()):
    kb = KB()
    nc = kb.nc
    dbg = dbg or {}
    x_d = T(nc.dram_tensor("x", [SEQ, D], F32, kind="ExternalInput").ap())
    ctx_d = T(nc.dram_tensor("ctx", [CTX, D], F32, kind="ExternalInput").ap())
    cvT_d = T(nc.dram_tensor("cvT", [128, KC, 2], F32, kind="ExternalInput").ap())
    ada_w_d = T(nc.dram_tensor("ada_w", [DEPTH, D, 6 * D], F32, kind="ExternalInput").ap())
    ada_bc_d = T(nc.dram_tensor("ada_bc", [DEPTH, 128, 48], F32, kind="ExternalInput").ap())
    ada_b_d = T(nc.dram_tensor("ada_b", [DEPTH, 6 * D], F32, kind="ExternalInput").ap())
    ng_d = T(nc.dram_tensor("norm_g", [DEPTH, 4, D], F32, kind="ExternalInput").ap())
    ngc_d = T(nc.dram_tensor("norm_gc", [DEPTH, 4, 128, KC], F32, kind="ExternalInput").ap())
    w_in_d = T(nc.dram_tensor("w_in", [DEPTH, D, INC], F32, kind="ExternalInput").ap())
    w_out_d = T(nc.dram_tensor("w_out", [DEPTH, D, D], F32, kind="ExternalInput").ap())
    ident_d = T(nc.dram_tensor("ident", [128, 128], F32, kind="ExternalInput").ap())
    out_d = T(nc.dram_tensor("out", [SEQ, D], F32, kind="ExternalOutput").ap())
    gates_d = kb.dram("gates_scr", [DEPTH, 2, 2, D])
    dbg_d = {}
    for name, (shp, dt_) in dbg.items():
        if name == "mix_in":
            continue
        dbg_d[name] = T(nc.dram_tensor("dbg_" + name, list(shp), BF16 if dt_ == "bf16" else F32,
                                       kind="ExternalOutput").ap())

    kb.init_psum()
    xs = [kb.sb("x%d" % i, [128, D]) for i in range(18)]
    hT = kb.sb("hT", [128, KC, NT], BF16)
    ident = kb.sb("ident", [128, 128])
    identb = kb.sb("identb", [128, 128], BF16)
    cvT = kb.sb("cvT", [128, KC, 2])
    scT = kb.sb("scT", [128, KC, 2], BF16)
    modc = kb.sb("modc", [128, DEPTH, 4, KC, 2])
    gs_c = kb.sb("gs_c", [128, DEPTH, 2, KC, 2])
    sh_c = kb.sb("sh_c", [128, DEPTH, 2, KC, 2])
    ngc = kb.sb("ngc", [128, DEPTH, 4, KC])
    adab_c = kb.sb("adab_c", [128, DEPTH, 48])
    wada = [kb.sb("wada%d" % i, [128, KC, 512], BF16) for i in range(2)]
    stat = kb.sb("stat", [128, 64])
    junk = kb.sb("junk", [128, D])
    xn = kb.sb("xn", [128, D], BF16)
    ph0 = kb.phase()
    grow = ph0.sb("grow", [2, 2, D])
    brow = ph0.sb("brow", [2, 2, D])
    grow_g = ph0.sb("grow_g", [2, 2, D])

    finals = []
    kb.dma("sp", ident, ident[:], ident_d, ident_d[:])
    kb.dma("sp", cvT, cvT[:], cvT_d, cvT_d[:])
    kb.dma("sp", ngc, ngc[:], ngc_d, ngc_d.ap.rearrange("l j p c -> p l j c"))
    kb.dma("sp", adab_c, adab_c[:], ada_bc_d, ada_bc_d.ap.rearrange("l p c -> p l c"))
    for i in range(2):
        kb.dma("sp", xs[i], xs[i][:], ctx_d, ctx_d[i * 128:(i + 1) * 128, :])
    for i in range(16):
        kb.dma("sp", xs[2 + i], xs[2 + i][:], x_d, x_d[i * 128:(i + 1) * 128, :])
    kb.op("dve", lambda e: e.tensor_copy(out=identb[:], in_=ident[:]), [ident], [identb])
    kb.op("act", lambda e: e.activation(out=scT[:], in_=cvT[:], func=AF.Silu), [cvT], [scT])

    for l in range(nlayers):
        for seg_i, j in enumerate((0, 1, 3, 4)):
            for half in range(2):
                wt = wada[(seg_i * 2 + half) % 2]
                c0 = j * D + half * 512
                kb.dma("pool", wt, wt[:], ada_w_d,
                       ada_w_d.ap[l, :, c0:c0 + 512].rearrange("(c p) n -> p c n", p=128))
                pst = kb.ps()
                for fc in range(4):
                    for kc in range(KC):
                        kb.op("pe", lambda e, fc=fc, kc=kc: e.matmul(
                            pst[:, fc * 2:fc * 2 + 2], lhsT=wt[:, kc, fc * 128:(fc + 1) * 128],
                            rhs=scT[:, kc, :], start=(kc == 0), stop=(kc == KC - 1)),
                            [wt, scT], [pst])
                for fc in range(4):
                    cc = half * 4 + fc
                    kb.op("dve", lambda e, fc=fc, cc=cc: e.tensor_scalar(
                        out=modc[:, l, seg_i, cc, :], in0=pst[:, fc * 2:fc * 2 + 2],
                        scalar1=adab_c[:, l, j * 8 + cc:j * 8 + cc + 1], scalar2=None, op0=ALU.add),
                        [pst, adab_c], [modc])
        for m in range(2):
            for r in range(2):
                kb.op("dve", lambda e, m=m, r=r: e.scalar_tensor_tensor(
                    out=gs_c[:, l, m, :, r], in0=modc[:, l, 2 * m + 1, :, r], scalar=1.0,
                    in1=ngc[:, l, 2 * m, :], op0=ALU.add, op1=ALU.mult), [modc, ngc], [gs_c])
                kb.op("dve", lambda e, m=m, r=r: e.tensor_copy(
                    out=sh_c[:, l, m, :, r], in_=modc[:, l, 2 * m, :, r]), [modc], [sh_c])
        for m, j in enumerate((2, 5)):
            kb.dma("sp", brow, brow[0:1, m, :], ada_b_d, ada_b_d.ap[l:l + 1, j * D:(j + 1) * D])
            kb.dma("sp", brow, brow[1:2, m, :], ada_b_d, ada_b_d.ap[l:l + 1, j * D:(j + 1) * D])
            kb.dma("sp", grow_g, grow_g[0:1, m, :], ng_d, ng_d.ap[l, 2 * m + 1:2 * m + 2, :])
            kb.dma("sp", grow_g, grow_g[1:2, m, :], ng_d, ng_d.ap[l, 2 * m + 1:2 * m + 2, :])
            for half in range(2):
                wt = wada[half]
                c0 = j * D + half * 512
                kb.dma("pool", wt, wt[:], ada_w_d,
                       ada_w_d.ap[l, :, c0:c0 + 512].rearrange("(c p) n -> p c n", p=128))
                pst = kb.ps()
                for kc in range(KC):
                    kb.op("pe", lambda e, kc=kc: e.matmul(
                        pst[0:2, :], lhsT=scT[:, kc, :], rhs=wt[:, kc, :],
                        start=(kc == 0), stop=(kc == KC - 1)), [wt, scT], [pst])
                kb.op("dve", lambda e, half=half, m=m: e.tensor_tensor(
                    out=grow[0:2, m, half * 512:(half + 1) * 512], in0=pst[0:2, :],
                    in1=brow[0:2, m, half * 512:(half + 1) * 512], op=ALU.add), [pst, brow], [grow])
            kb.op("dve", lambda e, m=m: e.tensor_tensor(
                out=grow[0:2, m, :], in0=grow[0:2, m, :], in1=grow_g[0:2, m, :], op=ALU.mult),
                [grow, grow_g], [grow])
            kb.dma("sp", gates_d, gates_d.ap[l, m, :, :], grow, grow[0:2, m, :], semt=grow)

    ph0.__exit__()
    st = dict(kb=kb, nc=nc, l=None, xs=xs, hT=hT, ident=ident, identb=identb, gs_c=gs_c, sh_c=sh_c,
              stat=stat, junk=junk, xn=xn, wbuf=wada, w_in_d=w_in_d, w_out_d=w_out_d, gates_d=gates_d,
              dbg_d=dbg_d, finals=finals, x_d=x_d, out_d=out_d, ng_d=ng_d)
    extra_inputs(st)
    if "mix_in" in dbg:
        st["mix_in"] = T(nc.dram_tensor("mix_in", [NT, D], BF16, kind="ExternalInput").ap())
    for l in range(nlayers):
        st["l"] = l
        norm_to_hT(st, l, 0)
        if "hT%d" % l in dbg_d:
            dump_hT(st, "hT%d" % l)
        if "stop_norm" in dbg:
            break
        if "da" not in skip:
            da_mixer(st, l)
        if "hg" not in skip:
            hg_mixer(st, l)
        if "gd" not in skip:
            (gd_mixer if "gd1" in skip else gd_mixer2)(st, l)
        if "mix_in" in dbg:
            pass
        if "mix%d" % l in dbg_d:
            d_ = dbg_d["mix%d" % l]
            for i_ in range(18):
                finals.append(kb.dma("sp", d_, d_.ap[i_ * 128:(i_ + 1) * 128, :], st["mix_d"],
                                     st["mix_d"].ap[i_ * 128:(i_ + 1) * 128, :], semt=d_))
        if "op" not in skip:
            out_proj(st, l)
        if "x1_%d" % l in dbg_d:
            d_ = dbg_d["x1_%d" % l]
            for i_ in range(18):
                finals.append(kb.dma("sp", d_, d_.ap[i_ * 128:(i_ + 1) * 128, :], xs[i_], xs[i_][:], semt=xs[i_]))
        ftiles = range(18) if l < DEPTH - 1 else range(2, 18)
        if "ffn" not in skip:
            norm_to_hT(st, l, 1, ftiles)
            ffn(st, l)
        if "x2_%d" % l in dbg_d:
            d_ = dbg_d["x2_%d" % l]
            for i_ in range(18):
                finals.append(kb.dma("sp", d_, d_.ap[i_ * 128:(i_ + 1) * 128, :], xs[i_], xs[i_][:], semt=xs[i_]))
    for i_ in range(16):
        finals.append(kb.dma("sp", out_d, out_d.ap[i_ * 128:(i_ + 1) * 128, :], xs[2 + i_], xs[2 + i_][:],
                             semt=xs[2 + i_]))
    for ev in finals:
        kb.wait_ev("sp", ev)
    kb.es.close()
    return nc


def extra_inputs(st):
    kb, nc = st["kb"], st["nc"]
    st["ropeC_d"] = T(nc.dram_tensor("ropeC", [128, SEQ], F32, kind="ExternalInput").ap())
    st["ropeS_d"] = T(nc.dram_tensor("ropeS", [128, SEQ], F32, kind="ExternalInput").ap())
    st["rotm_d"] = T(nc.dram_tensor("rotm", [128, 128], F32, kind="ExternalInput").ap())
    st["dalam_d"] = T(nc.dram_tensor("da_lambda", [DEPTH, 256], F32, kind="ExternalInput").ap())
    st["subln_d"] = T(nc.dram_tensor("da_subln_g", [DEPTH, 128], F32, kind="ExternalInput").ap())
    st["mix_d"] = kb.dram("mix_scr", [NT, D], BF16)
    st["wup_d"] = T(nc.dram_tensor("ffn_w_up", [DEPTH, D, 2 * DFF], F32, kind="ExternalInput").ap())
    st["wdn_d"] = T(nc.dram_tensor("ffn_w_down", [DEPTH, DFF, D], F32, kind="ExternalInput").ap())
    st["fcw_d"] = T(nc.dram_tensor("ffn_cw", [DEPTH, 128, 44, 3], F32, kind="ExternalInput").ap())
    st["fcb_d"] = T(nc.dram_tensor("ffn_cb", [DEPTH, 128, 44], F32, kind="ExternalInput").ap())
    st["hglb_d"] = T(nc.dram_tensor("hg_lb_c", [128, DEPTH, 4], F32, kind="ExternalInput").ap())
    st["hgng_d"] = T(nc.dram_tensor("hg_ng_c", [128, DEPTH], F32, kind="ExternalInput").ap())
    for nm, shp in (("cmask", [128, 512]), ("bdm", [128, 128]), ("hgmask", [32, 2, 64])):
        d_ = T(nc.dram_tensor("c_" + nm, shp, F32, kind="ExternalInput").ap())
        t_ = kb.sb(nm, shp)
        kb.dma("sp", t_, t_[:], d_, d_[:])
        st[nm] = t_
    st["ogd_d"] = kb.dram("ogd_scr", [64, 4, NT])
    st["gqkv_d"] = kb.dram("gqkv_scr", [12, 64, NT], BF16)
    st["gdcw128_d"] = T(nc.dram_tensor("gd_cw128", [128, DEPTH, 6, 3], F32, kind="ExternalInput").ap())
    st["gdcw_d"] = T(nc.dram_tensor("gd_cw_c", [64, DEPTH, 12, 3], F32, kind="ExternalInput").ap())
    st["gdng_d"] = T(nc.dram_tensor("gd_ng_c", [64, DEPTH], F32, kind="ExternalInput").ap())
    st["gddt_d"] = T(nc.dram_tensor("gd_dtb", [DEPTH, 8], F32, kind="ExternalInput").ap())
    st["gdal_d"] = T(nc.dram_tensor("gd_alog", [DEPTH, 8], F32, kind="ExternalInput").ap())
    for nm in ("M1c", "M2c", "M3c", "Ic"):
        st["gc_" + nm] = T(nc.dram_tensor("c_g" + nm, [64, 512], F32, kind="ExternalInput").ap())
    for nm in ("gm_L", "gm_U", "gm_LI", "gm_UI", "gm_ones"):
        d_ = T(nc.dram_tensor("c_" + nm, [64, 64], F32, kind="ExternalInput").ap())
        t_ = kb.sb(nm, [64, 64])
        kb.dma("sp", t_, t_[:], d_, d_[:])
        st[nm] = t_
    bdb = kb.sb("bdb", [128, 128], BF16)
    kb.op("dve", lambda e: e.tensor_copy(out=bdb[:], in_=st["bdm"][:]), [st["bdm"]], [bdb])
    st["bdb"] = bdb
    rotf = kb.sb("rotf", [128, 128])
    rotb = kb.sb("rotb", [128, 128], BF16)
    kb.dma("sp", rotf, rotf[:], st["rotm_d"], st["rotm_d"][:])
    kb.op("dve", lambda e: e.tensor_copy(out=rotb[:], in_=rotf[:]), [rotf], [rotb])
    st["rotb"] = rotb
    epst = kb.sb("epst", [128, 1])
    kb.op("dve", lambda e: e.memset(epst[:], EPS), [], [epst])
    st["epst"] = epst


def load_w(st, wt, w_d, l, col0, ncols, dcol0=0, q="pool"):
    kb = st["kb"]
    kb.dma(q, wt, wt[:, :, dcol0:dcol0 + ncols], w_d,
           w_d.ap[l, :, col0:col0 + ncols].rearrange("(c p) n -> p c n", p=128))


def rstd_cols(st, src_ap, dst_ap, n, tmp_ap, rd, wr):
    kb = st["kb"]
    epst = st["epst"]
    np_ = src_ap.shape[0]
    kb.op("act", lambda e: e.activation(out=tmp_ap, in_=src_ap, func=AF.Ln, scale=1.0 / n, bias=epst[0:np_, :]),
          rd + [epst], wr)
    kb.op("act", lambda e: e.activation(out=dst_ap, in_=tmp_ap, func=AF.Exp, scale=-0.5), wr, wr)


def norm_to_hT(st, l, m, tiles=range(18)):
    kb = st["kb"]
    xs, hT, stat, junk, xn, identb = st["xs"], st["hT"], st["stat"], st["junk"], st["xn"], st["identb"]
    gs_c, sh_c = st["gs_c"], st["sh_c"]
    for i in tiles:
        kb.op("dve", lambda e, i=i: e.tensor_tensor(out=junk[:], in0=xs[i][:], in1=xs[i][:], op=ALU.mult),
              [xs[i]], [junk])
        kb.op("dve", lambda e, i=i: e.reduce_sum(out=stat[:, i:i + 1], in_=junk[:], axis=AX.X), [junk], [stat])
    rstd_cols(st, stat[:, 0:18], stat[:, 36:54], D, stat[:, 18:36], [stat], [stat])
    xbufs = [(xn, xn.ap), (junk, junk.ap.bitcast(BF16)[:, 0:D])]
    for ii, i in enumerate(tiles):
        r = 1 if i < 2 else 0
        xt_, xap = xbufs[ii % 2]
        kb.op("dve", lambda e, i=i, xap=xap: e.tensor_scalar(out=xap[:, :], in0=xs[i][:], scalar1=stat[:, 36 + i:37 + i],
                                                             scalar2=None, op0=ALU.mult), [xs[i], stat], [xt_])
        pst = kb.ps()
        pb = pst.ap.bitcast(BF16)
        for kc in range(KC):
            kb.op("pe", lambda e, kc=kc, xap=xap: e.transpose(out=pb[:, kc * 128:(kc + 1) * 128],
                                                               in_=xap[:, kc * 128:(kc + 1) * 128], identity=identb[:]),
                  [xt_, identb], [pst])
        for kc in range(KC):
            kb.op("act", lambda e, kc=kc, i=i, r=r: e.activation(
                out=hT[:, kc, i * 128:(i + 1) * 128], in_=pb[:, kc * 128:(kc + 1) * 128], func=AF.Identity,
                scale=gs_c[:, l, m, kc, r:r + 1], bias=sh_c[:, l, m, kc, r:r + 1]), [pst, gs_c, sh_c], [hT])


def dump_hT(st, name):
    kb = st["kb"]
    d = st["dbg_d"][name]
    for kc in range(KC):
        st["finals"].append(kb.dma("sp", d, d.ap[:, kc, :], st["hT"], st["hT"][:, kc, :], semt=st["hT"]))


def dump(st, name, t, ap):
    kb = st["kb"]
    d = st["dbg_d"][name]
    st["finals"].append(kb.dma("sp", d, d[:], t, ap, semt=t))


def da_mixer(st, l):
    kb = st["kb"]
    hT, wbuf, w_in_d, stat, rotb, dbg_d = st["hT"], st["wbuf"], st["w_in_d"], st["stat"], st["rotb"], st["dbg_d"]
    PB = kb.psum_banks
    lam_init = 0.8 - 0.6 * math.exp(-0.3 * l)
    need_ctx = l < DEPTH - 1
    with kb.phase() as ph:
        kT = ph.sb("kT", [128, 4, NT], BF16)
        vaug = ph.sb("vaug", [128, 18, 4, 132], BF16)
        qTs = [ph.sb("qT%d" % i, [128, 512], BF16) for i in range(2)]
        rawb = ph.sb("rawb", [128, 512], BF16)
        rc = ph.sb("rc", [128, 512])
        rs = ph.sb("rs", [128, 512])
        t1 = ph.sb("t1", [128, 512])
        t2 = ph.sb("t2", [128, 512])
        pts = [ph.sb("pt%d" % i, [128, 512], BF16) for i in range(2)]
        o0 = ph.sb("o0", [128, 4, 128])
        o1s = [ph.sb("o1_%d" % i, [128, 128]) for i in range(4)]
        ostage = ph.sb("ostage", [128, 4, 512], BF16)
        lamt = ph.sb("lamt", [128, 256])
        lamj = ph.sb("lamj", [128, 128])
        lams = ph.sb("lams", [128, 8])
        sg = ph.sb("sg", [128, 128])
        dst = ph.sb("dast", [128, 16])
        dst2 = ph.sb("dast2", [128, 16])
        kb.dma("sp", lamt, lamt[:], st["dalam_d"], st["dalam_d"].ap[l:l + 1, :].partition_broadcast(128))
        kb.dma("sp", sg, sg[:], st["subln_d"], st["subln_d"].ap[l:l + 1, :].partition_broadcast(128))
        for j in range(2):
            kb.op("dve", lambda e, j=j: e.tensor_tensor(out=lamj[:, 0:64], in0=lamt[:, 128 * j:128 * j + 64],
                                                        in1=lamt[:, 128 * j + 64:128 * j + 128], op=ALU.mult),
                  [lamt], [lamj])
            kb.op("dve", lambda e, j=j: e.reduce_sum(out=lams[:, j:j + 1], in_=lamj[:, 0:64], axis=AX.X),
                  [lamj], [lams])
        kb.op("act", lambda e: e.activation(out=lams[:, 2:4], in_=lams[:, 0:2], func=AF.Exp), [lams], [lams])
        kb.op("dve", lambda e: e.scalar_tensor_tensor(out=lams[:, 4:5], in0=lams[:, 3:4], scalar=-lam_init,
                                                      in1=lams[:, 2:3], op0=ALU.add, op1=ALU.subtract),
              [lams], [lams])
        kb.op("dve", lambda e: e.tensor_scalar(out=sg[:], in0=sg[:], scalar1=1.0 - lam_init, scalar2=None,
                                               op0=ALU.mult), [sg], [sg])
        for i_ in range(18):
            kb.op("pool", lambda e, i_=i_: e.memset(vaug[:, i_, :, :], 1.0), [], [vaug])

        def rope_block(ps_t, n, tok0, dst_t, dst_ap, lat0):
            kb.dma("sp", rc, rc[:, 0:n], st["ropeC_d"], st["ropeC_d"][:, lat0:lat0 + n])
            kb.dma("sp", rs, rs[:, 0:n], st["ropeS_d"], st["ropeS_d"][:, lat0:lat0 + n])
            kb.op("act", lambda e: e.activation(out=rawb[:, 0:n], in_=ps_t[:, 0:n], func=AF.Identity), [ps_t], [rawb])
            ps2 = PB[6]
            kb.op("pe", lambda e: e.matmul(ps2[:, 0:n], lhsT=rotb[:], rhs=rawb[:, 0:n], start=True, stop=True),
                  [rotb, rawb], [ps2])
            kb.op("dve", lambda e: e.tensor_tensor(out=t1[:, 0:n], in0=ps_t[:, 0:n], in1=rc[:, 0:n], op=ALU.mult),
                  [ps_t, rc], [t1])
            kb.op("dve", lambda e: e.tensor_tensor(out=t2[:, 0:n], in0=ps2[:, 0:n], in1=rs[:, 0:n], op=ALU.mult),
                  [ps2, rs], [t2])
            kb.op("dve", lambda e: e.tensor_tensor(out=dst_ap, in0=t1[:, 0:n], in1=t2[:, 0:n], op=ALU.add),
                  [t1, t2], [dst_t])

        def proj_fm(wt, wc0, tok0, n, ps_t):
            for kc in range(KC):
                kb.op("pe", lambda e, kc=kc: e.matmul(ps_t[:, 0:n], lhsT=wt[:, kc, wc0:wc0 + 128],
                                                      rhs=hT[:, kc, tok0:tok0 + n], start=(kc == 0),
                                                      stop=(kc == KC - 1)), [wt, hT], [ps_t])

        def rope_ops(ps_t, n, dst_t, dst_ap, lat0):
            ps2 = PB[6]
            return [
                lambda: kb.dma("sp", rc, rc[:, 0:n], st["ropeC_d"], st["ropeC_d"][:, lat0:lat0 + n]),
                lambda: kb.dma("sp", rs, rs[:, 0:n], st["ropeS_d"], st["ropeS_d"][:, lat0:lat0 + n]),
                lambda: kb.op("act", lambda e: e.activation(out=rawb[:, 0:n], in_=ps_t[:, 0:n], func=AF.Identity),
                              [ps_t], [rawb]),
                lambda: kb.op("pe", lambda e: e.matmul(ps2[:, 0:n], lhsT=rotb[:], rhs=rawb[:, 0:n], start=True, stop=True),
                              [rotb, rawb], [ps2]),
                lambda: kb.op("dve", lambda e: e.tensor_tensor(out=t1[:, 0:n], in0=ps_t[:, 0:n], in1=rc[:, 0:n], op=ALU.mult),
                              [ps_t, rc], [t1]),
                lambda: kb.op("dve", lambda e: e.tensor_tensor(out=t2[:, 0:n], in0=ps2[:, 0:n], in1=rs[:, 0:n], op=ALU.mult),
                              [ps2, rs], [t2]),
                lambda: kb.op("dve", lambda e: e.tensor_tensor(out=dst_ap, in0=t1[:, 0:n], in1=t2[:, 0:n], op=ALU.add),
                              [t1, t2], [dst_t]),
            ]

        def q_ops(tok0, nq, is_lat, h, qdst):
            pq = PB[7]
            ops = []
            for kc in range(KC):
                ops.append(lambda kc=kc: kb.op("pe", lambda e: e.matmul(
                    pq[:, 0:nq], lhsT=wq[:, kc, h * 128:(h + 1) * 128], rhs=hT[:, kc, tok0:tok0 + nq],
                    start=(kc == 0), stop=(kc == KC - 1)), [wq, hT], [pq]))
            if is_lat:
                ops += rope_ops(pq, nq, qdst, qdst[:, 0:nq], tok0 - 256)
            else:
                ops.append(lambda: kb.op("act", lambda e: e.activation(out=qdst[:, 0:nq], in_=pq[:, 0:nq], func=AF.Identity),
                                         [pq], [qdst]))
            return ops

        wv = wbuf[0]
        load_w(st, wv, w_in_d, l, O_DAV, 512)
        for i in range(18):
            pv = PB[4 + (i % 2)]
            for kc in range(KC):
                kb.op("pe", lambda e, kc=kc, i=i: e.matmul(pv[:, :], lhsT=hT[:, kc, i * 128:(i + 1) * 128],
                                                           rhs=wv[:, kc, :], start=(kc == 0), stop=(kc == KC - 1)),
                      [wv, hT], [pv])
            for h_ in range(4):
                kb.op("act", lambda e, i=i, h_=h_: e.activation(out=vaug[:, i, h_, 0:128],
                                                         in_=pv[:, h_ * 128:(h_ + 1) * 128], func=AF.Identity), [pv], [vaug])
        wk = wbuf[1]
        load_w(st, wk, w_in_d, l, O_DAK, 512)
        for h in range(4):
            pk = PB[4 + (h % 2)]
            proj_fm(wk, h * 128, 0, 256, pk)
            kb.op("act", lambda e, h=h: e.activation(out=kT[:, h, 0:256], in_=pk[:, 0:256], func=AF.Identity), [pk], [kT])
            for j in range(4):
                pk = PB[4 + (j % 2)]
                proj_fm(wk, h * 128, 256 + 512 * j, 512, pk)
                rope_block(pk, 512, 256 + 512 * j, kT, kT[:, h, 256 + 512 * j:256 + 512 * (j + 1)], 512 * j)
        if "kT" in dbg_d and l == 0:
            for h_ in range(4):
                d_ = st["dbg_d"]["kT"]
                st["finals"].append(kb.dma("sp", d_, d_.ap[:, h_, :], kT, kT[:, h_, :], semt=kT))
        wq = wbuf[0]
        load_w(st, wq, w_in_d, l, O_DAQ, 512)
        qblocks = [(256 + 512 * j, 512, True, list(range(18))) for j in range(4)]
        if need_ctx:
            qblocks.append((0, 256, False, [0, 1]))
        pairs = [(tok0, nq, is_lat, ktiles, h) for (tok0, nq, is_lat, ktiles) in qblocks for h in range(4)]
        for op_ in q_ops(pairs[0][0], pairs[0][1], pairs[0][2], pairs[0][4], qTs[0]):
            op_()
        for pi_, (tok0, nq, is_lat, ktiles, h) in enumerate(pairs):
            nqt = nq // 128
            qT = qTs[pi_ % 2]
            if pi_ + 1 < len(pairs):
                nx = pairs[pi_ + 1]
                pending = q_ops(nx[0], nx[1], nx[2], nx[4], qTs[(pi_ + 1) % 2])
            else:
                pending = []
            if True:
                for m in range(2):
                    def score(ki, m=m, h=h):
                        kt = ktiles[ki]
                        sps = PB[4 + (ki % 2)]
                        kb.op("pe", lambda e: e.matmul(
                            sps[:, 0:nq], lhsT=kT[m * 64:(m + 1) * 64, h, kt * 128:(kt + 1) * 128],
                            rhs=qT[m * 64:(m + 1) * 64, 0:nq], start=True, stop=True), [kT, qT], [sps])
                        pt = pts[ki % 2]
                        kb.op("act", lambda e: e.activation(out=pt[:, 0:nq], in_=sps[:, 0:nq], func=AF.Exp,
                                                            scale=0.125), [sps], [pt])

                    def pv(ki, h=h):
                        kt = ktiles[ki]
                        pt = pts[ki % 2]
                        for qi in range(nqt):
                            kb.op("pe", lambda e, qi=qi: e.matmul(
                                PB[qi][:, 0:129], lhsT=pt[:, qi * 128:(qi + 1) * 128], rhs=vaug[:, kt, h, 0:129],
                                start=(ki == 0), stop=(ki == len(ktiles) - 1)), [pt, vaug], [PB[qi]])

                    score(0)
                    for ki in range(len(ktiles)):
                        if ki + 1 < len(ktiles):
                            score(ki + 1)
                        pv(ki)
                        if pending:
                            pending.pop(0)()
                    for qi in range(nqt):
                        acc = PB[qi]
                        c = m * 4 + qi
                        kb.op("dve", lambda e, acc=acc, c=c: e.reciprocal(out=dst[:, c:c + 1], in_=acc[:, 128:129]),
                              [acc], [dst])
                        if m == 0:
                            kb.op("dve", lambda e, acc=acc, c=c, qi=qi: e.tensor_scalar(
                                out=o0[:, qi, :], in0=acc[:, 0:128], scalar1=dst[:, c:c + 1], scalar2=None,
                                op0=ALU.mult), [acc, dst], [o0])
                        else:
                            kb.op("dve", lambda e, c=c: e.tensor_tensor(out=dst[:, c:c + 1], in0=dst[:, c:c + 1],
                                                                        in1=lams[:, 4:5], op=ALU.mult),
                                  [dst, lams], [dst])
                            kb.op("dve", lambda e, acc=acc, c=c, qi=qi: e.scalar_tensor_tensor(
                                out=o1s[qi][:], in0=acc[:, 0:128], scalar=dst[:, c:c + 1], in1=o0[:, qi, :],
                                op0=ALU.mult, op1=ALU.add), [acc, dst, o0], [o1s[qi]])
                    if m == 1:
                        for qi in range(nqt):
                            kb.op("dve", lambda e, qi=qi: e.tensor_tensor(out=lamj[:], in0=o1s[qi][:], in1=o1s[qi][:],
                                                                          op=ALU.mult), [o1s[qi]], [lamj])
                            kb.op("dve", lambda e, qi=qi: e.reduce_sum(out=dst2[:, qi:qi + 1], in_=lamj[:], axis=AX.X),
                                  [lamj], [dst2])
                        rstd_cols(st, dst2[:, 0:nqt], dst2[:, 8:8 + nqt], 128, dst2[:, 4:4 + nqt], [dst2], [dst2])
                        for qi in range(nqt):
                            kb.op("dve", lambda e, qi=qi, h=h: e.scalar_tensor_tensor(
                                out=ostage[:, qi, h * 128:(h + 1) * 128], in0=o1s[qi][:], scalar=dst2[:, 8 + qi:9 + qi],
                                in1=sg[:], op0=ALU.mult, op1=ALU.mult), [o1s[qi], dst2, sg], [ostage])
            while pending:
                pending.pop(0)()
            if h == 3:
                for qi in range(nqt):
                    kb.dma("sp", st["mix_d"], st["mix_d"].ap[tok0 + qi * 128:tok0 + (qi + 1) * 128, 0:512],
                           ostage, ostage[:, qi, :], semt=ostage)


def out_proj(st, l):
    kb = st["kb"]
    xs, stat, junk, identb, wbuf = st["xs"], st["stat"], st["junk"], st["identb"], st["wbuf"]
    PB = kb.psum_banks
    mix_src = st.get("mix_in", st["mix_d"])
    tiles = list(range(18)) if l < DEPTH - 1 else list(range(2, 18))
    with kb.phase() as ph:
        gts = [ph.sb("gt%d" % r, [128, D]) for r in range(2)]
        mt = ph.sb("mt", [128, D], BF16)
        mT = ph.sb("mT", [128, KC, 128], BF16)
        sq = ph.sb("sq", [128, D])
        for r in range(2):
            kb.dma("sp", gts[r], gts[r][:], st["gates_d"], st["gates_d"].ap[l, 0, r:r + 1, :].partition_broadcast(128))
        for half in range(2):
            load_w(st, wbuf[half], st["w_out_d"], l, half * 512, 512)
        for i in tiles:
            r = 1 if i < 2 else 0
            kb.dma("sp", mt, mt[:], mix_src, mix_src.ap[i * 128:(i + 1) * 128, :])
            pst = PB[6 + (i % 2)]
            pb = pst.ap.bitcast(BF16)
            for kc in range(KC):
                kb.op("pe", lambda e, kc=kc: e.transpose(out=pb[:, kc * 128:(kc + 1) * 128],
                                                          in_=mt[:, kc * 128:(kc + 1) * 128], identity=identb[:]),
                      [mt, identb], [pst])
            kb.op("act", lambda e: e.activation(out=mT[:].rearrange("p c t -> p (c t)"), in_=pb[:, :],
                                                func=AF.Identity), [pst], [mT])
            pss = [PB[2 * (i % 2)], PB[2 * (i % 2) + 1]]
            for half in range(2):
                for kc in range(KC):
                    kb.op("pe", lambda e, kc=kc, half=half: e.matmul(
                        pss[half][:, :], lhsT=mT[:, kc, :], rhs=wbuf[half][:, kc, :], start=(kc == 0),
                        stop=(kc == KC - 1)), [mT, wbuf[half]], [pss[half]])
                kb.op("act", lambda e, half=half: e.activation(out=sq[:, half * 512:(half + 1) * 512],
                                                               in_=pss[half][:, :], func=AF.Square),
                      [pss[half]], [sq])
            kb.op("dve", lambda e: e.reduce_sum(out=stat[:, 56:57], in_=sq[:], axis=AX.X), [sq], [stat])
            rstd_cols(st, stat[:, 56:57], stat[:, 58:59], D, stat[:, 57:58], [stat], [stat])
            for half in range(2):
                kb.op("dve", lambda e, half=half, r=r: e.scalar_tensor_tensor(
                    out=sq[:, half * 512:(half + 1) * 512], in0=pss[half][:, :], scalar=stat[:, 58:59],
                    in1=gts[r][:, half * 512:(half + 1) * 512], op0=ALU.mult, op1=ALU.mult),
                    [pss[half], stat, gts[r]], [sq])
            kb.op("dve", lambda e, i=i: e.tensor_tensor(out=xs[i][:], in0=xs[i][:], in1=sq[:], op=ALU.add),
                  [xs[i], sq], [xs[i]])


def ffn(st, l):
    kb, nc = st["kb"], st["nc"]
    xs, hT, stat = st["xs"], st["hT"], st["stat"]
    PB = kb.psum_banks
    sbs = [(256 + 512 * j, 512, 1 if j > 0 else 0, 1 if j < 3 else 0) for j in range(4)]
    if l < DEPTH - 1:
        sbs.append((0, 256, 0, 0))
    NJ = DFF // 128
    with kb.phase() as ph:
        wdb = [ph.sb("wd%d" % i, [128, D], BF16) for i in range(4)]
        aT = ph.sb("aT", [128, NJ, 512], BF16)
        ur_ = [[ph.sb("ur%d_%d" % (g, p_), [128, 516]) for g in range(2)] for p_ in range(2)]
        cg_ = [[ph.sb("cg%d_%d" % (g, p_), [128, 512]) for g in range(2)] for p_ in range(2)]
        sgt_ = [ph.sb("sgt%d" % p_, [128, 512]) for p_ in range(2)]
        wu = [[ph.sb("wu%d_%d" % (i, g_), [128, KC, 128], BF16) for g_ in range(2)] for i in range(2)]
        cw = ph.sb("cw", [128, 2 * NJ, 3])
        cb = ph.sb("cb", [128, 2 * NJ])
        gts = [ph.sb("fgt%d" % r, [128, D]) for r in range(2)]
        sq = st["junk"]
        kb.dma("sp", cw, cw[:], st["fcw_d"], st["fcw_d"].ap[l])
        kb.dma("sp", cb, cb[:], st["fcb_d"], st["fcb_d"].ap[l])
        for r in range(2):
            kb.dma("sp", gts[r], gts[r][:], st["gates_d"], st["gates_d"].ap[l, 1, r:r + 1, :].partition_broadcast(128))
        wi = 0
        wdi = 0
        for (tok0, n, hl, hr) in sbs:
            w = n + hl + hr
            c0 = tok0 - hl
            blocks = [(0, w // 2), (w // 2, w - w // 2)] if w > 512 else [(0, w)]
            for p_ in range(2):
                for g in range(2):
                    if not hl:
                        kb.op("dve", lambda e, g=g, p_=p_: e.memset(ur_[p_][g][:, 0:1], 0.0), [], [ur_[p_][g]])
                    if not hr:
                        kb.op("dve", lambda e, g=g, n=n, p_=p_: e.memset(ur_[p_][g][:, n + 1:n + 2], 0.0), [], [ur_[p_][g]])
            for j in range(NJ):
                ur, cg, sgt = ur_[j % 2], cg_[j % 2], sgt_[j % 2]
                wt = wu[wi % 2]
                wi += 1
                load_w(st, wt[0], st["wup_d"], l, j * 128, 128, 0)
                load_w(st, wt[1], st["wup_d"], l, DFF + j * 128, 128, 0)
                for g in range(2):
                    for bi, (b0, bw) in enumerate(blocks):
                        pu = PB[4 + ((2 * g + bi) % 4)]
                        for kc in range(KC):
                            kb.op("pe", lambda e, kc=kc, g=g, b0=b0, bw=bw, pu=pu, wt=wt: e.matmul(
                                pu[:, 0:bw], lhsT=wt[g][:, kc, :],
                                rhs=hT[:, kc, c0 + b0:c0 + b0 + bw], start=(kc == 0), stop=(kc == KC - 1)),
                                [wt[g], hT], [pu])
                        o0_ = 1 - hl + b0
                        kb.op("act", lambda e, g=g, pu=pu, bw=bw, o0_=o0_: e.activation(
                            out=ur[g][:, o0_:o0_ + bw], in_=pu[:, 0:bw], func=AF.Identity), [pu], [ur[g]])
                    ch = g * NJ + j
                    kb.op("dve", lambda e, g=g, ch=ch, n=n: e.tensor_scalar(
                        out=cg[g][:, 0:n], in0=ur[g][:, 0:n], scalar1=cw[:, ch, 0:1], scalar2=cb[:, ch:ch + 1],
                        op0=ALU.mult, op1=ALU.add), [ur[g], cw, cb], [cg[g]])
                    for k_ in (1, 2):
                        kb.op("dve", lambda e, g=g, ch=ch, n=n, k_=k_: e.scalar_tensor_tensor(
                            out=cg[g][:, 0:n], in0=ur[g][:, k_:k_ + n], scalar=cw[:, ch, k_:k_ + 1],
                            in1=cg[g][:, 0:n], op0=ALU.mult, op1=ALU.add), [ur[g], cw, cg[g]], [cg[g]])
                kb.op("act", lambda e, n=n: e.activation(out=sgt[:, 0:n], in_=cg[0][:, 0:n], func=AF.Silu),
                      [cg[0]], [sgt])
                kb.op("dve", lambda e, n=n, j=j: e.tensor_tensor(out=aT[:, j, 0:n], in0=sgt[:, 0:n],
                                                                 in1=cg[1][:, 0:n], op=ALU.mult),
                      [sgt, cg[1]], [aT])
            nt = n // 128
            for p0 in range(0, nt, 4):
                ntp = min(4, nt - p0)
                for j in range(NJ):
                    wd = wdb[wdi % 4]
                    wdi += 1
                    kb.dma("pool", wd, wd[:], st["wdn_d"], st["wdn_d"].ap[l, j * 128:(j + 1) * 128, :])
                    for t in range(ntp):
                        for half in range(2):
                            acc = PB[t * 2 + half]
                            kb.op("pe", lambda e, j=j, t=t, half=half, acc=acc, p0=p0, wd=wd: e.matmul(
                                acc[:, :], lhsT=aT[:, j, (p0 + t) * 128:(p0 + t + 1) * 128],
                                rhs=wd[:, half * 512:(half + 1) * 512], start=(j == 0), stop=(j == NJ - 1)),
                                [aT, wd], [acc])
                for t in range(ntp):
                    i = tok0 // 128 + p0 + t
                    r = 1 if i < 2 else 0
                    pss = [PB[t * 2], PB[t * 2 + 1]]
                    for half in range(2):
                        kb.op("act", lambda e, half=half, pss=pss: e.activation(
                            out=sq[:, half * 512:(half + 1) * 512], in_=pss[half][:, :], func=AF.Square),
                            [pss[half]], [sq])
                    kb.op("dve", lambda e: e.reduce_sum(out=stat[:, 56:57], in_=sq[:], axis=AX.X), [sq], [stat])
                    rstd_cols(st, stat[:, 56:57], stat[:, 58:59], D, stat[:, 57:58], [stat], [stat])
                    for half in range(2):
                        kb.op("dve", lambda e, half=half, r=r, pss=pss: e.scalar_tensor_tensor(
                            out=sq[:, half * 512:(half + 1) * 512], in0=pss[half][:, :], scalar=stat[:, 58:59],
                            in1=gts[r][:, half * 512:(half + 1) * 512], op0=ALU.mult, op1=ALU.mult),
                            [pss[half], stat, gts[r]], [sq])
                    kb.op("dve", lambda e, i=i: e.tensor_tensor(out=xs[i][:], in0=xs[i][:], in1=sq[:], op=ALU.add),
                          [xs[i], sq], [xs[i]])


def proj_fm_g(st, wt, wc0, tok0, n, ps_t, ncols=128):
    kb, hT = st["kb"], st["hT"]
    for kc in range(KC):
        kb.op("pe", lambda e, kc=kc: e.matmul(ps_t[0:ncols, 0:n], lhsT=wt[:, kc, wc0:wc0 + ncols],
                                              rhs=hT[:, kc, tok0:tok0 + n], start=(kc == 0),
                                              stop=(kc == KC - 1)), [wt, hT], [ps_t])


def store_fm_to_mix(st, ph_tiles, res, tok0, n, col0):
    kb = st["kb"]
    PB = kb.psum_banks
    stage = ph_tiles["stage"]
    identb = st["identb"]
    for ti in range(n // 128):
        pst = PB[2 + (ti % 2)]
        pb = pst.ap.bitcast(BF16)
        kb.op("pe", lambda e, ti=ti: e.transpose(out=pb[:, 0:128], in_=res[:, ti * 128:(ti + 1) * 128],
                                                  identity=identb[:]), [res, identb], [pst])
        kb.op("act", lambda e, ti=ti: e.activation(out=stage[:, ti, :], in_=pb[:, 0:128], func=AF.Identity),
              [pst], [stage])
    for ti in range(n // 128):
        kb.dma("sp", st["mix_d"], st["mix_d"].ap[tok0 + ti * 128:tok0 + (ti + 1) * 128, col0:col0 + 128],
               stage, stage[:, ti, :], semt=stage)


def hg_mixer(st, l):
    kb, nc = st["kb"], st["nc"]
    hT, wbuf, w_in_d = st["hT"], st["wbuf"], st["w_in_d"]
    PB = kb.psum_banks
    C = 32
    blocks = [(0, 256)] + [(256 + 512 * j, 512) for j in range(4)]
    with kb.phase() as ph:
        lbl = ph.sb("lbl", [128, DEPTH, 4])
        lbs = ph.sb("lbs", [128, 8, 4])
        ngc_h = ph.sb("ngc_h", [128, DEPTH])
        OT = ph.sb("OT", [128, 2, NT])
        wqg = ph.sb("wqg", [128, KC, 512], BF16)
        wv = wbuf[0]
        wf = wbuf[1]
        E = ph.sb("hE", [128, 512])
        SG = ph.sb("hSG", [128, 512])
        LF = ph.sb("hLF", [128, 512])
        KT = ph.sb("hKT", [128, 512])
        Bb = ph.sb("hB", [128, 512])
        Bd = ph.sb("hBd", [128, 512])
        EX = ph.sb("hEX", [128, 512])
        QS = ph.sb("hQS", [128, 512])
        eBt = [ph.sb("heBt%d" % i, [128, 16]) for i in range(4)]
        qt = [ph.sb("hqt%d" % i, [128, 512], BF16) for i in range(4)]
        kt = [ph.sb("hkt%d" % i, [128, 512], BF16) for i in range(4)]
        kh = [ph.sb("hkh%d" % i, [128, 512], BF16) for i in range(4)]
        vT = [ph.sb("hvT%d" % i, [128, 512], BF16) for i in range(4)]
        S = [ph.sb("hS%d" % i, [128, 128]) for i in range(4)]
        Sb = [ph.sb("hSb%d" % i, [128, 128], BF16) for i in range(4)]
        tmpS = [ph.sb("htS%d" % i, [128, 128]) for i in range(4)]
        sc = [ph.sb("hsc%d" % i, [32, 64], BF16) for i in range(4)]
        khat = [ph.sb("hkhat%d" % i, [32, 128], BF16) for i in range(4)]
        vch = [ph.sb("hvch%d" % i, [32, 128], BF16) for i in range(4)]
        vm0 = [ph.sb("hvm0%d" % i, [32, 128], BF16) for i in range(4)]
        vm1 = [ph.sb("hvm1%d" % i, [32, 128], BF16) for i in range(4)]
        RES = ph.sb("hRES", [128, 512], BF16)
        stage = T(st["xn"].ap[:, 0:512].rearrange("p (a b) -> p a b", a=4), "hstage")
        cm = st["cmask"]
        bdm, bdb = st["bdm"], st["bdb"]
        mk = st["hgmask"]
        kb.dma("sp", lbl, lbl[:], st["hglb_d"], st["hglb_d"][:])
        kb.dma("sp", ngc_h, ngc_h[:], st["hgng_d"], st["hgng_d"][:])
        kb.op("act", lambda e: e.activation(out=lbl[:], in_=lbl[:], func=AF.Exp), [lbl], [lbl])
        kb.op("dve", lambda e: e.tensor_copy(out=lbs[:, 0, :], in_=lbl[:, 0, :]), [lbl], [lbs])
        for j in range(1, DEPTH):
            kb.op("dve", lambda e, j=j: e.tensor_tensor(out=lbs[:, 0, :], in0=lbs[:, 0, :], in1=lbl[:, j, :],
                                                        op=ALU.add), [lbs, lbl], [lbs])
        kb.op("dve", lambda e: e.reciprocal(out=lbs[:, 1, :], in_=lbs[:, 0, :]), [lbs], [lbs])
        kb.op("dve", lambda e: e.memset(lbs[:, 2, :], 0.0), [], [lbs])
        for j in range(1, l + 1):
            kb.op("dve", lambda e, j=j: e.tensor_tensor(out=lbs[:, 2, :], in0=lbs[:, 2, :], in1=lbl[:, j, :],
                                                        op=ALU.add), [lbs, lbl], [lbs])
        kb.op("dve", lambda e: e.tensor_tensor(out=lbs[:, 3, :], in0=lbs[:, 2, :], in1=lbs[:, 1, :], op=ALU.mult),
              [lbs], [lbs])
        kb.op("dve", lambda e: e.tensor_scalar(out=lbs[:, 4, :], in0=lbs[:, 3, :], scalar1=-1.0, scalar2=1.0,
                                               op0=ALU.mult, op1=ALU.add), [lbs], [lbs])
        kb.op("dve", lambda e: e.tensor_scalar(out=lbs[:, 5, :], in0=lbs[:, 3, :], scalar1=-1.0, scalar2=None,
                                               op0=ALU.add), [lbs], [lbs])
        kb.op("dve", lambda e: e.tensor_scalar(out=lbs[:, 6, :], in0=lbs[:, 3, :], scalar1=1e-30, scalar2=None,
                                               op0=ALU.max), [lbs], [lbs])
        load_w(st, wqg, w_in_d, l, O_HGQ, 256, 0)
        load_w(st, wqg, w_in_d, l, O_HGG, 256, 256)
        load_w(st, wv, w_in_d, l, O_HGI, 256, 0)
        load_w(st, wf, w_in_d, l, O_HGF, 512, 0)
        rot = [0]

        def psr():
            t = PB[4 + (rot[0] % 4)]
            rot[0] += 1
            return t

        for q in range(4):
            kb.op("dve", lambda e, q=q: e.memset(S[q][:], 0.0), [], [S[q]])
            kb.op("dve", lambda e, q=q: e.memset(Sb[q][:], 0.0), [], [Sb[q]])
            kb.op("dve", lambda e, q=q: e.memset(vm0[q][:], 0.0), [], [vm0[q]])
            kb.op("dve", lambda e, q=q: e.memset(vm1[q][:], 0.0), [], [vm1[q]])
        borders = [blocks, [blocks[0]] + blocks[:0:-1]]
        written = set()
        for bi in range(len(blocks)):
            for dr in range(2):
                tok0, n = borders[dr][bi]
                nch = n // C
                for hp in range(2):
                    q = dr * 2 + hp
                    c = dr * 2 + hp
                    pz = psr()
                    proj_fm_g(st, wf, c * 128, tok0, n, pz)
                    kb.op("act", lambda e, pz=pz: e.activation(out=E[:, 0:n], in_=pz[:, 0:n], func=AF.Exp, scale=-1.0),
                          [pz], [E])
                    kb.op("dve", lambda e: e.tensor_scalar(out=SG[:, 0:n], in0=E[:, 0:n], scalar1=1.0, scalar2=None,
                                                           op0=ALU.add), [E], [SG])
                    kb.op("dve", lambda e: e.reciprocal(out=SG[:, 0:n], in_=SG[:, 0:n]), [SG], [SG])
                    kb.op("dve", lambda e, c=c: e.tensor_scalar(out=E[:, 0:n], in0=SG[:, 0:n], scalar1=lbs[:, 4, c:c + 1],
                                                                scalar2=lbs[:, 6, c:c + 1], op0=ALU.mult, op1=ALU.add),
                          [SG, lbs], [E])
                    kb.op("act", lambda e: e.activation(out=LF[:, 0:n], in_=E[:, 0:n], func=AF.Ln), [E], [LF])
                    kb.op("dve", lambda e, c=c: e.tensor_scalar(out=KT[:, 0:n], in0=SG[:, 0:n], scalar1=-1.0,
                                                                scalar2=lbs[:, 5, c:c + 1], op0=ALU.add, op1=ALU.mult),
                          [SG, lbs], [KT])
                    kb.op("dve", lambda e: e.tensor_tensor_scan(out=Bb[:, 0:n], data0=cm[:, 0:n], data1=LF[:, 0:n],
                                                                initial=0.0, op0=ALU.mult, op1=ALU.add),
                          [cm, LF], [Bb])
                    B3 = Bb[:, 0:n].rearrange("p (c k) -> p c k", k=C)
                    Bd3 = Bd[:, 0:n].rearrange("p (c k) -> p c k", k=C)
                    LF3 = LF[:, 0:n].rearrange("p (c k) -> p c k", k=C)
                    btb = B3[:, :, C - 1:C].to_broadcast([128, nch, C])
                    if dr == 1:
                        kb.op("dve", lambda e, btb=btb, B3=B3, Bd3=Bd3: e.tensor_tensor(out=Bd3, in0=btb, in1=B3,
                                                                                      op=ALU.subtract), [Bb], [Bd])
                        kb.op("dve", lambda e, Bd3=Bd3, LF3=LF3: e.tensor_tensor(out=Bd3, in0=Bd3, in1=LF3, op=ALU.add),
                              [Bd, LF], [Bd])
                    else:
                        kb.op("dve", lambda e: e.tensor_copy(out=Bd[:, 0:n], in_=Bb[:, 0:n]), [Bb], [Bd])
                    kb.op("act", lambda e, hp=hp, q=q, B3=B3: e.activation(
                        out=eBt[q][:, 0:nch], in_=B3[:, :, C - 1], func=AF.Exp), [Bb], [eBt[q]])
                    pvv = psr()
                    proj_fm_g(st, wv, hp * 128, tok0, n, pvv)
                    kb.op("act", lambda e, pvv=pvv, hp=hp, q=q: e.activation(out=vT[q][:, 0:n], in_=pvv[:, 0:n],
                                                                        func=AF.Identity), [pvv], [vT[q]])
                    pq = psr()
                    proj_fm_g(st, wqg, hp * 128, tok0, n, pq)
                    kb.op("act", lambda e, pq=pq: e.activation(out=QS[:, 0:n], in_=pq[:, 0:n], func=AF.Silu), [pq], [QS])
                    kb.op("act", lambda e: e.activation(out=EX[:, 0:n], in_=Bd[:, 0:n], func=AF.Exp), [Bd], [EX])
                    kb.op("dve", lambda e, hp=hp, q=q: e.tensor_tensor(out=qt[q][:, 0:n], in0=QS[:, 0:n], in1=EX[:, 0:n],
                                                                  op=ALU.mult), [QS, EX], [qt[q]])
                    kb.op("act", lambda e: e.activation(out=EX[:, 0:n], in_=Bd[:, 0:n], func=AF.Exp, scale=-1.0),
                          [Bd], [EX])
                    kb.op("dve", lambda e, hp=hp, q=q: e.tensor_tensor(out=kt[q][:, 0:n], in0=KT[:, 0:n], in1=EX[:, 0:n],
                                                                  op=ALU.mult), [KT, EX], [kt[q]])
                    kb.op("dve", lambda e, btb=btb, Bd3=Bd3: e.tensor_tensor(out=Bd3, in0=btb, in1=Bd3, op=ALU.subtract),
                          [Bb, Bd], [Bd])
                    kb.op("act", lambda e: e.activation(out=EX[:, 0:n], in_=Bd[:, 0:n], func=AF.Exp), [Bd], [EX])
                    kb.op("dve", lambda e, hp=hp, q=q: e.tensor_tensor(out=kh[q][:, 0:n], in0=KT[:, 0:n], in1=EX[:, 0:n],
                                                                  op=ALU.mult), [KT, EX], [kh[q]])
            nch = borders[0][bi][1] // C
            for ci in range(nch):
                for dr in range(2):
                    tok0, n = borders[dr][bi]
                    ck = ci if dr == 0 else nch - 1 - ci
                    c0 = ck * C
                    for hp in range(2):
                        q = dr * 2 + hp
                        pss_ = psr()
                        for hh in range(2):
                            kb.op("pe", lambda e, hh=hh, hp=hp, q=q, pss_=pss_: e.matmul(
                                pss_[0:32, hh * 32:(hh + 1) * 32], lhsT=kt[q][hh * 64:(hh + 1) * 64, c0:c0 + C],
                                rhs=qt[q][hh * 64:(hh + 1) * 64, c0:c0 + C], start=True, stop=True),
                                [kt[q], qt[q]], [pss_], pe_self=(hh == 1))
                        kb.op("dve", lambda e, hp=hp, q=q, pss_=pss_: e.tensor_tensor(
                            out=sc[q][:, :], in0=pss_[0:32, 0:64], in1=mk[:, dr, :], op=ALU.mult), [pss_, mk], [sc[q]])
                        psk = psr()
                        pkb = psk.ap.bitcast(BF16)
                        kb.op("pe", lambda e, hp=hp, q=q, pkb=pkb: e.transpose(out=pkb[0:32, 0:128], in_=kh[q][:, c0:c0 + C],
                                                                          identity=st["identb"][:]),
                              [kh[q], st["identb"]], [psk])
                        kb.op("act", lambda e, hp=hp, q=q, pkb=pkb: e.activation(out=khat[q][:, :], in_=pkb[0:32, 0:128],
                                                                            func=AF.Identity), [psk], [khat[q]])
                        psv = psr()
                        pvb = psv.ap.bitcast(BF16)
                        kb.op("pe", lambda e, hp=hp, q=q, pvb=pvb: e.transpose(out=pvb[0:32, 0:128], in_=vT[q][:, c0:c0 + C],
                                                                          identity=st["identb"][:]),
                              [vT[q], st["identb"]], [psv])
                        kb.op("act", lambda e, hp=hp, q=q, pvb=pvb: e.activation(out=vch[q][:, :], in_=pvb[0:32, 0:128],
                                                                            func=AF.Identity), [psv], [vch[q]])
                        kb.op("dve", lambda e, hp=hp, q=q: e.tensor_copy(out=vm0[q][:, 0:64], in_=vch[q][:, 0:64]),
                              [vch[q]], [vm0[q]])
                        kb.op("dve", lambda e, hp=hp, q=q: e.tensor_copy(out=vm1[q][:, 64:128], in_=vch[q][:, 64:128]),
                              [vch[q]], [vm1[q]])
                        po = PB[q]
                        kb.op("pe", lambda e, hp=hp, q=q, po=po: e.matmul(po[:, c0:c0 + C], lhsT=vm0[q][:, :],
                                                                     rhs=sc[q][:, 0:32], start=True, stop=False),
                              [vm0[q], sc[q]], [po])
                        kb.op("pe", lambda e, hp=hp, q=q, po=po: e.matmul(po[:, c0:c0 + C], lhsT=vm1[q][:, :],
                                                                     rhs=sc[q][:, 32:64], start=False, stop=False),
                              [vm1[q], sc[q]], [po])
                        kb.op("pe", lambda e, hp=hp, q=q, po=po: e.matmul(po[:, c0:c0 + C], lhsT=Sb[q][:, :],
                                                                     rhs=qt[q][:, c0:c0 + C], start=False, stop=True),
                              [Sb[q], qt[q]], [po])
                        pst_ = psr()
                        kb.op("pe", lambda e, hp=hp, q=q, pst_=pst_: e.matmul(pst_[:, 0:128], lhsT=khat[q][:, :],
                                                                         rhs=vch[q][:, :], start=True, stop=True),
                              [khat[q], vch[q]], [pst_])
                        kb.op("dve", lambda e, hp=hp, q=q, pst_=pst_: e.tensor_tensor(
                            out=tmpS[q][:, :], in0=pst_[:, 0:128], in1=bdm[:, :], op=ALU.mult), [pst_, bdm], [tmpS[q]])
                        kb.op("dve", lambda e, hp=hp, q=q, ck=ck: e.scalar_tensor_tensor(
                            out=S[q][:, :], in0=S[q][:, :], scalar=eBt[q][:, ck:ck + 1], in1=tmpS[q][:, :],
                            op0=ALU.mult, op1=ALU.add), [S[q], eBt[q], tmpS[q]], [S[q]])
                        kb.op("act", lambda e, hp=hp, q=q: e.activation(out=Sb[q][:, :], in_=S[q][:, :], func=AF.Identity),
                              [S[q]], [Sb[q]])

            for dr in range(2):
                tok0, n = borders[dr][bi]
                for hp in range(2):
                    po = PB[dr * 2 + hp]
                    if (hp, tok0) not in written:
                        written.add((hp, tok0))
                        kb.op("act", lambda e, hp=hp, po=po, tok0=tok0, n=n: e.activation(
                            out=OT[:, hp, tok0:tok0 + n], in_=po[:, 0:n], func=AF.Identity), [po], [OT])
                    else:
                        kb.op("dve", lambda e, hp=hp, po=po, tok0=tok0, n=n: e.tensor_tensor(
                            out=OT[:, hp, tok0:tok0 + n], in0=po[:, 0:n], in1=OT[:, hp, tok0:tok0 + n], op=ALU.add),
                            [po, OT], [OT])
        for (tok0, n) in blocks:
            for hp in range(2):
                kb.op("act", lambda e, hp=hp: e.activation(out=RES[:, 0:n], in_=OT[:, hp, tok0:tok0 + n],
                                                           func=AF.Square), [OT], [RES])
                pn = psr()
                kb.op("pe", lambda e, pn=pn: e.matmul(pn[:, 0:n], lhsT=bdb[:, :], rhs=RES[:, 0:n], start=True, stop=True),
                      [bdb, RES], [pn])
                kb.op("act", lambda e, pn=pn: e.activation(out=E[:, 0:n], in_=pn[:, 0:n], func=AF.Ln, scale=1.0 / 64,
                                                           bias=st["epst"][:, :]), [pn, st["epst"]], [E])
                kb.op("act", lambda e: e.activation(out=E[:, 0:n], in_=E[:, 0:n], func=AF.Exp, scale=-0.5), [E], [E])
                pg = psr()
                proj_fm_g(st, wqg, 256 + hp * 128, tok0, n, pg)
                kb.op("act", lambda e, pg=pg: e.activation(out=QS[:, 0:n], in_=pg[:, 0:n], func=AF.Silu), [pg], [QS])
                kb.op("dve", lambda e, hp=hp: e.tensor_tensor(out=E[:, 0:n], in0=E[:, 0:n], in1=OT[:, hp, tok0:tok0 + n],
                                                              op=ALU.mult), [E, OT], [E])
                kb.op("dve", lambda e: e.scalar_tensor_tensor(out=RES[:, 0:n], in0=E[:, 0:n], scalar=ngc_h[:, l:l + 1],
                                                              in1=QS[:, 0:n], op0=ALU.mult, op1=ALU.mult),
                      [E, ngc_h, QS], [RES])
                store_fm_to_mix(st, dict(stage=stage), RES, tok0, n, 512 + hp * 128)


def gd_mixer(st, l):
    kb, nc = st["kb"], st["nc"]
    hT, wbuf, w_in_d, ident = st["hT"], st["wbuf"], st["w_in_d"], st["ident"]
    PB = kb.psum_banks
    C = 64
    NB = 256
    blocks = [(0, 256, 0, 0)] + [(256 + NB * j, NB, 1 if j > 0 else 0, 1 if j < 7 else 0) for j in range(8)]
    mL, mU, mLI, mUI, ones64 = st["gm_L"], st["gm_U"], st["gm_LI"], st["gm_UI"], st["gm_ones"]
    I64 = ident[0:64, 0:64]
    with kb.phase() as ph:
        wqk = wbuf[0]
        wvab = wbuf[1]
        wg = ph.sb("gwg", [128, KC, 256], BF16)
        obuf = ph.sb("gobuf", [64, 4, NB])
        obuf2 = ph.sb("gobuf2", [64, 4, NB])
        ogd_d = st["ogd_d"]
        cwc = ph.sb("gcw", [64, DEPTH, 12, 3])
        ngc_g = ph.sb("gng", [64, DEPTH])
        dtb = ph.sb("gdtb", [64, 8])
        nexpA = ph.sb("gnexpA", [64, 8])
        ones1 = ph.sb("gones1", [64, 1])
        ur = ph.sb("gur", [64, NB + 4])
        cg = ph.sb("gcg", [64, NB])
        sqb = ph.sb("gsq", [64, NB])
        rsd = ph.sb("grsd", [64, NB])
        QT = [ph.sb("gQT%d" % h, [64, NB]) for h in range(4)]
        KT = [ph.sb("gKT%d" % h, [64, NB]) for h in range(4)]
        VT = [ph.sb("gVT%d" % h, [64, NB]) for h in range(4)]
        RES = [ph.sb("gRES%d" % h, [64, NB], BF16) for h in range(4)]
        stage = ph.sb("gstage", [128, 256], BF16)
        gab = ph.sb("gab", [64, 16])
        gx = ph.sb("ggx", [64, 8])
        la_all = ph.sb("gla", [64, 8])
        be_all = ph.sb("gbe", [64, 8])
        names = ["Lm", "LM2", "DT", "DTs", "N", "NT", "P0", "P1", "PT0", "PT1", "R0", "R1", "Z", "vn", "SC", "QgT",
                 "Khat", "Ktok", "Vtok", "EG"]
        tl = [{nm: ph.sb("g%s%d" % (nm, h), [64, 64]) for nm in names} for h in range(4)]
        gsb = [ph.sb("ggsb%d" % h, [64, 8]) for h in range(4)]
        S = [ph.sb("gS%d" % h, [64, 64]) for h in range(4)]
        kb.dma("sp", cwc, cwc[:], st["gdcw_d"], st["gdcw_d"][:])
        kb.dma("sp", ngc_g, ngc_g[:], st["gdng_d"], st["gdng_d"][:])
        kb.dma("sp", dtb, dtb[:], st["gddt_d"], st["gddt_d"].ap[l:l + 1, :].partition_broadcast(64))
        kb.dma("sp", nexpA, nexpA[:], st["gdal_d"], st["gdal_d"].ap[l:l + 1, :].partition_broadcast(64))
        kb.op("act", lambda e: e.activation(out=nexpA[:], in_=nexpA[:], func=AF.Exp), [nexpA], [nexpA])
        kb.op("dve", lambda e: e.tensor_scalar(out=nexpA[:], in0=nexpA[:], scalar1=-1.0, scalar2=None, op0=ALU.mult),
              [nexpA], [nexpA])
        kb.op("dve", lambda e: e.memset(ones1[:], 1.0), [], [ones1])
        load_w(st, wqk, w_in_d, l, O_GDQKV, 512, 0)
        load_w(st, wvab, w_in_d, l, O_GDQKV + 512, 256, 0)
        load_w(st, wvab, w_in_d, l, O_GDA, 16, 256)
        load_w(st, wg, w_in_d, l, O_GDG, 256, 0)
        rot = [0]

        def psr():
            t = PB[2 + (rot[0] % 6)]
            rot[0] += 1
            return t

        def precompute_block(tok0, n, hl, hr):
            w = n + hl + hr
            c0 = tok0 - hl
            if not hl:
                kb.op("dve", lambda e: e.memset(ur[:, 0:1], 0.0), [], [ur])
            if not hr:
                kb.op("dve", lambda e: e.memset(ur[:, n + 1:n + 2], 0.0), [], [ur])
            for ty in range(3):
                for h in range(4):
                    wt = wqk if ty < 2 else wvab
                    wc0 = ty * 256 + h * 64 if ty < 2 else h * 64
                    pu = psr()
                    proj_fm_g(st, wt, wc0, c0, w, pu, ncols=64)
                    o0_ = 1 - hl
                    kb.op("act", lambda e, pu=pu, o0_=o0_: e.activation(out=ur[:, o0_:o0_ + w], in_=pu[0:64, 0:w],
                                                                        func=AF.Identity), [pu], [ur])
                    ch = ty * 4 + h
                    kb.op("dve", lambda e, ch=ch: e.tensor_scalar(out=cg[:, 0:n], in0=ur[:, 0:n],
                                                                  scalar1=cwc[:, l, ch, 0:1], scalar2=None, op0=ALU.mult),
                          [ur, cwc], [cg])
                    for k_ in (1, 2):
                        kb.op("dve", lambda e, ch=ch, k_=k_: e.scalar_tensor_tensor(
                            out=cg[:, 0:n], in0=ur[:, k_:k_ + n], scalar=cwc[:, l, ch, k_:k_ + 1], in1=cg[:, 0:n],
                            op0=ALU.mult, op1=ALU.add), [ur, cwc, cg], [cg])
                    dstt = (QT, KT, VT)[ty][h]
                    if ty == 2:
                        kb.op("act", lambda e, dstt=dstt: e.activation(out=dstt[:, 0:n], in_=cg[:, 0:n], func=AF.Silu),
                              [cg], [dstt])
                        continue
                    kb.op("act", lambda e: e.activation(out=cg[:, 0:n], in_=cg[:, 0:n], func=AF.Silu), [cg], [cg])
                    kb.op("act", lambda e: e.activation(out=sqb[:, 0:n], in_=cg[:, 0:n], func=AF.Square), [cg], [sqb])
                    pn = psr()
                    kb.op("pe", lambda e, pn=pn: e.matmul(pn[0:64, 0:n], lhsT=ones64[:, :], rhs=sqb[:, 0:n],
                                                          start=True, stop=True), [ones64, sqb], [pn])
                    kb.op("act", lambda e, pn=pn: e.activation(out=rsd[:, 0:n], in_=pn[0:64, 0:n], func=AF.Ln,
                                                               bias=st["epst"][0:64, :]), [pn, st["epst"]], [rsd])
                    kb.op("act", lambda e: e.activation(out=rsd[:, 0:n], in_=rsd[:, 0:n], func=AF.Exp, scale=-0.5),
                          [rsd], [rsd])
                    sc_ = 0.125 if ty == 0 else 1.0
                    kb.op("dve", lambda e, dstt=dstt, sc_=sc_: e.scalar_tensor_tensor(
                        out=dstt[:, 0:n], in0=cg[:, 0:n], scalar=sc_, in1=rsd[:, 0:n], op0=ALU.mult, op1=ALU.mult),
                        [cg, rsd], [dstt])

        def finish_block(tok0, n):
            for h in range(4):
                kb.op("act", lambda e, h=h: e.activation(out=sqb[:, 0:n], in_=obuf2[:, h, 0:n], func=AF.Square),
                      [obuf2], [sqb])
                pn = psr()
                kb.op("pe", lambda e, pn=pn: e.matmul(pn[0:64, 0:n], lhsT=ones64[:, :], rhs=sqb[:, 0:n], start=True,
                                                      stop=True), [ones64, sqb], [pn])
                kb.op("act", lambda e, pn=pn: e.activation(out=rsd[:, 0:n], in_=pn[0:64, 0:n], func=AF.Ln, scale=1.0 / 64,
                                                           bias=st["epst"][0:64, :]), [pn, st["epst"]], [rsd])
                kb.op("act", lambda e: e.activation(out=rsd[:, 0:n], in_=rsd[:, 0:n], func=AF.Exp, scale=-0.5),
                      [rsd], [rsd])
                pg = psr()
                proj_fm_g(st, wg, h * 64, tok0, n, pg, ncols=64)
                kb.op("act", lambda e, pg=pg: e.activation(out=cg[:, 0:n], in_=pg[0:64, 0:n], func=AF.Silu), [pg], [cg])
                kb.op("dve", lambda e, h=h: e.tensor_tensor(out=rsd[:, 0:n], in0=rsd[:, 0:n], in1=obuf2[:, h, 0:n],
                                                            op=ALU.mult), [rsd, obuf2], [rsd])
                kb.op("dve", lambda e, h=h: e.scalar_tensor_tensor(out=RES[h][:, 0:n], in0=rsd[:, 0:n],
                                                                   scalar=ngc_g[:, l:l + 1], in1=cg[:, 0:n],
                                                                   op0=ALU.mult, op1=ALU.mult), [rsd, ngc_g, cg], [RES[h]])
            for ti in range(n // 128):
                pst = psr()
                pb = pst.ap.bitcast(BF16)
                for h in range(4):
                    kb.op("pe", lambda e, h=h, ti=ti, pb=pb: e.transpose(
                        out=pb[:, h * 64:(h + 1) * 64], in_=RES[h][:, ti * 128:(ti + 1) * 128],
                        identity=st["identb"][0:64, 0:64]), [RES[h], st["identb"]], [pst])
                kb.op("act", lambda e, pb=pb: e.activation(out=stage[:, :], in_=pb[:, 0:256], func=AF.Identity),
                      [pst], [stage])
                kb.dma("sp", st["mix_d"], st["mix_d"].ap[tok0 + ti * 128:tok0 + (ti + 1) * 128, 768:1024],
                       stage, stage[:, :], semt=stage)

        for dr in range(2):
            M1 = mL if dr == 0 else mU
            M2 = mUI if dr == 0 else mLI
            M3 = mU if dr == 0 else mL
            for h in range(4):
                kb.op("dve", lambda e, h=h: e.memset(S[h][:], 0.0), [], [S[h]])
            border = blocks if dr == 0 else [blocks[0]] + blocks[:0:-1]
            for (tok0, n, hl, hr) in border:
                precompute_block(tok0, n, hl, hr)
                nch = n // C
                corder = range(nch) if dr == 0 else range(nch - 1, -1, -1)
                for ck in corder:
                    c0 = ck * C
                    pab = psr()
                    for kc in range(KC):
                        kb.op("pe", lambda e, kc=kc, pab=pab: e.matmul(
                            pab[0:64, 0:16], lhsT=hT[:, kc, tok0 + c0:tok0 + c0 + C], rhs=wvab[:, kc, 256:272],
                            start=(kc == 0), stop=(kc == KC - 1)), [hT, wvab], [pab])
                    kb.op("act", lambda e, pab=pab: e.activation(out=gab[:, :], in_=pab[0:64, 0:16], func=AF.Identity),
                          [pab], [gab])
                    kb.op("dve", lambda e: e.tensor_tensor(out=gx[:, :], in0=gab[:, 0:8], in1=dtb[:, :], op=ALU.add),
                          [gab, dtb], [gx])
                    kb.op("act", lambda e: e.activation(out=gx[:, :], in_=gx[:, :], func=AF.Exp), [gx], [gx])
                    kb.op("act", lambda e: e.activation(out=gx[:, :], in_=gx[:, :], func=AF.Ln, bias=ones1[:, :]),
                          [gx, ones1], [gx])
                    kb.op("dve", lambda e: e.tensor_tensor(out=la_all[:, :], in0=gx[:, :], in1=nexpA[:, :], op=ALU.mult),
                          [gx, nexpA], [la_all])
                    kb.op("act", lambda e: e.activation(out=be_all[:, :], in_=gab[:, 8:16], func=AF.Exp, scale=-1.0),
                          [gab], [be_all])
                    kb.op("dve", lambda e: e.tensor_scalar(out=be_all[:, :], in0=be_all[:, :], scalar1=1.0, scalar2=None,
                                                           op0=ALU.add), [be_all], [be_all])
                    kb.op("dve", lambda e: e.reciprocal(out=be_all[:, :], in_=be_all[:, :]), [be_all], [be_all])
                    for h in range(4):
                        t = tl[h]
                        g = gsb[h]
                        cidx = dr * 4 + h
                        la = la_all[:, cidx:cidx + 1]
                        be = be_all[:, cidx:cidx + 1]
                        KTc = KT[h][:, c0:c0 + C]
                        QTc = QT[h][:, c0:c0 + C]
                        VTc = VT[h][:, c0:c0 + C]
                        for src, dn in ((KTc, "Ktok"), (VTc, "Vtok")):
                            pt_ = psr()
                            kb.op("pe", lambda e, pt_=pt_, src=src: e.transpose(out=pt_[0:64, 0:64], in_=src, identity=I64),
                                  [KT[h], VT[h], ident], [pt_])
                            kb.op("act", lambda e, pt_=pt_, dn=dn, t=t: e.activation(out=t[dn][:, :], in_=pt_[0:64, 0:64],
                                                                                 func=AF.Identity), [pt_], [t[dn]])
                        kb.op("dve", lambda e, t=t, la=la: e.tensor_scalar(out=t["Lm"][:, :], in0=M1[:, :], scalar1=la,
                                                                           scalar2=None, op0=ALU.mult),
                              [M1, la_all], [t["Lm"]])
                        kb.op("dve", lambda e, t=t, la=la: e.tensor_scalar(out=t["LM2"][:, :], in0=M2[:, :], scalar1=la,
                                                                           scalar2=None, op0=ALU.mult),
                              [M2, la_all], [t["LM2"]])
                        pd = psr()
                        kb.op("pe", lambda e, pd=pd, t=t: e.matmul(pd[0:64, 0:64], lhsT=t["Lm"][:, :], rhs=M2[:, :],
                                                                   start=True, stop=True), [t["Lm"], M2], [pd])
                        kb.op("pe", lambda e, pd=pd, t=t: e.matmul(pd[0:64, 64:128], lhsT=ones64[:, :], rhs=t["LM2"][:, :],
                                                                   start=True, stop=True), [t["LM2"], ones64], [pd])
                        kb.op("pe", lambda e, pd=pd, la=la: e.matmul(pd[0:64, 128:129], lhsT=M2[:, :], rhs=la,
                                                                     start=True, stop=True), [M2, la_all], [pd])
                        kb.op("pe", lambda e, pd=pd, la=la: e.matmul(pd[0:64, 129:130], lhsT=ones64[:, :], rhs=la,
                                                                     start=True, stop=True), [ones64, la_all], [pd])
                        kb.op("act", lambda e, pd=pd, t=t: e.activation(out=t["DT"][:, :], in_=pd[0:64, 0:64], func=AF.Exp),
                              [pd], [t["DT"]])
                        kb.op("act", lambda e, pd=pd, t=t: e.activation(out=t["EG"][:, :], in_=pd[0:64, 64:128],
                                                                        func=AF.Exp), [pd], [t["EG"]])
                        kb.op("act", lambda e, pd=pd, g=g: e.activation(out=g[:, 0:2], in_=pd[0:64, 128:130],
                                                                        func=AF.Identity), [pd], [g])
                        kb.op("dve", lambda e, t=t: e.tensor_tensor(out=t["DTs"][:, :], in0=t["DT"][:, :], in1=M3[:, :],
                                                                    op=ALU.mult), [t["DT"], M3], [t["DTs"]])
                        kb.op("dve", lambda e, t=t: e.tensor_tensor(out=t["DT"][:, :], in0=t["DT"][:, :], in1=M2[:, :],
                                                                    op=ALU.mult), [t["DT"], M2], [t["DT"]])
                        kb.op("act", lambda e, g=g: e.activation(out=g[:, 2:3], in_=g[:, 0:1], func=AF.Exp), [g], [g])
                        kb.op("dve", lambda e, g=g: e.tensor_scalar(out=g[:, 3:4], in0=g[:, 2:3], scalar1=-1.0, scalar2=None,
                                                                    op0=ALU.mult), [g], [g])
                        kb.op("dve", lambda e, g=g: e.tensor_tensor(out=g[:, 6:7], in0=g[:, 1:2], in1=g[:, 0:1],
                                                                    op=ALU.subtract), [g], [g])
                        kb.op("act", lambda e, g=g: e.activation(out=g[:, 4:5], in_=g[:, 6:7], func=AF.Exp), [g], [g])
                        kb.op("act", lambda e, g=g: e.activation(out=g[:, 5:6], in_=g[:, 1:2], func=AF.Exp), [g], [g])
                        pkk = psr()
                        kb.op("pe", lambda e, pkk=pkk, KTc=KTc: e.matmul(pkk[0:64, 0:64], lhsT=KTc, rhs=KTc, start=True,
                                                                         stop=True), [KT[h]], [pkk])
                        kb.op("pe", lambda e, pkk=pkk, KTc=KTc, QTc=QTc: e.matmul(pkk[0:64, 64:128], lhsT=KTc, rhs=QTc,
                                                                                   start=True, stop=True),
                              [KT[h], QT[h]], [pkk])
                        kb.op("dve", lambda e, pkk=pkk, t=t, be=be: e.scalar_tensor_tensor(
                            out=t["N"][:, :], in0=pkk[0:64, 0:64], scalar=be, in1=t["DTs"][:, :], op0=ALU.mult,
                            op1=ALU.mult), [pkk, be_all, t["DTs"]], [t["N"]])
                        kb.op("dve", lambda e, pkk=pkk, t=t: e.tensor_tensor(out=t["SC"][:, :], in0=pkk[0:64, 64:128],
                                                                            in1=t["DT"][:, :], op=ALU.mult),
                              [pkk, t["DT"]], [t["SC"]])
                        kb.op("dve", lambda e, t=t, QTc=QTc: e.tensor_tensor(out=t["QgT"][:, :], in0=QTc, in1=t["EG"][:, :],
                                                                            op=ALU.mult), [QT[h], t["EG"]], [t["QgT"]])
                        kb.op("dve", lambda e, t=t, g=g: e.tensor_scalar(out=t["Khat"][:, :], in0=t["Ktok"][:, :],
                                                                         scalar1=g[:, 4:5], scalar2=None, op0=ALU.mult),
                              [t["Ktok"], g], [t["Khat"]])
                        pt_ = psr()
                        kb.op("pe", lambda e, pt_=pt_, t=t: e.transpose(out=pt_[0:64, 0:64], in_=t["N"][:, :], identity=I64),
                              [t["N"], ident], [pt_])
                        kb.op("act", lambda e, pt_=pt_, t=t: e.activation(out=t["NT"][:, :], in_=pt_[0:64, 0:64],
                                                                          func=AF.Identity), [pt_], [t["NT"]])
                        kb.op("dve", lambda e, t=t: e.tensor_tensor(out=t["R0"][:, :], in0=I64, in1=t["N"][:, :],
                                                                    op=ALU.subtract), [ident, t["N"]], [t["R0"]])
                    for lev in range(5):
                        for h in range(4):
                            t = tl[h]
                            Pc = t["N"] if lev == 0 else t["P%d" % (lev % 2)]
                            PTc = t["NT"] if lev == 0 else t["PT%d" % (lev % 2)]
                            Pn = t["P%d" % ((lev + 1) % 2)]
                            PTn = t["PT%d" % ((lev + 1) % 2)]
                            Rc = t["R%d" % (lev % 2)]
                            Rn = t["R%d" % ((lev + 1) % 2)]
                            pp = psr()
                            kb.op("pe", lambda e, pp=pp, Pc=Pc, PTc=PTc: e.matmul(pp[0:64, 0:64], lhsT=Pc[:, :], rhs=PTc[:, :],
                                                                                  start=True, stop=True), [Pc, PTc], [pp])
                            if lev < 4:
                                kb.op("pe", lambda e, pp=pp, Pc=Pc, PTc=PTc: e.matmul(pp[0:64, 64:128], lhsT=PTc[:, :],
                                                                                      rhs=Pc[:, :], start=True, stop=True),
                                      [Pc, PTc], [pp])
                            kb.op("act", lambda e, pp=pp, PTn=PTn: e.activation(out=PTn[:, :], in_=pp[0:64, 0:64],
                                                                                func=AF.Identity), [pp], [PTn])
                            if lev < 4:
                                kb.op("act", lambda e, pp=pp, Pn=Pn: e.activation(out=Pn[:, :], in_=pp[0:64, 64:128],
                                                                                  func=AF.Identity), [pp], [Pn])
                            pr = psr()
                            kb.op("pe", lambda e, pr=pr, PTn=PTn, Rc=Rc: e.matmul(pr[0:64, 0:64], lhsT=PTn[:, :], rhs=Rc[:, :],
                                                                                  start=True, stop=True), [PTn, Rc], [pr])
                            kb.op("dve", lambda e, pr=pr, Rc=Rc, Rn=Rn: e.tensor_tensor(out=Rn[:, :], in0=pr[0:64, 0:64],
                                                                                        in1=Rc[:, :], op=ALU.add),
                                  [pr, Rc], [Rn])
                    for h in range(4):
                        t = tl[h]
                        pks = psr()
                        kb.op("pe", lambda e, pks=pks, h=h: e.matmul(pks[0:64, 0:64], lhsT=KT[h][:, c0:c0 + C], rhs=S[h][:, :],
                                                                     start=True, stop=True), [KT[h], S[h]], [pks])
                        tl[h]["_pks"] = pks
                    for h in range(4):
                        t, g = tl[h], gsb[h]
                        pks = t["_pks"]
                        kb.op("dve", lambda e, pks=pks, t=t, g=g: e.scalar_tensor_tensor(
                            out=t["Z"][:, :], in0=pks[0:64, 0:64], scalar=g[:, 3:4], in1=t["Vtok"][:, :], op0=ALU.mult,
                            op1=ALU.add), [pks, g, t["Vtok"]], [t["Z"]])
                    for h in range(4):
                        t = tl[h]
                        pvn = psr()
                        kb.op("pe", lambda e, pvn=pvn, t=t: e.matmul(pvn[0:64, 0:64], lhsT=t["R1"][:, :], rhs=t["Z"][:, :],
                                                                     start=True, stop=True), [t["R1"], t["Z"]], [pvn])
                        t["_pvn"] = pvn
                    for h in range(4):
                        t = tl[h]
                        pvn = t["_pvn"]
                        cidx = dr * 4 + h
                        kb.op("act", lambda e, pvn=pvn, t=t, cidx=cidx: e.activation(
                            out=t["vn"][:, :], in_=pvn[0:64, 0:64], func=AF.Identity, scale=be_all[:, cidx:cidx + 1]),
                            [pvn, be_all], [t["vn"]])
                    for h in range(4):
                        t, g = tl[h], gsb[h]
                        po = PB[h // 2]
                        oc = (h % 2) * 256 + c0
                        kb.op("pe", lambda e, po=po, t=t, oc=oc: e.matmul(po[0:64, oc:oc + C], lhsT=t["vn"][:, :],
                                                                          rhs=t["SC"][:, :], start=True, stop=False),
                              [t["vn"], t["SC"]], [po])
                        kb.op("pe", lambda e, po=po, t=t, oc=oc, h=h: e.matmul(po[0:64, oc:oc + C], lhsT=S[h][:, :],
                                                                               rhs=t["QgT"][:, :], start=False, stop=True),
                              [S[h], t["QgT"]], [po])
                        psn = psr()
                        kb.op("pe", lambda e, psn=psn, t=t: e.matmul(psn[0:64, 0:64], lhsT=t["Khat"][:, :], rhs=t["vn"][:, :],
                                                                     start=True, stop=True), [t["Khat"], t["vn"]], [psn])
                        kb.op("dve", lambda e, psn=psn, h=h, g=g: e.scalar_tensor_tensor(
                            out=S[h][:, :], in0=S[h][:, :], scalar=g[:, 5:6], in1=psn[0:64, 0:64], op0=ALU.mult,
                            op1=ALU.add), [S[h], g, psn], [S[h]])
                if dr == 0:
                    for h in range(4):
                        po = PB[h // 2]
                        oc = (h % 2) * 256
                        kb.op("act", lambda e, po=po, oc=oc, h=h: e.activation(
                            out=obuf[:, h, 0:n], in_=po[0:64, oc:oc + n], func=AF.Identity), [po], [obuf])
                    for h in range(4):
                        kb.dma("sp", ogd_d, ogd_d.ap[:, h, tok0:tok0 + n], obuf, obuf[:, h, 0:n], semt=obuf)
                else:
                    for h in range(4):
                        kb.dma("sp", obuf2, obuf2[:, h, 0:n], ogd_d, ogd_d.ap[:, h, tok0:tok0 + n])
                    for h in range(4):
                        po = PB[h // 2]
                        oc = (h % 2) * 256
                        kb.op("dve", lambda e, po=po, oc=oc, h=h: e.tensor_tensor(
                            out=obuf2[:, h, 0:n], in0=po[0:64, oc:oc + n], in1=obuf2[:, h, 0:n], op=ALU.add),
                            [po, obuf2], [obuf2])
                    finish_block(tok0, n)


def gd_precompute_all(st, l):
    kb = st["kb"]
    hT, wbuf, w_in_d = st["hT"], st["wbuf"], st["w_in_d"]
    PB = kb.psum_banks
    gq_d = st["gqkv_d"]
    bdb = st["bdb"]
    blocks = [(0, 256, 0, 0)] + [(256 + 512 * j, 512, 1 if j > 0 else 0, 1 if j < 3 else 0) for j in range(4)]
    with kb.phase() as ph:
        wqk = wbuf[0]
        wv = wbuf[1]
        cw = ph.sb("pcw", [128, DEPTH, 6, 3])
        ur = [ph.sb("pur%d" % i, [128, 516]) for i in range(2)]
        cg = [ph.sb("pcg%d" % i, [128, 512]) for i in range(2)]
        ee = [ph.sb("pee%d" % i, [128, 512]) for i in range(2)]
        sq = [ph.sb("psq%d" % i, [128, 512], BF16) for i in range(2)]
        rs = [ph.sb("prs%d" % i, [128, 512]) for i in range(2)]
        ot = [ph.sb("pot%d" % i, [128, 512], BF16) for i in range(2)]
        kb.dma("sp", cw, cw[:], st["gdcw128_d"], st["gdcw128_d"][:])
        load_w(st, wqk, w_in_d, l, O_GDQKV, 512, 0)
        load_w(st, wv, w_in_d, l, O_GDQKV + 512, 256, 0)
        it = 0
        for (tok0, n, hl, hr) in blocks:
            w = n + hl + hr
            c0 = tok0 - hl
            mblocks = [(0, w // 2), (w // 2, w - w // 2)] if w > 512 else [(0, w)]
            for p_ in range(2):
                if not hl:
                    kb.op("dve", lambda e, p_=p_: e.memset(ur[p_][:, 0:1], 0.0), [], [ur[p_]])
                if not hr:
                    kb.op("dve", lambda e, p_=p_: e.memset(ur[p_][:, n + 1:n + 2], 0.0), [], [ur[p_]])
            def stage_a(cc, p_, it_):
                ty, hp = cc // 2, cc % 2
                u_, c_, e_, s_, r_, o_ = ur[p_], cg[p_], ee[p_], sq[p_], rs[p_], ot[p_]
                wt = wqk if ty < 2 else wv
                wc0 = ty * 256 + hp * 128 if ty < 2 else hp * 128
                for bi, (b0, bw) in enumerate(mblocks):
                    pu = PB[(2 * it_ + bi) % 8]
                    for kc in range(KC):
                        kb.op("pe", lambda e, kc=kc, pu=pu, b0=b0, bw=bw: e.matmul(
                            pu[:, 0:bw], lhsT=wt[:, kc, wc0:wc0 + 128], rhs=hT[:, kc, c0 + b0:c0 + b0 + bw],
                            start=(kc == 0), stop=(kc == KC - 1)), [wt, hT], [pu])
                    o0_ = 1 - hl + b0
                    kb.op("act", lambda e, pu=pu, bw=bw, o0_=o0_: e.activation(out=u_[:, o0_:o0_ + bw], in_=pu[:, 0:bw],
                                                                              func=AF.Identity), [pu], [u_])
                kb.op("dve", lambda e: e.tensor_scalar(out=c_[:, 0:n], in0=u_[:, 0:n], scalar1=cw[:, l, cc, 0:1],
                                                       scalar2=None, op0=ALU.mult), [u_, cw], [c_])
                for k_ in (1, 2):
                    kb.op("dve", lambda e, k_=k_: e.scalar_tensor_tensor(
                        out=c_[:, 0:n], in0=u_[:, k_:k_ + n], scalar=cw[:, l, cc, k_:k_ + 1], in1=c_[:, 0:n],
                        op0=ALU.mult, op1=ALU.add), [u_, cw, c_], [c_])
                kb.op("act", lambda e: e.activation(out=e_[:, 0:n], in_=c_[:, 0:n], func=AF.Exp, scale=-1.0), [c_], [e_])
                kb.op("dve", lambda e: e.tensor_scalar(out=e_[:, 0:n], in0=e_[:, 0:n], scalar1=1.0, scalar2=None,
                                                       op0=ALU.add), [e_], [e_])
                kb.op("dve", lambda e: e.reciprocal(out=e_[:, 0:n], in_=e_[:, 0:n]), [e_], [e_])

                if ty < 2:
                    kb.op("dve", lambda e: e.tensor_tensor(out=c_[:, 0:n], in0=c_[:, 0:n], in1=e_[:, 0:n], op=ALU.mult),
                          [c_, e_], [c_])
                    kb.op("act", lambda e: e.activation(out=s_[:, 0:n], in_=c_[:, 0:n], func=AF.Square), [c_], [s_])

            def stage_b(cc, p_, it_):
                ty, hp = cc // 2, cc % 2
                u_, c_, e_, s_, r_, o_ = ur[p_], cg[p_], ee[p_], sq[p_], rs[p_], ot[p_]
                if ty == 2:
                    kb.op("dve", lambda e: e.tensor_tensor(out=o_[:, 0:n], in0=c_[:, 0:n], in1=e_[:, 0:n], op=ALU.mult),
                          [c_, e_], [o_])
                else:
                    pn = PB[(2 * it_ + 5) % 8]
                    kb.op("pe", lambda e, pn=pn: e.matmul(pn[:, 0:n], lhsT=bdb[:, :], rhs=s_[:, 0:n], start=True, stop=True),
                          [bdb, s_], [pn])
                    kb.op("act", lambda e, pn=pn: e.activation(out=r_[:, 0:n], in_=pn[:, 0:n], func=AF.Ln,
                                                               bias=st["epst"][:, :]), [pn, st["epst"]], [r_])
                    kb.op("act", lambda e: e.activation(out=r_[:, 0:n], in_=r_[:, 0:n], func=AF.Exp, scale=-0.5), [r_], [r_])
                    sc_ = 0.125 if ty == 0 else 1.0
                    kb.op("dve", lambda e, sc_=sc_: e.scalar_tensor_tensor(
                        out=o_[:, 0:n], in0=c_[:, 0:n], scalar=sc_, in1=r_[:, 0:n], op0=ALU.mult, op1=ALU.mult),
                        [c_, r_], [o_])
                i0 = ty * 4 + 2 * hp
                kb.dma("sp", gq_d, gq_d.ap[i0:i0 + 2].rearrange("t d n -> (t d) n")[:, tok0:tok0 + n], o_, o_[:, 0:n],
                       semt=o_)

            stage_a(0, it % 2, it)
            for cc in range(6):
                if cc + 1 < 6:
                    stage_a(cc + 1, (it + cc + 1) % 2, it + cc + 1)
                stage_b(cc, (it + cc) % 2, it + cc)
            it += 6


def gd_mixer2(st, l):
    kb, nc = st["kb"], st["nc"]
    hT, wbuf, w_in_d, identb = st["hT"], st["wbuf"], st["w_in_d"], st["identb"]
    PB = kb.psum_banks
    C = 64
    NB = 256
    blocks = [(0, 256, 0, 0)] + [(256 + NB * j, NB, 1 if j > 0 else 0, 1 if j < 7 else 0) for j in range(8)]
    order = [list(range(9)), [0] + list(range(8, 0, -1))]
    ones64 = st["gm_ones"]
    M2f = [st["gm_UI"], st["gm_LI"]]
    Ib64 = identb[0:64, 0:64]
    I64f = st["ident"][0:64, 0:64]

    def v3(t_):
        return t_[:, :].rearrange("p (c k) -> p c k", k=64)

    def bc8(ap2):
        return ap2.rearrange("p (c o) -> p c o", o=1).to_broadcast([64, 8, 64])

    gd_precompute_all(st, l)
    with kb.phase() as ph:
        wvab = wbuf[1]
        wg = ph.sb("gwg", [128, KC, 256], BF16)
        cwc = ph.sb("gcw", [64, DEPTH, 12, 3])
        ngc_g = ph.sb("gng", [64, DEPTH])
        dtb = ph.sb("gdtb", [64, 8])
        nexpA = ph.sb("gnexpA", [64, 8])
        ones1 = ph.sb("gones1", [64, 1])
        ob = [ph.sb("gob%d" % d_, [64, 4, NB]) for d_ in range(2)]
        obuf2 = T(st["junk"].ap[0:64, :].rearrange("p (h n) -> p h n", h=4), "obuf2")
        obuf2_owner = st["junk"]
        ur = ph.sb("gur", [64, NB + 4])
        cg = ph.sb("gcg", [64, NB])
        sqb = ph.sb("gsq", [64, NB])
        rsd = ph.sb("grsd", [64, NB])
        QT = [[ph.sb("gQT%d%d" % (d_, h), [64, NB], BF16) for h in range(4)] for d_ in range(2)]
        KT = [[ph.sb("gKT%d%d" % (d_, h), [64, NB], BF16) for h in range(4)] for d_ in range(2)]
        VT = [[ph.sb("gVT%d%d" % (d_, h), [64, NB], BF16) for h in range(4)] for d_ in range(2)]
        RES = [ph.sb("gRES%d" % h, [64, NB], BF16) for h in range(4)]
        stage = ph.sb("gstage", [128, 256], BF16)
        mk = {nm: ph.sb("g" + nm, [64, 512], BF16) for nm in ("M1c", "M2c", "M3c", "Ic")}
        f32n = ["Lm", "DT", "EG", "S", "tmpA"]
        f32rn = ["R0", "R1", "N", "NT", "P0", "P1", "PT0", "PT1"]
        b16n = ["Rb", "Z", "vn", "SC", "Khat", "Ktok", "Vtok", "Sb"]
        t = {nm: ph.sb("g2" + nm, [64, 512]) for nm in f32n}
        t.update({nm: ph.sb("g2" + nm, [64, 512], mybir.dt.float32r) for nm in f32rn})
        t.update({nm: ph.sb("g2" + nm, [64, 512], BF16) for nm in b16n})
        t["LM2"] = t["tmpA"]
        aB = [ph.sb("gaB%d" % d_, [64, 4, 4]) for d_ in range(2)]
        bB = [ph.sb("gbB%d" % d_, [64, 4, 4]) for d_ in range(2)]
        laB = [ph.sb("glaB%d" % d_, [64, 4, 4]) for d_ in range(2)]
        beB = [ph.sb("gbeB%d" % d_, [64, 4, 4]) for d_ in range(2)]
        la_all = ph.sb("gla", [64, 8])
        be_all = ph.sb("gbe", [64, 8])
        g = ph.sb("ggsb", [64, 48])
        for nm in ("M1c", "M2c", "M3c", "Ic"):
            kb.dma("pool", mk[nm], mk[nm][:], st["gc_" + nm], st["gc_" + nm][:])
        kb.dma("sp", cwc, cwc[:], st["gdcw_d"], st["gdcw_d"][:])
        kb.dma("sp", ngc_g, ngc_g[:], st["gdng_d"], st["gdng_d"][:])
        kb.dma("sp", dtb, dtb[:], st["gddt_d"], st["gddt_d"].ap[l:l + 1, :].partition_broadcast(64))
        kb.dma("sp", nexpA, nexpA[:], st["gdal_d"], st["gdal_d"].ap[l:l + 1, :].partition_broadcast(64))
        kb.op("act", lambda e: e.activation(out=nexpA[:], in_=nexpA[:], func=AF.Exp), [nexpA], [nexpA])
        kb.op("dve", lambda e: e.tensor_scalar(out=nexpA[:], in0=nexpA[:], scalar1=-1.0, scalar2=None, op0=ALU.mult),
              [nexpA], [nexpA])
        kb.op("dve", lambda e: e.memset(ones1[:], 1.0), [], [ones1])
        kb.op("dve", lambda e: e.memset(t["S"][:], 0.0), [], [t["S"]])
        kb.op("dve", lambda e: e.memset(t["Sb"][:], 0.0), [], [t["Sb"]])
        load_w(st, wvab, w_in_d, l, O_GDA, 16, 256)
        load_w(st, wg, w_in_d, l, O_GDG, 256, 0)
        rot = [0]

        def psr():
            t_ = PB[rot[0] % 8]
            rot[0] += 1
            return t_

        def precompute_block(dr, tok0, n, hl, hr):
            nchb = n // C
            for ck in range(nchb):
                pab = psr()
                for kc in range(KC):
                    kb.op("pe", lambda e, kc=kc, pab=pab, ck=ck: e.matmul(
                        pab[0:64, 0:16], lhsT=hT[:, kc, tok0 + ck * C:tok0 + (ck + 1) * C], rhs=wvab[:, kc, 256:272],
                        start=(kc == 0), stop=(kc == KC - 1)), [hT, wvab], [pab])
                kb.op("act", lambda e, pab=pab, ck=ck: e.activation(out=aB[dr][:, ck, :], in_=pab[0:64, dr * 4:dr * 4 + 4],
                                                                    func=AF.Identity), [pab], [aB[dr]])
                kb.op("act", lambda e, pab=pab, ck=ck: e.activation(out=bB[dr][:, ck, :], in_=pab[0:64, 8 + dr * 4:12 + dr * 4],
                                                                    func=AF.Identity), [pab], [bB[dr]])
                kb.op("dve", lambda e, ck=ck: e.tensor_tensor(out=aB[dr][:, ck, :], in0=aB[dr][:, ck, :],
                                                              in1=dtb[:, dr * 4:dr * 4 + 4], op=ALU.add), [aB[dr], dtb], [aB[dr]])
            kb.op("act", lambda e: e.activation(out=aB[dr][:, :, :], in_=aB[dr][:, :, :], func=AF.Exp), [aB[dr]], [aB[dr]])
            kb.op("act", lambda e: e.activation(out=aB[dr][:, :, :], in_=aB[dr][:, :, :], func=AF.Ln, bias=ones1[:, :]),
                  [aB[dr], ones1], [aB[dr]])
            for ck in range(nchb):
                kb.op("dve", lambda e, ck=ck: e.tensor_tensor(out=laB[dr][:, ck, :], in0=aB[dr][:, ck, :],
                                                              in1=nexpA[:, dr * 4:dr * 4 + 4], op=ALU.mult),
                      [aB[dr], nexpA], [laB[dr]])
            kb.op("act", lambda e: e.activation(out=beB[dr][:, :, :], in_=bB[dr][:, :, :], func=AF.Exp, scale=-1.0),
                  [bB[dr]], [beB[dr]])
            kb.op("dve", lambda e: e.tensor_scalar(out=beB[dr][:, :, :], in0=beB[dr][:, :, :], scalar1=1.0, scalar2=None,
                                                   op0=ALU.add), [beB[dr]], [beB[dr]])
            kb.op("dve", lambda e: e.reciprocal(out=beB[dr][:, :, :], in_=beB[dr][:, :, :]), [beB[dr]], [beB[dr]])
            gq_d = st["gqkv_d"]
            for ty in range(3):
                for h in range(4):
                    dstt = (QT, KT, VT)[ty][dr][h]
                    kb.dma("sp", dstt, dstt[:, 0:n], gq_d, gq_d.ap[ty * 4 + h, :, tok0:tok0 + n])

        def finish_block(tok0, n):
            for h in range(4):
                kb.op("act", lambda e, h=h: e.activation(out=sqb[:, 0:n], in_=obuf2[:, h, 0:n], func=AF.Square),
                      [st["junk"]], [sqb])
                pn = psr()
                kb.op("pe", lambda e, pn=pn: e.matmul(pn[0:64, 0:n], lhsT=ones64[:, :], rhs=sqb[:, 0:n], start=True,
                                                      stop=True), [ones64, sqb], [pn])
                kb.op("act", lambda e, pn=pn: e.activation(out=rsd[:, 0:n], in_=pn[0:64, 0:n], func=AF.Ln, scale=1.0 / 64,
                                                           bias=st["epst"][0:64, :]), [pn, st["epst"]], [rsd])
                kb.op("act", lambda e: e.activation(out=rsd[:, 0:n], in_=rsd[:, 0:n], func=AF.Exp, scale=-0.5),
                      [rsd], [rsd])
                pg = psr()
                proj_fm_g(st, wg, h * 64, tok0, n, pg, ncols=64)
                kb.op("act", lambda e, pg=pg: e.activation(out=cg[:, 0:n], in_=pg[0:64, 0:n], func=AF.Silu), [pg], [cg])
                kb.op("dve", lambda e, h=h: e.tensor_tensor(out=rsd[:, 0:n], in0=rsd[:, 0:n], in1=obuf2[:, h, 0:n],
                                                            op=ALU.mult), [rsd, st["junk"]], [rsd])
                kb.op("dve", lambda e, h=h: e.scalar_tensor_tensor(out=RES[h][:, 0:n], in0=rsd[:, 0:n],
                                                                   scalar=ngc_g[:, l:l + 1], in1=cg[:, 0:n],
                                                                   op0=ALU.mult, op1=ALU.mult), [rsd, ngc_g, cg], [RES[h]])
            for ti in range(n // 128):
                pst = psr()
                pb = pst.ap.bitcast(BF16)
                for h in range(4):
                    kb.op("pe", lambda e, h=h, ti=ti, pb=pb: e.transpose(
                        out=pb[:, h * 64:(h + 1) * 64], in_=RES[h][:, ti * 128:(ti + 1) * 128],
                        identity=Ib64), [RES[h], identb], [pst])
                kb.op("act", lambda e, pb=pb: e.activation(out=stage[:, :], in_=pb[:, 0:256], func=AF.Identity),
                      [pst], [stage])
                kb.dma("sp", st["mix_d"], st["mix_d"].ap[tok0 + ti * 128:tok0 + (ti + 1) * 128, 768:1024],
                       stage, stage[:, :], semt=stage)

        ogd_d = st["ogd_d"]
        stored = set()
        nsteps = 36
        for step in range(nsteps):
            bi, cj = step // 4, step % 4
            cur = []
            for dr in range(2):
                b = order[dr][bi]
                tok0, n, hl, hr = blocks[b]
                if cj == 0:
                    precompute_block(dr, tok0, n, hl, hr)
                ck = cj if dr == 0 else 3 - cj
                cur.append((b, tok0, ck * C))
            for dr in range(2):
                ckd = cur[dr][2] // C
                kb.op("dve", lambda e, dr=dr, ckd=ckd: e.tensor_copy(out=la_all[:, dr * 4:dr * 4 + 4], in_=laB[dr][:, ckd, :]),
                      [laB[dr]], [la_all])
                kb.op("dve", lambda e, dr=dr, ckd=ckd: e.tensor_copy(out=be_all[:, dr * 4:dr * 4 + 4], in_=beB[dr][:, ckd, :]),
                      [beB[dr]], [be_all])

            def opnd(tiles, c):
                dr, h = c // 4, c % 4
                return tiles[dr][h][:, cur[dr][2]:cur[dr][2] + C], tiles[dr][h]

            for src, dn in ((KT, "Ktok"), (VT, "Vtok")):
                pt_ = psr()
                ptb = pt_.ap.bitcast(BF16)
                for c in range(8):
                    ap_, tt_ = opnd(src, c)
                    kb.op("pe", lambda e, ptb=ptb, ap_=ap_, c=c: e.transpose(out=ptb[0:64, c * 64:(c + 1) * 64], in_=ap_,
                                                                             identity=Ib64), [tt_, identb], [pt_])
                kb.op("act", lambda e, ptb=ptb, dn=dn: e.activation(out=t[dn][:, :], in_=ptb[0:64, 0:512],
                                                                    func=AF.Identity), [pt_], [t[dn]])
            kb.op("dve", lambda e: e.tensor_tensor(out=v3(t["Lm"]), in0=v3(mk["M1c"]), in1=bc8(la_all[:, 0:8]), op=ALU.mult),
                  [mk["M1c"], la_all], [t["Lm"]])
            kb.op("dve", lambda e: e.tensor_tensor(out=v3(t["LM2"]), in0=v3(mk["M2c"]), in1=bc8(la_all[:, 0:8]), op=ALU.mult),
                  [mk["M2c"], la_all], [t["LM2"]])
            pd1 = psr()
            for c in range(8):
                kb.op("pe", lambda e, c=c, pd1=pd1: e.matmul(pd1[0:64, c * 64:(c + 1) * 64], lhsT=t["Lm"][:, c * 64:(c + 1) * 64],
                                                             rhs=M2f[c // 4][:, :], start=True, stop=True),
                      [t["Lm"], M2f[c // 4]], [pd1])
            pd2 = psr()
            kb.op("pe", lambda e, pd2=pd2: e.matmul(pd2[0:64, 0:512], lhsT=ones64[:, :], rhs=t["LM2"][:, :], start=True,
                                                    stop=True), [ones64, t["LM2"]], [pd2])
            pd3 = psr()
            for dr in range(2):
                kb.op("pe", lambda e, dr=dr, pd3=pd3: e.matmul(pd3[0:64, dr * 4:dr * 4 + 4], lhsT=M2f[dr][:, :],
                                                               rhs=la_all[:, dr * 4:dr * 4 + 4], start=True, stop=True),
                      [M2f[dr], la_all], [pd3])
            kb.op("pe", lambda e, pd3=pd3: e.matmul(pd3[0:64, 8:16], lhsT=ones64[:, :], rhs=la_all[:, 0:8], start=True,
                                                    stop=True), [ones64, la_all], [pd3])
            kb.op("act", lambda e, pd1=pd1: e.activation(out=t["DT"][:, :], in_=pd1[0:64, 0:512], func=AF.Exp), [pd1], [t["DT"]])
            kb.op("act", lambda e, pd2=pd2: e.activation(out=t["EG"][:, :], in_=pd2[0:64, 0:512], func=AF.Exp), [pd2], [t["EG"]])
            kb.op("act", lambda e, pd3=pd3: e.activation(out=g[:, 0:16], in_=pd3[0:64, 0:16], func=AF.Identity), [pd3], [g])
            kb.op("dve", lambda e: e.tensor_tensor(out=t["DT"][:, :], in0=t["DT"][:, :], in1=mk["M2c"][:, :], op=ALU.mult),
                  [t["DT"], mk["M2c"]], [t["DT"]])
            kb.op("act", lambda e: e.activation(out=g[:, 16:24], in_=g[:, 0:8], func=AF.Exp), [g], [g])
            kb.op("dve", lambda e: e.tensor_scalar(out=g[:, 16:24], in0=g[:, 16:24], scalar1=-1.0, scalar2=None,
                                                   op0=ALU.mult), [g], [g])
            kb.op("dve", lambda e: e.tensor_tensor(out=g[:, 40:48], in0=g[:, 8:16], in1=g[:, 0:8], op=ALU.subtract),
                  [g], [g])
            kb.op("act", lambda e: e.activation(out=g[:, 24:32], in_=g[:, 40:48], func=AF.Exp), [g], [g])
            kb.op("act", lambda e: e.activation(out=g[:, 32:40], in_=g[:, 8:16], func=AF.Exp), [g], [g])
            pkk = psr()
            psc = psr()
            for c in range(8):
                kap, ktt = opnd(KT, c)
                qap, qtt = opnd(QT, c)
                kb.op("pe", lambda e, c=c, kap=kap, pkk=pkk: e.matmul(pkk[0:64, c * 64:(c + 1) * 64], lhsT=kap, rhs=kap,
                                                                      start=True, stop=True), [ktt], [pkk])
                kb.op("pe", lambda e, c=c, kap=kap, qap=qap, psc=psc: e.matmul(psc[0:64, c * 64:(c + 1) * 64], lhsT=kap,
                                                                               rhs=qap, start=True, stop=True),
                      [ktt, qtt], [psc])
            kb.op("dve", lambda e, pkk=pkk: e.tensor_tensor(out=t["tmpA"][:, :], in0=pkk[0:64, 0:512], in1=t["DT"][:, :],
                                                            op=ALU.mult), [pkk, t["DT"]], [t["tmpA"]])
            kb.op("dve", lambda e: e.tensor_tensor(out=t["tmpA"][:, :], in0=t["tmpA"][:, :], in1=mk["M3c"][:, :], op=ALU.mult),
                  [t["tmpA"], mk["M3c"]], [t["tmpA"]])
            kb.op("dve", lambda e: e.tensor_tensor(out=v3(t["N"]), in0=v3(t["tmpA"]), in1=bc8(be_all[:, 0:8]), op=ALU.mult),
                  [t["tmpA"], be_all], [t["N"]])
            kb.op("dve", lambda e, psc=psc: e.tensor_tensor(out=t["SC"][:, :], in0=psc[0:64, 0:512], in1=t["DT"][:, :],
                                                            op=ALU.mult), [psc, t["DT"]], [t["SC"]])
            kb.op("dve", lambda e: e.tensor_tensor(out=v3(t["Khat"]), in0=v3(t["Ktok"]), in1=bc8(g[:, 24:32]), op=ALU.mult),
                  [t["Ktok"], g], [t["Khat"]])
            pnt = psr()
            for c in range(8):
                kb.op("pe", lambda e, c=c, pnt=pnt: e.transpose(out=pnt[0:64, c * 64:(c + 1) * 64],
                                                                in_=t["N"][:, c * 64:(c + 1) * 64].bitcast(F32), identity=I64f),
                      [t["N"], st["ident"]], [pnt])
            kb.op("act", lambda e, pnt=pnt: e.activation(out=t["NT"][:, :], in_=pnt[0:64, 0:512], func=AF.Identity),
                  [pnt], [t["NT"]])
            kb.op("dve", lambda e: e.tensor_tensor(out=t["R0"][:, :], in0=mk["Ic"][:, :], in1=t["N"][:, :], op=ALU.subtract),
                  [mk["Ic"], t["N"]], [t["R0"]])
            for lev in range(5):
                Pc = t["N"] if lev == 0 else t["P%d" % (lev % 2)]
                PTc = t["NT"] if lev == 0 else t["PT%d" % (lev % 2)]
                Pn = t["P%d" % ((lev + 1) % 2)]
                PTn = t["PT%d" % ((lev + 1) % 2)]
                Rc = t["R%d" % (lev % 2)]
                Rn = t["R%d" % ((lev + 1) % 2)]
                pp1 = psr()
                for c in range(8):
                    sl = slice(c * 64, (c + 1) * 64)
                    kb.op("pe", lambda e, sl=sl, pp1=pp1: e.matmul(pp1[0:64, sl], lhsT=Pc[:, sl], rhs=PTc[:, sl], start=True,
                                                                   stop=True), [Pc, PTc], [pp1])
                kb.op("act", lambda e, pp1=pp1: e.activation(out=PTn[:, :], in_=pp1[0:64, 0:512], func=AF.Identity),
                      [pp1], [PTn])
                if lev < 4:
                    pp2 = psr()
                    for c in range(8):
                        sl = slice(c * 64, (c + 1) * 64)
                        kb.op("pe", lambda e, sl=sl, pp2=pp2: e.matmul(pp2[0:64, sl], lhsT=PTc[:, sl], rhs=Pc[:, sl],
                                                                       start=True, stop=True), [Pc, PTc], [pp2])
                    kb.op("dve", lambda e, pp2=pp2: e.tensor_copy(out=Pn[:, :], in_=pp2[0:64, 0:512]), [pp2], [Pn])
                pr = psr()
                for c in range(8):
                    sl = slice(c * 64, (c + 1) * 64)
                    kb.op("pe", lambda e, sl=sl, pr=pr: e.matmul(pr[0:64, sl], lhsT=PTn[:, sl], rhs=Rc[:, sl], start=True,
                                                                 stop=True), [PTn, Rc], [pr])
                kb.op("dve", lambda e, pr=pr: e.tensor_tensor(out=Rn[:, :], in0=pr[0:64, 0:512], in1=Rc[:, :], op=ALU.add),
                      [pr, Rc], [Rn])
                if lev == 4:
                    kb.op("act", lambda e: e.activation(out=t["Rb"][:, :], in_=Rn[:, :], func=AF.Identity), [Rn], [t["Rb"]])
            pks = psr()
            for c in range(8):
                sl = slice(c * 64, (c + 1) * 64)
                kap, ktt = opnd(KT, c)
                kb.op("pe", lambda e, sl=sl, kap=kap, pks=pks: e.matmul(pks[0:64, sl], lhsT=kap, rhs=t["Sb"][:, sl], start=True,
                                                                        stop=True), [ktt, t["Sb"]], [pks])
            kb.op("dve", lambda e, pks=pks: e.tensor_tensor(out=v3(t["tmpA"]), in0=pks[0:64, 0:512].rearrange("p (c k) -> p c k", k=64),
                                                            in1=bc8(g[:, 16:24]), op=ALU.mult), [pks, g], [t["tmpA"]])
            kb.op("dve", lambda e: e.tensor_tensor(out=t["Z"][:, :], in0=t["tmpA"][:, :], in1=t["Vtok"][:, :], op=ALU.add),
                  [t["tmpA"], t["Vtok"]], [t["Z"]])
            pvn = psr()
            for c in range(8):
                sl = slice(c * 64, (c + 1) * 64)
                kb.op("pe", lambda e, sl=sl, pvn=pvn: e.matmul(pvn[0:64, sl], lhsT=t["Rb"][:, sl], rhs=t["Z"][:, sl], start=True,
                                                               stop=True), [t["Rb"], t["Z"]], [pvn])
            kb.op("dve", lambda e, pvn=pvn: e.tensor_tensor(out=v3(t["vn"]), in0=pvn[0:64, 0:512].rearrange("p (c k) -> p c k", k=64),
                                                            in1=bc8(be_all[:, 0:8]), op=ALU.mult), [pvn, be_all], [t["vn"]])
            poi = psr()
            pos_ = psr()
            psn = psr()
            for c in range(8):
                sl = slice(c * 64, (c + 1) * 64)
                qap, qtt = opnd(QT, c)
                kb.op("pe", lambda e, sl=sl, poi=poi: e.matmul(poi[0:64, sl], lhsT=t["vn"][:, sl], rhs=t["SC"][:, sl], start=True,
                                                               stop=True), [t["vn"], t["SC"]], [poi])
                kb.op("pe", lambda e, sl=sl, qap=qap, pos_=pos_: e.matmul(pos_[0:64, sl], lhsT=t["Sb"][:, sl], rhs=qap, start=True,
                                                                          stop=True), [t["Sb"], qtt], [pos_])
                kb.op("pe", lambda e, sl=sl, psn=psn: e.matmul(psn[0:64, sl], lhsT=t["Khat"][:, sl], rhs=t["vn"][:, sl], start=True,
                                                               stop=True), [t["Khat"], t["vn"]], [psn])
            kb.op("dve", lambda e: e.tensor_tensor(out=v3(t["S"]), in0=v3(t["S"]), in1=bc8(g[:, 32:40]), op=ALU.mult),
                  [t["S"], g], [t["S"]])
            kb.op("dve", lambda e, psn=psn: e.tensor_tensor(out=t["S"][:, :], in0=psn[0:64, 0:512], in1=t["S"][:, :], op=ALU.add),
                  [psn, t["S"]], [t["S"]])
            kb.op("act", lambda e: e.activation(out=t["Sb"][:, :], in_=t["S"][:, :], func=AF.Identity), [t["S"]], [t["Sb"]])
            kb.op("dve", lambda e, pos_=pos_: e.tensor_tensor(out=t["tmpA"][:, :], in0=pos_[0:64, 0:512], in1=t["EG"][:, :],
                                                              op=ALU.mult), [pos_, t["EG"]], [t["tmpA"]])
            for dr in range(2):
                c0 = cur[dr][2]
                kb.op("dve", lambda e, dr=dr, c0=c0, poi=poi: e.tensor_tensor(
                    out=ob[dr][:, :, c0:c0 + C], in0=poi[0:64, dr * 256:(dr + 1) * 256].rearrange("p (c k) -> p c k", k=64),
                    in1=t["tmpA"][:, dr * 256:(dr + 1) * 256].rearrange("p (c k) -> p c k", k=64), op=ALU.add),
                    [poi, t["tmpA"]], [ob[dr]])
            if cj == 3:
                for dr in range(2):
                    b, tok0, _ = cur[dr]
                    n = blocks[b][1]
                    if b not in stored:
                        for h in range(4):
                            kb.dma("sp", ogd_d, ogd_d.ap[:, h, tok0:tok0 + n], ob[dr], ob[dr][:, h, 0:n], semt=ob[dr])
                        stored.add(b)
                    else:
                        for h in range(4):
                            kb.dma("sp", st["junk"], obuf2[:, h, 0:n], ogd_d, ogd_d.ap[:, h, tok0:tok0 + n])
                        kb.op("dve", lambda e, dr=dr, n=n: e.tensor_tensor(out=obuf2[:, :, 0:n], in0=obuf2[:, :, 0:n],
                                                                           in1=ob[dr][:, :, 0:n], op=ALU.add),
                              [st["junk"], ob[dr]], [st["junk"]])
                        finish_block(tok0, n)


_CONST = {}


def _consts():
    if _CONST:
        return _CONST
    _CONST["ident"] = np.eye(128, dtype=np.float32)
    nf = 16
    inv = (10000.0 ** (-np.arange(nf, dtype=np.float32) / nf)).astype(np.float32)
    t = np.arange(SEQ)
    rows = (t // 64).astype(np.float32)
    cols = (t % 64).astype(np.float32)
    ang = np.concatenate([rows[:, None] * inv, cols[:, None] * inv], axis=-1).astype(np.float32)
    cosT = np.cos(ang).T.astype(np.float32)
    sinT = np.sin(ang).T.astype(np.float32)
    _CONST["ropeC"] = np.ascontiguousarray(np.tile(cosT, (4, 1)))
    _CONST["ropeS"] = np.ascontiguousarray(np.tile(sinT, (4, 1)))
    rot = np.zeros((128, 128), np.float32)
    for m in range(2):
        for d in range(64):
            i = m * 64 + d
            if d < 32:
                rot[m * 64 + d + 32, i] = -1.0
            else:
                rot[m * 64 + d - 32, i] = 1.0
    _CONST["rotm"] = rot
    cm = np.ones((128, 512), np.float32)
    cm[:, ::32] = 0.0
    _CONST["c_cmask"] = cm
    bd = np.zeros((128, 128), np.float32)
    bd[:64, :64] = 1.0
    bd[64:, 64:] = 1.0
    _CONST["c_bdm"] = bd
    ii = np.arange(32)
    up = (ii[:, None] <= ii[None, :]).astype(np.float32)
    lo = (ii[:, None] >= ii[None, :]).astype(np.float32)
    i6 = np.arange(64)
    _CONST["c_gm_L"] = (i6[:, None] > i6[None, :]).astype(np.float32)
    _CONST["c_gm_U"] = (i6[:, None] < i6[None, :]).astype(np.float32)
    _CONST["c_gm_LI"] = (i6[:, None] >= i6[None, :]).astype(np.float32)
    _CONST["c_gm_UI"] = (i6[:, None] <= i6[None, :]).astype(np.float32)
    _CONST["c_gm_ones"] = np.ones((64, 64), np.float32)
    L_, U_, LI_, UI_ = _CONST["c_gm_L"], _CONST["c_gm_U"], _CONST["c_gm_LI"], _CONST["c_gm_UI"]
    _CONST["c_gM1c"] = np.ascontiguousarray(np.concatenate([L_] * 4 + [U_] * 4, axis=1))
    _CONST["c_gM2c"] = np.ascontiguousarray(np.concatenate([UI_] * 4 + [LI_] * 4, axis=1))
    _CONST["c_gM3c"] = np.ascontiguousarray(np.concatenate([U_] * 4 + [L_] * 4, axis=1))
    _CONST["c_gIc"] = np.ascontiguousarray(np.concatenate([np.eye(64, dtype=np.float32)] * 8, axis=1))
    _CONST["c_hgmask"] = np.ascontiguousarray(np.stack([np.tile(up, (1, 2)), np.tile(lo, (1, 2))], axis=1))
    return _CONST


def make_in_maps(inputs):
    cst = _consts()
    f = lambda a: np.ascontiguousarray(np.asarray(a, dtype=np.float32))
    x, c, ctx, c_ctx = f(inputs["x"]), f(inputs["c"]), f(inputs["ctx"]), f(inputs["c_ctx"])
    ada_w, ada_b = f(inputs["ada_w"]), f(inputs["ada_b"])
    norm_g = f(inputs["norm_g"])
    shared = {
        "ada_w": ada_w, "ada_b": ada_b,
        "ada_bc": np.ascontiguousarray(ada_b.reshape(DEPTH, 48, 128).transpose(0, 2, 1)),
        "norm_g": norm_g,
        "norm_gc": np.ascontiguousarray(norm_g.reshape(DEPTH, 4, KC, 128).transpose(0, 1, 3, 2)),
        "w_in": f(inputs["w_in"]), "w_out": f(inputs["w_out"]),
        "ident": cst["ident"], "ropeC": cst["ropeC"], "ropeS": cst["ropeS"], "rotm": cst["rotm"],
        "da_lambda": f(inputs["da_lambda"]).reshape(DEPTH, 256),
        "da_subln_g": f(inputs["da_subln_g"]),
        "c_cmask": cst["c_cmask"], "c_bdm": cst["c_bdm"], "c_hgmask": cst["c_hgmask"],
        "hg_lb_c": np.ascontiguousarray(f(inputs["hg_lb_logits"]).reshape(DEPTH, 4, 128).transpose(2, 0, 1)),
        "hg_ng_c": np.ascontiguousarray(np.tile(f(inputs["hg_norm_g"]), (1, 2)).T),
        "c_gm_L": cst["c_gm_L"], "c_gm_U": cst["c_gm_U"], "c_gm_LI": cst["c_gm_LI"], "c_gm_UI": cst["c_gm_UI"],
        "c_gm_ones": cst["c_gm_ones"], "c_gM1c": cst["c_gM1c"], "c_gM2c": cst["c_gM2c"], "c_gM3c": cst["c_gM3c"],
        "c_gIc": cst["c_gIc"],
        "gd_cw_c": np.ascontiguousarray(f(inputs["gd_conv_w"]).reshape(DEPTH, 3, 12, 64).transpose(3, 0, 2, 1)),
        "gd_ng_c": np.ascontiguousarray(f(inputs["gd_norm_g"]).T),
        "gd_cw128": np.ascontiguousarray(f(inputs["gd_conv_w"]).reshape(DEPTH, 3, 6, 128).transpose(3, 0, 2, 1)),
        "gd_dtb": f(inputs["gd_dt_bias"]).reshape(DEPTH, 8), "gd_alog": f(inputs["gd_a_log"]).reshape(DEPTH, 8),
        "ffn_w_up": f(inputs["ffn_w_up"]), "ffn_w_down": f(inputs["ffn_w_down"]),
        "ffn_cw": np.ascontiguousarray(f(inputs["ffn_conv_w"]).reshape(DEPTH, 3, 44, 128).transpose(0, 3, 2, 1)),
        "ffn_cb": np.ascontiguousarray(f(inputs["ffn_conv_b"]).reshape(DEPTH, 44, 128).transpose(0, 2, 1)),
    }
    maps = []
    for b in range(8):
        cv = np.stack([c[b], c_ctx], axis=1)
        m = dict(shared)
        m["x"] = x[b]
        m["ctx"] = ctx[b]
        m["cvT"] = np.ascontiguousarray(cv.reshape(KC, 128, 2).transpose(1, 0, 2))
        maps.append(m)
    return maps


def kernel(**inputs):
    nc = build()
    maps = make_in_maps(inputs)
    res = run_bass_kernel_spmd(nc, maps, core_ids=list(range(8)))
    return np.stack([np.asarray(r["out"]) for r in res.results], axis=0).astype(np.float32)
```
